# Optimizing a Trainium2 kernel written in Bass

```python
import math
import jax
import jax.numpy as jnp
from jax import lax
import numpy as np

D_MODEL = 1024
BATCH = 8
SEQ = 2048
DEPTH = 1
DEC_BATCH = 128
DEC_SEQ = 1
PAST_LEN = 16384
PAGE_SIZE = 128

POOL_WIDTH = D_MODEL // 2
POOL_WINDOWS = (2, 4, 8, 16)
POOL_GROUPS = len(POOL_WINDOWS)
POOL_GROUP_W = POOL_WIDTH // POOL_GROUPS
POOL_BUF = max(POOL_WINDOWS) - 1
SSM_WIDTH = D_MODEL // 2
SSM_GROUP_W = 16
SSM_GROUPS = SSM_WIDTH // SSM_GROUP_W
SSM_STATE = 64
DT_MIN = 1e-3
DT_MAX = 1e-1
D_FF = 4 * D_MODEL
RMS_EPS = 1e-6
IN_WIDTH = POOL_WIDTH + SSM_WIDTH + 2 * D_MODEL

kernel_name = 'hybrid_pool_s5_gated_decoder_step'


def rmsnorm(x, g):
    xf = x.astype(jnp.float32)
    y = xf * lax.rsqrt(jnp.mean(xf * xf, axis=-1, keepdims=True) + RMS_EPS)
    return (y * g.astype(jnp.float32)).astype(x.dtype)


def pool_mixer(v, buf, start_pos, w_grp, scale):
    bsz, seq_len, _ = v.shape
    vf = v.astype(jnp.float32)
    cat = jnp.concatenate([buf.astype(jnp.float32), vf], axis=1)
    csum = jnp.cumsum(cat, axis=1)
    csum = jnp.concatenate([jnp.zeros_like(csum[:, :1]), csum], axis=1)
    end = csum[:, POOL_BUF + 1:]
    pos = start_pos + jnp.arange(seq_len) + 1
    outs = []
    for g, w in enumerate(POOL_WINDOWS):
        sl = slice(g * POOL_GROUP_W, (g + 1) * POOL_GROUP_W)
        win_sum = end[..., sl] - csum[:, POOL_BUF + 1 - w:POOL_BUF + 1 - w + seq_len, sl]
        cnt = jnp.minimum(w, pos).astype(jnp.float32)[None, :, None]
        outs.append(win_sum / cnt - vf[..., sl])
    pooled = jnp.stack(outs, axis=2)
    mixed = jnp.einsum('blgc,gcd->blgd', pooled, w_grp.astype(jnp.float32))
    y = mixed.reshape(bsz, seq_len, POOL_WIDTH) * scale.astype(jnp.float32)
    new_buf = cat[:, -POOL_BUF:]
    return y.astype(v.dtype), new_buf.astype(buf.dtype)


def ssm_discretize(a_re, a_im, log_dt, b_re, b_im):
    a_re = a_re.astype(jnp.float32)
    a_im = a_im.astype(jnp.float32)
    b_re = b_re.astype(jnp.float32)
    b_im = b_im.astype(jnp.float32)
    dt = jnp.exp(log_dt.astype(jnp.float32))[:, None]
    mag = jnp.exp(a_re * dt)
    ang = a_im * dt
    lam_re = mag * jnp.cos(ang)
    lam_im = mag * jnp.sin(ang)
    p = lam_re - 1.0
    q = lam_im
    den = a_re * a_re + a_im * a_im
    coef_re = (p * a_re + q * a_im) / den
    coef_im = (q * a_re - p * a_im) / den
    bbar_re = coef_re[..., None] * b_re - coef_im[..., None] * b_im
    bbar_im = coef_re[..., None] * b_im + coef_im[..., None] * b_re
    return lam_re, lam_im, bbar_re, bbar_im


def _ssm_combine(e1, e2):
    a1r, a1i, b1r, b1i = e1
    a2r, a2i, b2r, b2i = e2
    return (a2r * a1r - a2i * a1i,
            a2r * a1i + a2i * a1r,
            a2r * b1r - a2i * b1i + b2r,
            a2r * b1i + a2i * b1r + b2i)


def ssm_mixer(u, h0_re, h0_im, a_re, a_im, log_dt, b_re, b_im, c_re, c_im, d_skip):
    bsz, seq_len, _ = u.shape
    uf = u.astype(jnp.float32).reshape(bsz, seq_len, SSM_GROUPS, SSM_GROUP_W)
    lam_re, lam_im, bbar_re, bbar_im = ssm_discretize(a_re, a_im, log_dt, b_re, b_im)
    bu_re = jnp.einsum('blgc,gnc->blgn', uf, bbar_re)
    bu_im = jnp.einsum('blgc,gnc->blgn', uf, bbar_im)
    h0r = h0_re.astype(jnp.float32)
    h0i = h0_im.astype(jnp.float32)
    bu_re = bu_re.at[:, 0].add(lam_re * h0r - lam_im * h0i)
    bu_im = bu_im.at[:, 0].add(lam_re * h0i + lam_im * h0r)
    ar = jnp.broadcast_to(lam_re, bu_re.shape)
    ai = jnp.broadcast_to(lam_im, bu_im.shape)
    _, _, h_re, h_im = lax.associative_scan(_ssm_combine, (ar, ai, bu_re, bu_im), axis=1)
    y = (jnp.einsum('blgn,gcn->blgc', h_re, c_re.astype(jnp.float32))
         - jnp.einsum('blgn,gcn->blgc', h_im, c_im.astype(jnp.float32))
         + d_skip.astype(jnp.float32).reshape(SSM_GROUPS, SSM_GROUP_W) * uf)
    y = y.reshape(bsz, seq_len, SSM_WIDTH)
    return y.astype(u.dtype), h_re[:, -1].astype(h0_re.dtype), h_im[:, -1].astype(h0_im.dtype)


def hybrid_layer(x, pool_buf, h_re, h_im, start_pos, norm1_g, w_in, pool_w, pool_scale, pool_out,
                 ssm_a_re, ssm_a_im, ssm_log_dt, ssm_b_re, ssm_b_im, ssm_c_re, ssm_c_im, ssm_d,
                 ssm_glu, w_out, norm2_g, w_up, w_down):
    h = rmsnorm(x, norm1_g)
    proj = h @ w_in
    v = proj[..., :POOL_WIDTH]
    u = proj[..., POOL_WIDTH:POOL_WIDTH + SSM_WIDTH]
    gates = jax.nn.sigmoid(proj[..., POOL_WIDTH + SSM_WIDTH:].astype(jnp.float32))
    g_pool = gates[..., :D_MODEL]
    g_ssm = gates[..., D_MODEL:]
    yp, new_buf = pool_mixer(v, pool_buf, start_pos, pool_w, pool_scale)
    branch_pool = (yp @ pool_out).astype(jnp.float32)
    ys, new_re, new_im = ssm_mixer(u, h_re, h_im, ssm_a_re, ssm_a_im, ssm_log_dt,
                                   ssm_b_re, ssm_b_im, ssm_c_re, ssm_c_im, ssm_d)
    z = (ys @ ssm_glu).astype(jnp.float32)
    branch_ssm = z[..., :D_MODEL] * jax.nn.sigmoid(z[..., D_MODEL:])
    merged = (g_pool * branch_pool + g_ssm * branch_ssm).astype(x.dtype)
    x = x + merged @ w_out
    h2 = rmsnorm(x, norm2_g)
    x = x + jnp.square(jax.nn.relu(h2 @ w_up)) @ w_down
    return x, new_buf, new_re, new_im


def setup_inputs(seed: int = 0) -> dict:
    key = jax.random.key(seed)
    ks = jax.random.split(key, 24)
    f32 = jnp.float32
    nrm = lambda k, shape, s: jax.random.normal(k, shape, f32) * s
    a_im_base = jnp.broadcast_to(math.pi * jnp.arange(SSM_STATE, dtype=f32), (DEPTH, SSM_GROUPS, SSM_STATE))
    return {
        'x_prompt': nrm(ks[0], (BATCH, SEQ, D_MODEL), 1.0),
        'x_sample': nrm(ks[1], (DEC_BATCH, DEC_SEQ, D_MODEL), 1.0),
        'cache_pool': nrm(ks[2], (DEPTH, DEC_BATCH, POOL_BUF, POOL_WIDTH), 1.0),
        'state_ssm_re': nrm(ks[3], (DEPTH, DEC_BATCH, SSM_GROUPS, SSM_STATE), 0.3),
        'state_ssm_im': nrm(ks[4], (DEPTH, DEC_BATCH, SSM_GROUPS, SSM_STATE), 0.3),
        'norm1_g': 1.0 + nrm(ks[5], (DEPTH, D_MODEL), 0.02),
        'w_in': nrm(ks[6], (DEPTH, D_MODEL, IN_WIDTH), D_MODEL ** -0.5),
        'pool_w': nrm(ks[7], (DEPTH, POOL_GROUPS, POOL_GROUP_W, POOL_GROUP_W), POOL_GROUP_W ** -0.5),
        'pool_scale': 1.0 + nrm(ks[8], (DEPTH, POOL_WIDTH), 0.1),
        'pool_out': nrm(ks[9], (DEPTH, POOL_WIDTH, D_MODEL), POOL_WIDTH ** -0.5),
        'ssm_a_re': -0.5 + nrm(ks[10], (DEPTH, SSM_GROUPS, SSM_STATE), 0.01),
        'ssm_a_im': a_im_base + nrm(ks[11], (DEPTH, SSM_GROUPS, SSM_STATE), 0.01),
        'ssm_log_dt': jax.random.uniform(ks[12], (DEPTH, SSM_GROUPS), f32, math.log(DT_MIN), math.log(DT_MAX)),
        'ssm_b_re': nrm(ks[13], (DEPTH, SSM_GROUPS, SSM_STATE, SSM_GROUP_W), (2 * SSM_GROUP_W) ** -0.5),
        'ssm_b_im': nrm(ks[14], (DEPTH, SSM_GROUPS, SSM_STATE, SSM_GROUP_W), (2 * SSM_GROUP_W) ** -0.5),
        'ssm_c_re': nrm(ks[15], (DEPTH, SSM_GROUPS, SSM_GROUP_W, SSM_STATE), (2 * SSM_STATE) ** -0.5),
        'ssm_c_im': nrm(ks[16], (DEPTH, SSM_GROUPS, SSM_GROUP_W, SSM_STATE), (2 * SSM_STATE) ** -0.5),
        'ssm_d': nrm(ks[17], (DEPTH, SSM_WIDTH), 1.0),
        'ssm_glu': nrm(ks[18], (DEPTH, SSM_WIDTH, 2 * D_MODEL), SSM_WIDTH ** -0.5),
        'w_out': nrm(ks[19], (DEPTH, D_MODEL, D_MODEL), D_MODEL ** -0.5),
        'norm2_g': 1.0 + nrm(ks[20], (DEPTH, D_MODEL), 0.02),
        'w_up': nrm(ks[21], (DEPTH, D_MODEL, D_FF), D_MODEL ** -0.5),
        'w_down': nrm(ks[22], (DEPTH, D_FF, D_MODEL), D_FF ** -0.5),
        'normf_g': 1.0 + nrm(ks[23], (D_MODEL,), 0.02),
    }


def reference(x_prompt, x_sample, cache_pool, state_ssm_re, state_ssm_im, norm1_g, w_in, pool_w,
              pool_scale, pool_out, ssm_a_re, ssm_a_im, ssm_log_dt, ssm_b_re, ssm_b_im, ssm_c_re,
              ssm_c_im, ssm_d, ssm_glu, w_out, norm2_g, w_up, w_down, normf_g):
    xp = x_prompt
    xs = x_sample
    pool_p, re_p, im_p, pool_s, re_s, im_s = [], [], [], [], [], []
    for l in range(DEPTH):
        w = (norm1_g[l], w_in[l], pool_w[l], pool_scale[l], pool_out[l], ssm_a_re[l], ssm_a_im[l],
             ssm_log_dt[l], ssm_b_re[l], ssm_b_im[l], ssm_c_re[l], ssm_c_im[l], ssm_d[l], ssm_glu[l],
             w_out[l], norm2_g[l], w_up[l], w_down[l])
        zero_buf = jnp.zeros((xp.shape[0], POOL_BUF, POOL_WIDTH), cache_pool.dtype)
        zero_h = jnp.zeros((xp.shape[0], SSM_GROUPS, SSM_STATE), state_ssm_re.dtype)
        xp, bp, rp, ip = hybrid_layer(xp, zero_buf, zero_h, zero_h, 0, *w)
        xs, bs, rs, is_ = hybrid_layer(xs, cache_pool[l], state_ssm_re[l], state_ssm_im[l], PAST_LEN, *w)
        pool_p.append(bp)
        re_p.append(rp)
        im_p.append(ip)
        pool_s.append(bs)
        re_s.append(rs)
        im_s.append(is_)
    y_prompt = rmsnorm(xp, normf_g)
    y_sample = rmsnorm(xs, normf_g)
    return (y_prompt, y_sample,
            jnp.stack(pool_p), jnp.stack(re_p), jnp.stack(im_p),
            jnp.stack(pool_s), jnp.stack(re_s), jnp.stack(im_s))
```

```python
import math
SKIP = ''
ENG_M = 'pool'
from contextlib import ExitStack

import numpy as np
import concourse.bass as bass
import concourse.mybir as mybir
from concourse.bass_utils import run_bass_kernel_spmd

F32 = mybir.dt.float32
BF16 = mybir.dt.bfloat16
ALU = mybir.AluOpType
AF = mybir.ActivationFunctionType
AX = mybir.AxisListType

NCORES = 8
D = 1024
SEQ = 2048
NS = 16
NCOL = SEQ + NS
PB = 15
PW = 512
DFF = 4096
EPS = 1e-6
PAD = 16
CHUNKS = [(0, 512), (512, 512), (1024, 512), (1536, 512), (2048, NS)]
TTILES = [(i * 128, 128) for i in range(16)] + [(SEQ, NS)]
MLIST = [1, 2, 3, 4, 5, 6, 7, 8, 16, 24, 32, 64, 96, 128, 256, 384, 512, 1024, 1536]
NM = len(MLIST)
MI = {m: i for i, m in enumerate(MLIST)}
TC = 8
NJ = SEQ // TC
TWO_PI = 2.0 * math.pi


class Reg:
    __slots__ = ("w", "rs")

    def __init__(self):
        self.w = None
        self.rs = []


class Op:
    __slots__ = ("eng", "fn", "deps", "dma", "sig", "idx", "sem", "val", "phase")


class Prog:
    ND = 8

    def __init__(self, nc, es):
        self.nc = nc
        self.es = es
        self.ops = []
        self.phase = 0
        self.esem = {}
        self.ecount = {}
        for e in ("pe", "act", "dve", "pool", "sp"):
            self.esem[e] = es.enter_context(nc.semaphore("s_" + e))
            self.ecount[e] = 0
        self.dsem = {}
        self.dcount = {}
        for q in ("sp", "pool", "act"):
            self.dsem[q] = [es.enter_context(nc.semaphore("d_%s%d" % (q, i))) for i in range(self.ND)]
            self.dcount[q] = 0
        self.waited = {e: {} for e in ("pe", "act", "dve", "pool", "sp")}
        self.out_dmas = []

    def op(self, eng, fn, reads=(), writes=(), dma=False, out=False):
        o = Op()
        o.eng = eng
        o.fn = fn
        o.dma = dma
        o.sig = False
        o.idx = 0
        o.sem = None
        o.val = 0
        o.phase = self.phase
        deps = []
        for r in reads:
            if r.w is not None:
                deps.append(r.w)
        for w in writes:
            if w.w is not None:
                deps.append(w.w)
            deps.extend(w.rs)
        for r in reads:
            r.rs.append(o)
        for w in writes:
            w.w = o
            w.rs = []
        seen = set()
        o.deps = []
        for d in deps:
            if d is o or id(d) in seen:
                continue
            seen.add(id(d))
            o.deps.append(d)
        self.ops.append(o)
        if out:
            self.out_dmas.append(o)
        return o

    def interleave(self, n_head, n0):
        ops = self.ops
        head, a, b = ops[:n_head], ops[n_head:n0], ops[n0:]
        out = list(head)
        ia = 0
        for ib, o in enumerate(b):
            tgt = (ib + 1) * len(a) // max(len(b), 1)
            while ia < tgt:
                out.append(a[ia])
                ia += 1
            out.append(o)
        out.extend(a[ia:])
        self.ops = out

    def flush(self, final=False):
        nc = self.nc
        ops = self.ops
        self.ops = []
        if True:
            fence = Op()
            fence.eng = "sp"
            fence.fn = None
            fence.dma = False
            fence.sig = False
            fence.idx = 0
            fence.sem = None
            fence.val = 0
            fence.phase = self.phase
            fence.deps = [o for o in ops if o.dma] + (list(self.out_dmas) if final else [])
            ops.append(fence)
        for o in ops:
            o.deps = [d for d in o.deps if d.dma or d.phase == self.phase]
        for o in ops:
            for d in o.deps:
                if d.dma:
                    continue
                if d.eng == "pe" and o.eng == "pe":
                    continue
                d.sig = True
        for o in ops:
            if o.fn is None:
                continue
            if o.dma:
                q = o.eng
                j = self.dcount[q]
                self.dcount[q] = j + 1
                o.sem = self.dsem[q][j % self.ND]
                o.val = 16 * (j // self.ND + 1)
            elif o.sig:
                self.ecount[o.eng] += 1
                o.idx = self.ecount[o.eng]
        per = {e: [o for o in ops if o.eng == e] for e in ("pe", "act", "dve", "pool", "sp")}

        def emit(engname, eng):
            wd = self.waited[engname]
            for o in per[engname]:
                need = {}
                for d in o.deps:
                    if d.dma:
                        s, v = d.sem, d.val
                    else:
                        if d.eng == "pe" and o.eng == "pe":
                            continue
                        s, v = self.esem[d.eng], d.idx
                    key = id(s)
                    if key not in need or need[key][1] < v:
                        need[key] = (s, v)
                if o.dma and o.val > 16:
                    key = id(o.sem)
                    v = o.val - 16
                    if key not in need or need[key][1] < v:
                        need[key] = (o.sem, v)
                for key, (s, v) in need.items():
                    if wd.get(key, 0) >= v:
                        continue
                    eng.wait_ge(s, v)
                    wd[key] = v
                if o.fn is None:
                    continue
                ins = o.fn(eng)
                if o.dma:
                    ins.then_inc(o.sem, 16)
                elif o.sig:
                    ins.then_inc(self.esem[engname], 1)

        with nc.Block() as block:
            if per["pe"]:
                @block.tensor
                def _(eng):
                    emit("pe", eng)
            if per["act"]:
                @block.scalar
                def _(eng):
                    emit("act", eng)
            if per["dve"]:
                @block.vector
                def _(eng):
                    emit("dve", eng)
            if per["pool"]:
                @block.gpsimd
                def _(eng):
                    emit("pool", eng)
            if per["sp"]:
                @block.sync
                def _(eng):
                    emit("sp", eng)
        self.phase += 1


class Sub:
    def __init__(self, tl, lo):
        self.tl = tl
        self.lo = lo

    def __getitem__(self, k):
        p, q, c = k
        return self.tl.t[p, self.lo + q, c]


class Tl:
    def __init__(self, t):
        self.t = t
        self.r = Reg()

    def __getitem__(self, k):
        return self.t[k]


def build(dbg=None):
    nc = bass.Bass("TRN2", target_bir_lowering=False)

    def din(name, shape):
        return nc.dram_tensor(name, list(shape), F32, kind="ExternalInput").ap()

    def dout(name, shape):
        return nc.dram_tensor(name, list(shape), F32, kind="ExternalOutput").ap()

    x = din("x", [SEQ, D])
    xs = din("xs", [NS, D])
    cpool = din("cpool", [NS * PB, PW])
    sre = din("sre", [NS, 2048])
    sim = din("sim", [NS, 2048])
    n1g = din("n1g", [1, D])
    w_in = din("w_in", [D, 3072])
    pool_w = din("pool_w", [512, 128])
    pscale = din("pscale", [1, PW])
    pool_out = din("pool_out", [PW, D])
    a_re = din("a_re", [1, 2048])
    a_im = din("a_im", [1, 2048])
    log_dt = din("log_dt", [1, 32])
    b_re = din("b_re", [1, 32 * 64 * 16])
    b_im = din("b_im", [1, 32 * 64 * 16])
    c_re = din("c_re", [512, 64])
    c_im = din("c_im", [512, 64])
    ssm_d = din("ssm_d", [1, 512])
    glu = din("glu", [512, 2048])
    w_out = din("w_out", [D, D])
    n2g = din("n2g", [1, D])
    w_up = din("w_up", [D, DFF])
    w_down = din("w_down", [DFF, D])
    nfg = din("nfg", [1, D])

    y = dout("y", [SEQ, D])
    ysamp = dout("ysamp", [NS, D])
    o_pool_p = dout("o_pool_p", [PB, PW])
    o_re_p = dout("o_re_p", [16, 128])
    o_im_p = dout("o_im_p", [16, 128])
    o_pool_s = dout("o_pool_s", [NS * PB, PW])
    o_re_s = dout("o_re_s", [NS, 2048])
    o_im_s = dout("o_im_s", [NS, 2048])

    g1s = nc.dram_tensor("g1s", [128, 8, NCOL], BF16, kind="Internal").ap()
    g1s_r = [Reg() for _ in CHUNKS]
    g2s = nc.dram_tensor("g2s", [128, 8, NCOL], BF16, kind="Internal").ap()
    g2s_r = [Reg() for _ in CHUNKS]
    dbg_outs = {}

    with ExitStack() as es:
        P = Prog(nc, es)

        def sb(name, shape, dt, stack=None):
            return Tl((stack or es).enter_context(nc.sbuf_tensor(name, list(shape), dt)))

        def pst(name, shape, dt, stack=None):
            return Tl((stack or es).enter_context(nc.psum_tensor(name, list(shape), dt)))

        def dram_ap(base, offset, pattern):
            return bass.AP(base.tensor, offset, pattern)

        ident_f = sb("ident_f", [128, 128], F32)
        ident_b = sb("ident_b", [128, 128], BF16)
        dcol = sb("dcol", [128, 4], F32)
        pscol = sb("pscol", [128, 4], F32)
        rct = sb("rct", [128, 16], F32)
        hT = sb("hT", [128, 8, NCOL], BF16)
        YU = sb("YU", [128, 8, NCOL], BF16)
        sT = es.enter_context(ExitStack())
        lam = sb("lam", [128, 16, NM, 3], F32, sT)
        BbTs = sb("BbTs", [128, 2, 4, TC, 128], BF16, sT)
        CT = sb("CT", [128, 2, 16, 32], BF16, sT)
        Wt = sb("Wt", [128, 2, TC, 16, 32], BF16, sT)
        Kt = sb("Kt", [128, 4, TC, 32], BF16, sT)
        Hnb = sb("Hnb", [128, 2, 16, 16], BF16, sT)
        Ktf = sb("Ktf", [128, 4, TC, 128], BF16, sT)

        sW = es.enter_context(ExitStack())
        wvu = sb("wvu", [128, 8, 1024], BF16, sW)
        s0 = es.enter_context(ExitStack())
        if True:
            ar = sb("ar", [128, 16], F32, s0)
            ai = sb("ai", [128, 16], F32, s0)
            ldt = sb("ldt", [128, 16], F32, s0)
            dtt = sb("dtt", [128, 16], F32, s0)
            adt = sb("adt", [128, 16], F32, s0)
            wdt = sb("wdt", [128, 16], F32, s0)
            MT = sb("MT", [128, 16, NM], F32, s0)
            ARG = sb("ARG", [128, 16, NM], F32, s0)
            EXA = sb("EXA", [128, 16, NM], F32, s0)
            MAG = sb("MAG", [128, 16, NM], F32, s0)
            SA = sb("SA", [128, 16, NM], F32, s0)
            CA = sb("CA", [128, 16, NM], F32, s0)
            SN = sb("SN", [128, 16, NM], F32, s0)
            CS = sb("CS", [128, 16, NM], F32, s0)
            negpi = sb("negpi", [128, 1], F32, s0)
            t1 = sb("t1", [128, 16], F32, s0)
            t2 = sb("t2", [128, 16], F32, s0)
            t3 = sb("t3", [128, 16], F32, s0)
            t4 = sb("t4", [128, 16], F32, s0)
            cre = sb("cre", [128, 16], F32, s0)
            cim = sb("cim", [128, 16], F32, s0)
            Bn_re = sb("Bn_re", [128, 16, 16], F32, s0)
            Bn_im = sb("Bn_im", [128, 16, 16], F32, s0)
            Bb_re = sb("Bb_re", [128, 16, 16], F32, s0)
            Bb_im = sb("Bb_im", [128, 16, 16], F32, s0)
            Bt1 = sb("Bt1", [128, 16, 16], F32, s0)
            Bt2 = sb("Bt2", [128, 16, 16], F32, s0)
            Bpad = sb("Bpad", [128, 2, 16, 32], F32, s0)
            Cin = [sb("Cin%d" % i, [128, 4, 128], F32, s0) for i in range(2)]
            iot = sb("iot", [128, 16], F32, s0)
            ps0 = [pst("ps0_%d" % i, [128, 512], F32, s0) for i in range(4)]

            P.op("pool", lambda e: e.memset(ident_f[:, :], 0.0), writes=[ident_f.r])
            P.op("pool", lambda e: e.affine_select(
                out=ident_f[:, :], in_=ident_f[:, :], pattern=[[-1, 128]], compare_op=ALU.not_equal,
                fill=1.0, base=0, channel_multiplier=1), reads=[ident_f.r], writes=[ident_f.r])
            P.op("dve", lambda e: e.tensor_copy(ident_b[:, :], ident_f[:, :]), reads=[ident_f.r], writes=[ident_b.r])
            P.op("pool", lambda e: e.iota(iot[:, :], [[1, 16]], base=1, channel_multiplier=0,
                                          allow_small_or_imprecise_dtypes=True), writes=[iot.r])
            P.op("dve", lambda e: e.reciprocal(rct[:, :], iot[:, :]), reads=[iot.r], writes=[rct.r])
            P.op("dve", lambda e: e.memset(negpi[:, :], -math.pi), writes=[negpi.r])

            def bcast_row(dst, src_row, n):
                return lambda e: e.dma_start(out=dst[:, :], in_=src_row.partition_broadcast(128))

            def small_t(dst, src):
                def f(e):
                    with nc.allow_non_contiguous_dma(reason="tiny param load"):
                        return e.dma_start(out=dst, in_=src)
                return f
            prow = sb("prow", [16, 4, 128], F32, s0)
            Lb = sb("Lb", [128, 32], F32, s0)
            P.op("pool", lambda e: e.memset(prow[:, :, :], 0.0), writes=[prow.r])
            P.op("sp", lambda e: e.dma_start(out=prow[0:16, 0, :], in_=dram_ap(a_re, 0, [[128, 16], [1, 128]])), writes=[prow.r], dma=True)
            P.op("sp", lambda e: e.dma_start(out=prow[0:16, 1, :], in_=dram_ap(a_im, 0, [[128, 16], [1, 128]])), writes=[prow.r], dma=True)
            P.op("sp", lambda e: e.dma_start(out=prow[0:4, 2, :], in_=dram_ap(ssm_d, 0, [[128, 4], [1, 128]])), writes=[prow.r], dma=True)
            P.op("sp", lambda e: e.dma_start(out=prow[0:4, 3, :], in_=dram_ap(pscale, 0, [[128, 4], [1, 128]])), writes=[prow.r], dma=True)
            P.op("sp", lambda e: e.dma_start(out=Lb[:, :], in_=log_dt.partition_broadcast(128)), writes=[Lb.r], dma=True)
            ptp = ps0[3]

            def ptr_params(e):
                ins = None
                for w_ in range(4):
                    ins = e.transpose(ptp[:, w_ * 16:(w_ + 1) * 16], prow[0:16, w_, :], ident_f[0:16, 0:16])
                return ins
            P.op("pe", ptr_params, reads=[prow.r, ident_f.r], writes=[ptp.r])
            P.op("act", lambda e: e.copy(ar[:, :], ptp[:, 0:16]), reads=[ptp.r], writes=[ar.r, ptp.r])
            P.op("act", lambda e: e.copy(ai[:, :], ptp[:, 16:32]), reads=[ptp.r], writes=[ai.r, ptp.r])
            P.op("act", lambda e: e.copy(dcol[:, :], ptp[:, 32:36]), reads=[ptp.r], writes=[dcol.r, ptp.r])
            P.op("act", lambda e: e.copy(pscol[:, :], ptp[:, 48:52]), reads=[ptp.r], writes=[pscol.r, ptp.r])
            r_ldt = [Reg(), Reg()]
            for h in range(2):
                P.op("dve", (lambda h: lambda e: e.tensor_copy(ldt[h * 64:(h + 1) * 64, :], Lb[h * 64:(h + 1) * 64, h:32:2]))(h),
                     reads=[Lb.r], writes=[r_ldt[h]])
            P.op("sp", lambda e: e.dma_start(out=Bn_re[:, :, :], in_=dram_ap(b_re, 0, [[16, 128], [2048, 16], [1, 16]])),
                 writes=[Bn_re.r], dma=True)
            P.op("sp", lambda e: e.dma_start(out=Bn_im[:, :, :], in_=dram_ap(b_im, 0, [[16, 128], [2048, 16], [1, 16]])),
                 writes=[Bn_im.r], dma=True)

            P.op("act", lambda e: e.activation(out=dtt[:, :], in_=ldt[:, :], func=AF.Exp),
                 reads=r_ldt, writes=[dtt.r])
            P.op("dve", lambda e: e.tensor_mul(adt[:, :], ar[:, :], dtt[:, :]), reads=[ar.r, dtt.r], writes=[adt.r])
            P.op("dve", lambda e: e.tensor_mul(wdt[:, :], ai[:, :], dtt[:, :]), reads=[ai.r, dtt.r], writes=[wdt.r])
            for mi, m in enumerate(MLIST):
                P.op("pool", (lambda mi, m: lambda e: e.memset(MT[:, :, mi], float(m)))(mi, m), writes=[MT.r])
            P.op("dve", lambda e: e.tensor_mul(ARG[:, :, :], MT[:, :, :], wdt[:, :].unsqueeze(2).to_broadcast([128, 16, NM])),
                 reads=[MT.r, wdt.r], writes=[ARG.r])
            P.op("dve", lambda e: e.tensor_mul(EXA[:, :, :], MT[:, :, :], adt[:, :].unsqueeze(2).to_broadcast([128, 16, NM])),
                 reads=[MT.r, adt.r], writes=[EXA.r])
            P.op("act", lambda e: e.activation(out=MAG[:, :, :], in_=EXA[:, :, :], func=AF.Exp),
                 reads=[EXA.r], writes=[MAG.r])
            KI = sb("KI", [128, 16, NM], mybir.dt.int32, s0)
            KF = sb("KF", [128, 16, NM], F32, s0)
            MK = sb("MK", [128, 16, NM], F32, s0)

            def sin_of(dst, A, shift):
                P.op("dve", lambda e: e.tensor_scalar(A[:, :, :], ARG[:, :, :], float(shift), None, ALU.add),
                     reads=[ARG.r], writes=[A.r])
                P.op("dve", lambda e: e.tensor_scalar(KI[:, :, :], A[:, :, :], 1.0 / TWO_PI, None, ALU.mult),
                     reads=[A.r], writes=[KI.r])
                P.op("dve", lambda e: e.tensor_copy(KF[:, :, :], KI[:, :, :]), reads=[KI.r], writes=[KF.r])
                P.op("dve", lambda e: e.scalar_tensor_tensor(out=A[:, :, :], in0=KF[:, :, :], scalar=-TWO_PI, in1=A[:, :, :],
                                                             op0=ALU.mult, op1=ALU.add),
                     reads=[KF.r, A.r], writes=[A.r])
                P.op("dve", lambda e: e.tensor_scalar(MK[:, :, :], A[:, :, :], math.pi, TWO_PI, ALU.is_gt, ALU.mult),
                     reads=[A.r], writes=[MK.r])
                P.op("dve", lambda e: e.tensor_sub(A[:, :, :], A[:, :, :], MK[:, :, :]), reads=[A.r, MK.r], writes=[A.r])
                P.op("dve", lambda e: e.tensor_scalar(MK[:, :, :], A[:, :, :], -math.pi, TWO_PI, ALU.is_lt, ALU.mult),
                     reads=[A.r], writes=[MK.r])
                P.op("dve", lambda e: e.tensor_add(A[:, :, :], A[:, :, :], MK[:, :, :]), reads=[A.r, MK.r], writes=[A.r])
                P.op("dve", lambda e: e.tensor_scalar(A[:, :, :], A[:, :, :], -math.pi, math.pi, ALU.max, ALU.min),
                     reads=[A.r], writes=[A.r])
                P.op("act", lambda e: e.activation(out=dst[:, :, :], in_=A[:, :, :], func=AF.Sin),
                     reads=[A.r], writes=[dst.r])
            sin_of(SN, SA, 0.0)
            sin_of(CS, CA, 0.5 * math.pi)
            r_lre, r_lim, r_lni = Reg(), Reg(), Reg()
            P.op("dve", lambda e: e.tensor_mul(lam[:, :, :, 0], MAG[:, :, :], CS[:, :, :]), reads=[MAG.r, CS.r], writes=[r_lre])
            P.op("dve", lambda e: e.tensor_mul(lam[:, :, :, 1], MAG[:, :, :], SN[:, :, :]), reads=[MAG.r, SN.r], writes=[r_lim])
            P.op("dve", lambda e: e.tensor_scalar(lam[:, :, :, 2], lam[:, :, :, 1], -1.0, None, ALU.mult),
                 reads=[r_lim], writes=[r_lni])
            lam_regs = [r_lre, r_lim, r_lni]

            lr1 = lam[:, :, 0, 0]
            li1 = lam[:, :, 0, 1]
            P.op("dve", lambda e: e.tensor_scalar(t1[:, :], lr1, -1.0, None, ALU.add), reads=[r_lre], writes=[t1.r])
            P.op("dve", lambda e: e.tensor_mul(t2[:, :], ar[:, :], ar[:, :]), reads=[ar.r], writes=[t2.r])
            P.op("dve", lambda e: e.tensor_mul(t3[:, :], ai[:, :], ai[:, :]), reads=[ai.r], writes=[t3.r])
            P.op("dve", lambda e: e.tensor_add(t2[:, :], t2[:, :], t3[:, :]), reads=[t2.r, t3.r], writes=[t2.r])
            P.op("dve", lambda e: e.reciprocal(t2[:, :], t2[:, :]), reads=[t2.r], writes=[t2.r])
            P.op("dve", lambda e: e.tensor_mul(t3[:, :], t1[:, :], ar[:, :]), reads=[t1.r, ar.r], writes=[t3.r])
            P.op("dve", lambda e: e.tensor_mul(t4[:, :], li1, ai[:, :]), reads=[r_lim, ai.r], writes=[t4.r])
            P.op("dve", lambda e: e.tensor_add(t3[:, :], t3[:, :], t4[:, :]), reads=[t3.r, t4.r], writes=[t3.r])
            P.op("dve", lambda e: e.tensor_mul(cre[:, :], t3[:, :], t2[:, :]), reads=[t3.r, t2.r], writes=[cre.r])
            P.op("dve", lambda e: e.tensor_mul(t3[:, :], li1, ar[:, :]), reads=[r_lim, ar.r, cre.r], writes=[t3.r])
            P.op("dve", lambda e: e.tensor_mul(t4[:, :], t1[:, :], ai[:, :]), reads=[t1.r, ai.r, t3.r], writes=[t4.r])
            P.op("dve", lambda e: e.tensor_sub(t3[:, :], t3[:, :], t4[:, :]), reads=[t3.r, t4.r], writes=[t3.r])
            P.op("dve", lambda e: e.tensor_mul(cim[:, :], t3[:, :], t2[:, :]), reads=[t3.r, t2.r], writes=[cim.r])

            creb = cre[:, :].unsqueeze(2).to_broadcast([128, 16, 16])
            cimb = cim[:, :].unsqueeze(2).to_broadcast([128, 16, 16])
            P.op("dve", lambda e: e.tensor_mul(Bt1[:, :, :], Bn_re[:, :, :], creb), reads=[Bn_re.r, cre.r], writes=[Bt1.r])
            P.op("dve", lambda e: e.tensor_mul(Bt2[:, :, :], Bn_im[:, :, :], cimb), reads=[Bn_im.r, cim.r], writes=[Bt2.r])
            P.op("dve", lambda e: e.tensor_sub(Bb_re[:, :, :], Bt1[:, :, :], Bt2[:, :, :]), reads=[Bt1.r, Bt2.r], writes=[Bb_re.r])
            P.op("dve", lambda e: e.tensor_mul(Bt1[:, :, :], Bn_im[:, :, :], creb), reads=[Bn_im.r, cre.r, Bb_re.r], writes=[Bt1.r])
            P.op("dve", lambda e: e.tensor_mul(Bt2[:, :, :], Bn_re[:, :, :], cimb), reads=[Bn_re.r, cim.r, Bb_re.r], writes=[Bt2.r])
            P.op("dve", lambda e: e.tensor_add(Bb_im[:, :, :], Bt1[:, :, :], Bt2[:, :, :]), reads=[Bt1.r, Bt2.r], writes=[Bb_im.r])
            P.op("pool", lambda e: e.memset(Bpad[:, :, :, :], 0.0), writes=[Bpad.r])
            for ri, src in enumerate((Bb_re, Bb_im)):
                for h in range(2):
                    P.op("dve", (lambda ri, src, h: lambda e: e.tensor_copy(
                        Bpad[h * 64:(h + 1) * 64, ri, :, h * 16:(h + 1) * 16], src[h * 64:(h + 1) * 64, :, :]))(ri, src, h),
                        reads=[src.r], writes=[Bpad.r])
            CTf = sb("CTf", [128, 2, 16, 32], F32, s0)
            Bpb = sb("Bpb", [128, 2, 16, 32], BF16, s0)
            BL = [sb("BL%d" % i, [128, 2, 16, 32], F32, s0) for i in range(1)]
            Wtmp = [sb("Wtmp%d" % i, [128, 16, 32], F32, s0) for i in range(4)]
            I32 = sb("I32", [128, 32], F32, s0)
            wk = [0]

            def nxtw():
                t = Wtmp[wk[0] % 4]
                wk[0] += 1
                return t

            def lb(m, c):
                return lam[:, :, MI[m], c].unsqueeze(2).to_broadcast([128, 16, 32])
            k = 0
            for m in range(TC if 'b' not in SKIP else 1):
                if m == 0:
                    srcB = Bpad
                else:
                    srcB = BL[0]
                    w1, w2 = nxtw(), nxtw()
                    P.op("dve", (lambda m, srcB: lambda e: e.tensor_tensor(out=srcB[:, 0, :, :], in0=Bpad[:, 0, :, :], in1=lb(m, 0), op=ALU.mult))(m, srcB),
                         reads=[Bpad.r] + lam_regs, writes=[srcB.r])
                    P.op("dve", (lambda m, w1: lambda e: e.tensor_tensor(out=w1[:, :, :], in0=Bpad[:, 1, :, :], in1=lb(m, 2), op=ALU.mult))(m, w1),
                         reads=[Bpad.r] + lam_regs, writes=[w1.r])
                    P.op("dve", (lambda srcB, w1: lambda e: e.tensor_add(srcB[:, 0, :, :], srcB[:, 0, :, :], w1[:, :, :]))(srcB, w1),
                         reads=[srcB.r, w1.r], writes=[srcB.r])
                    P.op("dve", (lambda m, srcB: lambda e: e.tensor_tensor(out=srcB[:, 1, :, :], in0=Bpad[:, 0, :, :], in1=lb(m, 1), op=ALU.mult))(m, srcB),
                         reads=[Bpad.r] + lam_regs, writes=[srcB.r])
                    P.op("dve", (lambda m, w2: lambda e: e.tensor_tensor(out=w2[:, :, :], in0=Bpad[:, 1, :, :], in1=lb(m, 0), op=ALU.mult))(m, w2),
                         reads=[Bpad.r] + lam_regs, writes=[w2.r])
                    P.op("dve", (lambda srcB, w2: lambda e: e.tensor_add(srcB[:, 1, :, :], srcB[:, 1, :, :], w2[:, :, :]))(srcB, w2),
                         reads=[srcB.r, w2.r], writes=[srcB.r])
                for ri in range(2):
                    for gq in range(4):
                        pt = ps0[k % 4]
                        k += 1
                        P.op("pe", (lambda ri, gq, pt, srcB: lambda e: e.transpose(
                            pt[:, 0:128], srcB[:, ri, gq * 4:(gq + 1) * 4, :], ident_f[:, :]))(ri, gq, pt, srcB),
                            reads=[srcB.r, ident_f.r], writes=[pt.r])
                        P.op("act", (lambda ri, gq, pt, m: lambda e: e.copy(BbTs[:, ri, gq, TC - 1 - m, :], pt[:, 0:128]))(ri, gq, pt, m),
                             reads=[pt.r], writes=[BbTs.r])
            P.op("dve", lambda e: e.tensor_copy(Bpb[:, :, :, :], Bpad[:, :, :, :]), reads=[Bpad.r], writes=[Bpb.r])

            Cnat = [sb("Cnat%d" % i, [128, 4, 64], F32, s0) for i in range(2)]
            pidx = sb("pidx", [128, 1], mybir.dt.int32, s0)
            mko = sb("mko", [128, 1], F32, s0)
            mke = sb("mke", [128, 1], F32, s0)
            P.op("pool", lambda e: e.iota(pidx[:, :], [[0, 1]], base=0, channel_multiplier=1), writes=[pidx.r])
            P.op("dve", lambda e: e.tensor_scalar(pidx[:, :], pidx[:, :], 4, 1, ALU.arith_shift_right, ALU.bitwise_and),
                 reads=[pidx.r], writes=[pidx.r])
            P.op("dve", lambda e: e.tensor_copy(mko[:, :], pidx[:, :]), reads=[pidx.r], writes=[mko.r])
            P.op("dve", lambda e: e.tensor_scalar(mke[:, :], mko[:, :], -1.0, 1.0, ALU.mult, ALU.add), reads=[mko.r], writes=[mke.r])
            for ri, csrc in enumerate((c_re, c_im)):
                cin = Cin[ri]
                cn_ = Cnat[ri]
                P.op("sp", (lambda cn_, csrc: lambda e: e.dma_start(
                    out=cn_[:, :, :], in_=dram_ap(csrc, 0, [[64, 128], [128 * 64, 4], [1, 64]])))(cn_, csrc),
                    writes=[cn_.r], dma=True)
                P.op("dve", (lambda cin, cn_: lambda e: e.tensor_scalar(cin[:, :, 0:64], cn_[:, :, :], mke[:, 0:1], None, ALU.mult))(cin, cn_),
                     reads=[cn_.r, mke.r], writes=[cin.r])
                P.op("dve", (lambda cin, cn_: lambda e: e.tensor_scalar(cin[:, :, 64:128], cn_[:, :, :], mko[:, 0:1], None, ALU.mult))(cin, cn_),
                     reads=[cn_.r, mko.r], writes=[cin.r])
                for r in range(4):
                    pt = ps0[k % 4]
                    k += 1
                    P.op("pe", (lambda cin, r, pt: lambda e: e.transpose(pt[:, 0:128], cin[:, r, :], ident_f[:, :]))(cin, r, pt),
                         reads=[cin.r, ident_f.r], writes=[pt.r])
                    pv = pt[:, 0:128].rearrange("p (a c) -> p a c", c=32)
                    if ri == 0:
                        P.op("act", (lambda r, pv: lambda e: e.copy(CT[:, 0, r * 4:(r + 1) * 4, :], pv))(r, pv),
                             reads=[pt.r], writes=[CT.r, pt.r])
                    else:
                        P.op("act", (lambda r, pv: lambda e: e.mul(CT[:, 1, r * 4:(r + 1) * 4, :], pv, -1.0))(r, pv),
                             reads=[pt.r], writes=[CT.r, pt.r])
                    P.op("dve", (lambda ri, r, pv: lambda e: e.tensor_copy(CTf[:, ri, r * 4:(r + 1) * 4, :], pv))(ri, r, pv),
                         reads=[pt.r], writes=[CTf.r, pt.r])

            for kk in range(TC if 'c' not in SKIP else 0):
                m = kk + 1
                w1, w2, w3, w4 = nxtw(), nxtw(), nxtw(), nxtw()
                P.op("dve", (lambda m, w1: lambda e: e.tensor_tensor(out=w1[:, :, :], in0=CTf[:, 0, :, :], in1=lb(m, 0), op=ALU.mult))(m, w1),
                     reads=[CTf.r] + lam_regs, writes=[w1.r])
                P.op("dve", (lambda m, w2: lambda e: e.tensor_tensor(out=w2[:, :, :], in0=CTf[:, 1, :, :], in1=lb(m, 2), op=ALU.mult))(m, w2),
                     reads=[CTf.r] + lam_regs, writes=[w2.r])
                P.op("dve", (lambda kk, w1, w2: lambda e: e.tensor_add(Wt[:, 0, kk, :, :], w1[:, :, :], w2[:, :, :]))(kk, w1, w2),
                     reads=[w1.r, w2.r], writes=[Wt.r])
                P.op("dve", (lambda m, w3: lambda e: e.tensor_tensor(out=w3[:, :, :], in0=CTf[:, 0, :, :], in1=lb(m, 2), op=ALU.mult))(m, w3),
                     reads=[CTf.r] + lam_regs, writes=[w3.r])
                P.op("dve", (lambda m, w4: lambda e: e.tensor_tensor(out=w4[:, :, :], in0=CTf[:, 1, :, :], in1=lb(m, 0), op=ALU.mult))(m, w4),
                     reads=[CTf.r] + lam_regs, writes=[w4.r])
                P.op("dve", (lambda kk, w3, w4: lambda e: e.tensor_sub(Wt[:, 1, kk, :, :], w3[:, :, :], w4[:, :, :]))(kk, w3, w4),
                     reads=[w3.r, w4.r], writes=[Wt.r])

            for gq in range(4 if 't' not in SKIP else 0):
                pt = ps0[gq]

                def kmm(e, gq=gq, pt=pt):
                    ins = None
                    for a4 in range(4):
                        gp = gq * 4 + a4
                        R0, R1 = a4 * 32, a4 * 32 + 32
                        e.matmul(pt[R0:R1, 0:32], Bpb[:, 0, gp, :], CT[:, 0, gp, :], start=True, stop=False, tile_position=(0, R0))
                        e.matmul(pt[R0:R1, 0:32], Bpb[:, 1, gp, :], CT[:, 1, gp, :], start=False, stop=True, tile_position=(0, R0))
                        e.matmul(pt[R0:R1, 32:256], Bpb[:, 0, gp, :], Wt[:, 0, 0:TC - 1, gp, :], start=True, stop=False, tile_position=(0, R0))
                        ins = e.matmul(pt[R0:R1, 32:256], Bpb[:, 1, gp, :], Wt[:, 1, 0:TC - 1, gp, :], start=False, stop=True, tile_position=(0, R0))
                    return ins
                if "k" not in SKIP:
                    P.op("pe", kmm, reads=[Bpb.r, CT.r, Wt.r], writes=[pt.r])
                P.op("act", (lambda gq, pt: lambda e: e.copy(Kt[:, gq, :, :], pt[:, 0:256].rearrange("p (t c) -> p t c", c=32)))(gq, pt),
                     reads=[pt.r], writes=[Kt.r])
            P.op("dve", lambda e: e.tensor_add(I32[:, :], ident_f[:, 0:32], ident_f[:, 32:64]), reads=[ident_f.r], writes=[I32.r])
            P.op("dve", lambda e: e.tensor_add(I32[:, :], I32[:, :], ident_f[:, 64:96]), reads=[ident_f.r, I32.r], writes=[I32.r])
            P.op("dve", lambda e: e.tensor_add(I32[:, :], I32[:, :], ident_f[:, 96:128]), reads=[ident_f.r, I32.r], writes=[I32.r])
            for gq in range(4 if 't' not in SKIP else 0):
                P.op("dve", (lambda gq: lambda e: e.scalar_tensor_tensor(
                    out=Kt[:, gq, 0, :], in0=I32[:, :], scalar=dcol[:, gq:gq + 1], in1=Kt[:, gq, 0, :],
                    op0=ALU.mult, op1=ALU.add))(gq), reads=[I32.r, Kt.r, dcol.r], writes=[Kt.r])
            P.op("pool", lambda e: e.memset(Ktf[:, :, :, :], 0.0), writes=[Ktf.r])
            for a4 in range(4):
                P.op("dve", (lambda a4: lambda e: e.tensor_copy(
                    Ktf[a4 * 32:(a4 + 1) * 32, :, :, a4 * 32:(a4 + 1) * 32], Kt[a4 * 32:(a4 + 1) * 32, :, :, :]))(a4),
                    reads=[Kt.r], writes=[Ktf.r])
        n_ops_p0 = len(P.ops)

        with ExitStack() as sAC:
            ypT = Sub(YU, 0)
            uT = Sub(YU, 4)
            hT_r = [Reg() for _ in TTILES]
            yp_r = [[Reg() for _ in CHUNKS] for _ in range(4)]
            ys_r = [[Reg() for _ in CHUNKS] for _ in range(4)]

            with ExitStack() as sAB:
                vT_r = [[Reg() for _ in CHUNKS] for _ in range(4)]
                uT_r = [[Reg() for _ in CHUNKS] for _ in range(4)]
                vpad_r = [Reg() for _ in range(4)]
                vTs_r = [Reg() for _ in range(4)]

                with ExitStack() as sA:
                    g1b = sb("g1b", [128, D], F32, sA)
                    P.op("sp", lambda e: e.dma_start(out=g1b[:, :], in_=n1g.partition_broadcast(128)), writes=[g1b.r], dma=True)
                    xt = [sb("xt%d" % i, [128, D], F32, sA) for i in range(2)]
                    sq = [sb("sq%d" % i, [128, D], BF16, sA) for i in range(1)] * 2
                    hb = [sb("hb%d" % i, [128, D], BF16, sA) for i in range(2)]
                    ssq = [sb("ssq%d" % i, [128, 1], F32, sA) for i in range(2)]
                    rstd = [sb("rstd%d" % i, [128, 1], F32, sA) for i in range(2)]
                    ptr = [pst("ptr%d" % i, [128, 8, 128], BF16, sA) for i in range(2)]
                    pmm = [pst("pmm%d" % i, [128, 512], F32, sA) for i in range(2)]

                    wvu_r = [Reg() for _ in range(8)]
                    for kk in range(8):
                        P.op("pool", (lambda kk: lambda e: e.dma_start(
                            out=wvu[:, kk, :], in_=w_in[kk * 128:(kk + 1) * 128, 0:1024]))(kk),
                            writes=[wvu_r[kk]], dma=True)

                    pend_a = []
                    for ti, (t0, tn) in enumerate(TTILES):
                        b = ti % 2
                        src = x[t0:t0 + tn, :] if ti < 16 else xs[:, :]
                        P.op("sp", (lambda b, tn, src: lambda e: e.dma_start(out=xt[b][0:tn, :], in_=src))(b, tn, src),
                             writes=[xt[b].r], dma=True)
                        P.op("act", (lambda b, tn: lambda e: e.activation(
                            out=sq[b][0:tn, :], in_=xt[b][0:tn, :], func=AF.Square, accum_out=ssq[b][0:tn, :]))(b, tn),
                            reads=[xt[b].r], writes=[sq[b].r, ssq[b].r])
                        P.op("dve", (lambda b, tn: lambda e: e.tensor_scalar(
                            rstd[b][0:tn, :], ssq[b][0:tn, :], 1.0 / D, EPS, ALU.mult, ALU.add))(b, tn),
                            reads=[ssq[b].r], writes=[rstd[b].r])
                        P.op("act", (lambda b, tn: lambda e: e.activation(
                            out=rstd[b][0:tn, :], in_=rstd[b][0:tn, :], func=AF.Sqrt))(b, tn),
                            reads=[rstd[b].r], writes=[rstd[b].r])
                        P.op("dve", (lambda b, tn: lambda e: e.reciprocal(rstd[b][0:tn, :], rstd[b][0:tn, :]))(b, tn),
                             reads=[rstd[b].r], writes=[rstd[b].r])
                        P.op("dve", (lambda b, tn: lambda e: e.scalar_tensor_tensor(
                            out=hb[b][0:tn, :], in0=xt[b][0:tn, :], scalar=rstd[b][0:tn, 0:1], in1=g1b[0:tn, :],
                            op0=ALU.mult, op1=ALU.mult))(b, tn),
                            reads=[xt[b].r, rstd[b].r, g1b.r], writes=[hb[b].r])

                        def tr(e, b=b, tn=tn):
                            ins = None
                            for kk in range(8):
                                ins = e.transpose(ptr[b][:, kk, 0:tn], hb[b][0:tn, kk * 128:(kk + 1) * 128], ident_b[0:tn, 0:tn])
                            return ins
                        def late_a(tr=tr, b=b, t0=t0, tn=tn, ti=ti):
                            P.op("pe", tr, reads=[hb[b].r, ident_b.r], writes=[ptr[b].r])
                            P.op("act", (lambda b, t0, tn: lambda e: e.copy(hT[:, :, t0:t0 + tn], ptr[b][:, :, 0:tn]))(b, t0, tn),
                                 reads=[ptr[b].r], writes=[hT_r[ti], ptr[b].r])
                        while pend_a:
                            pend_a.pop(0)()
                        pend_a.append(late_a)
                    while pend_a:
                        pend_a.pop(0)()

                    k = 0
                    for ci, (c0, cn) in enumerate(CHUNKS):
                        tiles = [ti for ti, (t0, tn) in enumerate(TTILES) if t0 >= c0 and t0 < c0 + cn]
                        for m in range(4, 8):
                            pt = pmm[k % 2]
                            k += 1

                            def mmf(e, m=m, c0=c0, cn=cn, pt=pt):
                                ins = None
                                for kk in range(8):
                                    ins = e.matmul(pt[:, 0:cn], wvu[:, kk, m * 128:(m + 1) * 128], hT[:, kk, c0:c0 + cn],
                                                   start=(kk == 0), stop=(kk == 7))
                                return ins
                            P.op("pe", mmf, reads=[hT_r[ti] for ti in tiles] + wvu_r, writes=[pt.r])
                            P.op("dve", (lambda m, c0, cn, pt: lambda e: e.tensor_copy(
                                uT[:, m - 4, c0:c0 + cn], pt[:, 0:cn]))(m, c0, cn, pt),
                                reads=[pt.r], writes=[uT_r[m - 4][ci], pt.r])
                    P.interleave(8, n_ops_p0)
                    P.flush()

                s0.close()
                if dbg == "A":
                    dbg_outs["d_hT"] = (hT, [128, 8 * NCOL], BF16)
                    dbg_outs["d_YU"] = (YU, [128, 8 * NCOL], BF16)
                    dbg_outs["d_lam"] = (lam, [128, 16 * NM * 3], F32)
                    dbg_outs["d_BbTs"] = (BbTs, [128, 2 * 4 * TC * 128], BF16)
                    dbg_outs["d_CT"] = (CT, [128, 2 * 16 * 32], BF16)
                    dbg_outs["d_Wt"] = (Wt, [128, 2 * TC * 16 * 32], BF16)
                    dbg_outs["d_Kt"] = (Kt, [128, 4 * TC * 32], BF16)
                    _emit_dbg(nc, P, dbg_outs, es)
                    P.flush(final=True)
                    return nc, list(dbg_outs.keys())

                with ExitStack() as sB:
                    vT = sb("vT", [128, 4, PAD + SEQ], F32, sB)
                    vTs = sb("vTs", [128, 4, NS, 16], F32, sB)
                    pw_b = sb("pw_b", [128, 4, 128], BF16, sB)
                    Abuf = sb("Abuf", [128, PAD + SEQ], F32, sB)
                    Bbuf = sb("Bbuf", [128, PAD + SEQ], F32, sB)
                    ptmp = sb("ptmp", [128, 16], F32, sB)
                    wsum = sb("wsum", [128, 16], F32, sB)
                    cpl = [sb("cpl%d" % i, [120, 512], F32, sB) for i in range(2)]
                    tok15 = sb("tok15", [16, 512], F32, sB)
                    toks = sb("toks", [16, 512], F32, sB)
                    H0 = sb("H0", [128, 2, 16, 16], F32, sB)
                    Hn = sb("Hn", [128, 2, 16, 16], F32, sB)
                    BUs = sb("BUs", [128, 2, 16, 16], F32, sB)
                    T1 = sb("T1", [128, 16, 16], F32, sB)
                    T2 = sb("T2", [128, 16, 16], F32, sB)
                    srow = [Abuf, Bbuf]
                    pB = [pst("pB%d" % i, [128, 512], F32, sB) for i in range(8)]
                    pbk = [0]

                    def nextp():
                        t = pB[pbk[0] % 8]
                        pbk[0] += 1
                        return t

                    pw_r = [Reg() for _ in range(4)]
                    for q in range(4):
                        P.op("pool", (lambda q: lambda e: e.dma_start(out=pw_b[:, q, :], in_=pool_w[q * 128:(q + 1) * 128, :]))(q),
                             writes=[pw_r[q]], dma=True)
                    for j in range(2):
                        P.op("sp", (lambda j: lambda e: e.dma_start(out=cpl[j][:, :], in_=cpool[j * 120:(j + 1) * 120, :]))(j),
                             writes=[cpl[j].r], dma=True)
                    if 'd' not in SKIP:
                      P.op("sp", lambda e: e.dma_start(
                        out=dram_ap(o_pool_s, 0, [[PB * PW, NS], [1, 14 * PW]]),
                        in_=dram_ap(cpool, PW, [[PB * PW, NS], [1, 14 * PW]])), dma=True, out=True)
                    for j in range(2 if 't' not in SKIP else 0):
                        for q in range(4):
                            pt = nextp()
                            P.op("pe", (lambda j, q, pt: lambda e: e.transpose(
                                pt[:, 0:120], cpl[j][0:120, q * 128:(q + 1) * 128], ident_f[0:120, 0:120]))(j, q, pt),
                                reads=[cpl[j].r], writes=[pt.r])
                            P.op("act", (lambda j, q, pt: lambda e: e.copy(
                                vTs[:, q, j * 8:(j + 1) * 8, 0:15], pt[:, 0:120].rearrange("p (b r) -> p b r", r=15)))(j, q, pt),
                                reads=[pt.r], writes=[vTs_r[q]])

                    P.op("dve", lambda e: e.memset(vT[:, :, 0:PAD], 0.0), writes=vpad_r)
                    for m in range(4):
                        for ci, (c0, cn) in enumerate(CHUNKS):
                            pt = nextp()

                            def mmv(e, m=m, c0=c0, cn=cn, pt=pt):
                                ins = None
                                for kk in range(8):
                                    ins = e.matmul(pt[:, 0:cn], wvu[:, kk, m * 128:(m + 1) * 128], hT[:, kk, c0:c0 + cn],
                                                   start=(kk == 0), stop=(kk == 7))
                                return ins
                            P.op("pe", mmv, writes=[pt.r])
                            if ci < 4:
                                P.op("act", (lambda m, c0, cn, pt: lambda e: e.copy(
                                    vT[:, m, PAD + c0:PAD + c0 + cn], pt[:, 0:cn]))(m, c0, cn, pt),
                                    reads=[pt.r], writes=[vT_r[m][ci], pt.r])
                            else:
                                P.op("act", (lambda m, pt: lambda e: e.copy(vTs[:, m, :, 15], pt[:, 0:NS]))(m, pt),
                                     reads=[pt.r], writes=[vT_r[m][ci], pt.r])
                    P.op("dve", lambda e: e.memset(Abuf[:, 0:PAD], 0.0), writes=[Abuf.r])
                    P.op("dve", lambda e: e.memset(Bbuf[:, 0:PAD], 0.0), writes=[Bbuf.r])

                    for q in range(4):
                        w = 2 ** (q + 1)
                        allv = [vT_r[q][ci] for ci in range(4)] + [vpad_r[q]]
                        src = vT
                        P.op("dve", (lambda q: lambda e: e.tensor_add(
                            Abuf[:, PAD:], vT[:, q, PAD:], vT[:, q, PAD - 1:PAD + SEQ - 1]))(q),
                            reads=allv, writes=[Abuf.r])
                        cur, oth = Abuf, Bbuf
                        sh = 2
                        while sh < w:
                            P.op("dve", (lambda cur, oth, sh: lambda e: e.tensor_add(
                                oth[:, PAD:], cur[:, PAD:], cur[:, PAD - sh:PAD + SEQ - sh]))(cur, oth, sh),
                                reads=[cur.r], writes=[oth.r])
                            cur, oth = oth, cur
                            sh *= 2
                        P.op("dve", (lambda q, cur, w: lambda e: e.scalar_tensor_tensor(
                            out=ypT[:, q, 0:SEQ], in0=cur[:, PAD:], scalar=1.0 / w, in1=vT[:, q, PAD:],
                            op0=ALU.mult, op1=ALU.subtract))(q, cur, w),
                            reads=[cur.r] + allv, writes=[yp_r[q][ci] for ci in range(4)])
                        if w > 1:
                            P.op("dve", (lambda q, cur, w: lambda e: e.tensor_mul(
                                ptmp[:, 0:w - 1], cur[:, PAD:PAD + w - 1], rct[:, 0:w - 1]))(q, cur, w),
                                reads=[cur.r], writes=[ptmp.r])
                            P.op("dve", (lambda q, w: lambda e: e.tensor_sub(
                                ypT[:, q, 0:w - 1], ptmp[:, 0:w - 1], vT[:, q, PAD:PAD + w - 1]))(q, w),
                                reads=[ptmp.r] + allv, writes=[yp_r[q][0]])
                        P.op("dve", (lambda q, w: lambda e: e.tensor_reduce(
                            out=wsum[:, :], in_=vTs[:, q, :, 16 - w:16], axis=AX.X, op=ALU.add))(q, w),
                            reads=[vTs_r[q], vT_r[q][4]], writes=[wsum.r])
                        P.op("dve", (lambda q, w: lambda e: e.scalar_tensor_tensor(
                            out=ypT[:, q, SEQ:NCOL], in0=wsum[:, :], scalar=1.0 / w, in1=vTs[:, q, :, 15],
                            op0=ALU.mult, op1=ALU.subtract))(q, w),
                            reads=[wsum.r, vT_r[q][4]], writes=[yp_r[q][4]])
                        if 'o' in SKIP:
                            continue
                        pt = nextp()
                        P.op("pe", (lambda q, pt: lambda e: e.transpose(
                            pt[0:16, 0:128], vT[:, q, PAD + SEQ - 16:PAD + SEQ], ident_f[:, :]))(q, pt),
                            reads=[vT_r[q][3]], writes=[pt.r])
                        P.op("act", (lambda q, pt: lambda e: e.copy(tok15[0:16, q * 128:(q + 1) * 128], pt[0:16, 0:128]))(q, pt),
                             reads=[pt.r], writes=[tok15.r])
                        pt = nextp()
                        P.op("pe", (lambda q, pt: lambda e: e.transpose(
                            pt[0:16, 0:128], vTs[:, q, :, 15], ident_f[:, :]))(q, pt),
                            reads=[vT_r[q][4]], writes=[pt.r])
                        P.op("act", (lambda q, pt: lambda e: e.copy(toks[0:16, q * 128:(q + 1) * 128], pt[0:16, 0:128]))(q, pt),
                             reads=[pt.r], writes=[toks.r])
                        for ci, (c0, cn) in enumerate(CHUNKS):
                            pt = nextp()
                            P.op("pe", (lambda q, c0, cn, pt: lambda e: e.matmul(
                                pt[:, 0:cn], pw_b[:, q, :], ypT[:, q, c0:c0 + cn], start=True, stop=True))(q, c0, cn, pt),
                                reads=[pw_r[q], yp_r[q][ci]], writes=[pt.r])
                            P.op("act", (lambda q, c0, cn, pt: lambda e: e.mul(
                                ypT[:, q, c0:c0 + cn], pt[:, 0:cn], pscol[:, q:q + 1]))(q, c0, cn, pt),
                                reads=[pt.r], writes=[yp_r[q][ci]])
                    P.op("sp", lambda e: e.dma_start(out=o_pool_p[:, :], in_=tok15[1:16, :]), reads=[tok15.r], dma=True, out=True)
                    P.op("sp", lambda e: e.dma_start(out=dram_ap(o_pool_s, 14 * PW, [[PB * PW, NS], [1, PW]]), in_=toks[0:16, :]),
                         reads=[toks.r], dma=True, out=True)

                    if dbg == "B1":
                        P.flush()
                        dbg_outs["d_YU"] = (YU, [128, 8 * NCOL], BF16)
                        _emit_dbg(nc, P, dbg_outs, es)
                        P.flush(final=True)
                        return nc, list(dbg_outs.keys()) + ["o_pool_p", "o_pool_s"]
                    P.op("sp", lambda e: e.dma_start(out=srow[0][0:16, 0:2048], in_=sre[:, :]), writes=[srow[0].r], dma=True)
                    P.op("sp", lambda e: e.dma_start(out=srow[1][0:16, 0:2048], in_=sim[:, :]), writes=[srow[1].r], dma=True)
                    for ri in range(2):
                        pt = nextp()

                        def trs(e, ri=ri, pt=pt):
                            ins = None
                            for gp in range(16):
                                ins = e.transpose(pt[:, gp * 16:(gp + 1) * 16], srow[ri][0:16, gp * 128:(gp + 1) * 128], ident_f[0:16, 0:16])
                            return ins
                        if '1' not in SKIP:
                            P.op("pe", trs, reads=[srow[ri].r], writes=[pt.r])
                        P.op("act", (lambda ri, pt: lambda e: e.copy(
                            H0[:, ri, :, :], pt[:, 0:256].rearrange("p (g b) -> p g b", b=16)))(ri, pt),
                            reads=[pt.r], writes=[H0.r])
                    for ri in range(2):
                        for a4 in range(4):
                            pt = nextp()

                            def bus(e, ri=ri, pt=pt, a4=a4):
                                ins = None
                                for gq in range(4):
                                    ins = e.matmul(pt[:, gq * 16:(gq + 1) * 16], BbTs[a4 * 32:(a4 + 1) * 32, ri, gq, TC - 1, :],
                                                   uT[a4 * 32:(a4 + 1) * 32, gq, SEQ:NCOL], start=True, stop=True,
                                                   tile_position=(a4 * 32, 0))
                                return ins
                            if '2' not in SKIP:
                                P.op("pe", bus, reads=[uT_r[gq][4] for gq in range(4)], writes=[pt.r])
                            P.op("act", (lambda ri, pt, a4: lambda e: e.copy(
                                BUs[:, ri, a4:16:4, :], pt[:, 0:64].rearrange("p (g b) -> p g b", b=16)))(ri, pt, a4),
                                reads=[pt.r], writes=[BUs.r])
                    lreb = lam[:, :, 0, 0].unsqueeze(2).to_broadcast([128, 16, 16])
                    limb = lam[:, :, 0, 1].unsqueeze(2).to_broadcast([128, 16, 16])
                    P.op("dve", lambda e: e.tensor_mul(T1[:, :, :], H0[:, 0, :, :], lreb), reads=[H0.r], writes=[T1.r])
                    P.op("dve", lambda e: e.tensor_mul(T2[:, :, :], H0[:, 1, :, :], limb), reads=[H0.r], writes=[T2.r])
                    P.op("dve", lambda e: e.tensor_sub(T1[:, :, :], T1[:, :, :], T2[:, :, :]), reads=[T1.r, T2.r], writes=[T1.r])
                    P.op("dve", lambda e: e.tensor_add(Hn[:, 0, :, :], T1[:, :, :], BUs[:, 0, :, :]), reads=[T1.r, BUs.r], writes=[Hn.r])
                    P.op("dve", lambda e: e.tensor_mul(T1[:, :, :], H0[:, 1, :, :], lreb), reads=[H0.r, Hn.r], writes=[T1.r])
                    P.op("dve", lambda e: e.tensor_mul(T2[:, :, :], H0[:, 0, :, :], limb), reads=[H0.r, Hn.r], writes=[T2.r])
                    P.op("dve", lambda e: e.tensor_add(T1[:, :, :], T1[:, :, :], T2[:, :, :]), reads=[T1.r, T2.r], writes=[T1.r])
                    r_hn1 = Reg()
                    P.op("dve", lambda e: e.tensor_add(Hn[:, 1, :, :], T1[:, :, :], BUs[:, 1, :, :]), reads=[T1.r, BUs.r, Hn.r], writes=[r_hn1])
                    P.op("act", lambda e: e.copy(Hnb[:, :, :, :], Hn[:, :, :, :]), reads=[Hn.r, r_hn1], writes=[Hnb.r])
                    for ri in range(2):
                        for g4 in range(4):
                            pt = nextp()

                            def tro(e, ri=ri, g4=g4, pt=pt):
                                ins = None
                                for a in range(4):
                                    gp = g4 * 4 + a
                                    ins = e.transpose(pt[0:16, a * 128:(a + 1) * 128], Hn[:, ri, gp, :], ident_f[:, :])
                                return ins
                            if '3' not in SKIP:
                                P.op("pe", tro, reads=[Hn.r, r_hn1], writes=[pt.r])
                            P.op("act", (lambda ri, g4, pt: lambda e: e.copy(
                                srow[ri][0:16, g4 * 512:(g4 + 1) * 512], pt[0:16, 0:512]))(ri, g4, pt),
                                reads=[pt.r], writes=[srow[ri].r])
                    P.op("sp", lambda e: e.dma_start(out=o_re_s[:, :], in_=srow[0][0:16, 0:2048]), reads=[srow[0].r], dma=True, out=True)
                    P.op("sp", lambda e: e.dma_start(out=o_im_s[:, :], in_=srow[1][0:16, 0:2048]), reads=[srow[1].r], dma=True, out=True)

                    if dbg == "B2":
                        P.flush()
                        dbg_outs["d_YU"] = (YU, [128, 8 * NCOL], BF16)
                        _emit_dbg(nc, P, dbg_outs, es)
                        P.flush(final=True)
                        return nc, list(dbg_outs.keys()) + ["o_pool_p", "o_pool_s", "o_re_s", "o_im_s"]
                    P.flush()

            sW.close()
            with ExitStack() as sS:
                Eall = sb("Eall", [128, 2, 16, NJ], F32, sS)
                Sbf = sb("Sbf", [128, 2, 16, NJ], BF16, sS)
                uD = sb("uD", [128, 4, TC, NJ], BF16, sS)
                wg1 = sb("wg1", [128, 8, 1024], BF16, sS)
                stg = [sb("stg%d" % i, [128, 512], BF16, sS) for i in range(2)]
                wg1_r = [Reg() for _ in range(8)]
                for kk in range(8):
                    P.op("pool", (lambda kk: lambda e: e.dma_start(out=wg1[:, kk, :], in_=w_in[kk * 128:(kk + 1) * 128, 1024:2048]))(kk),
                         writes=[wg1_r[kk]], dma=True)
                tmpS = [sb("tmpS%d" % i, [128, 16, 64], F32, sS) for i in range(2)]
                tmpSs = [sb("tmpSs%d" % i, [128, 16, 16], F32, sS) for i in range(2)]
                Hfin = sb("Hfin", [128, 2, 16], F32, sS)
                fin = sb("fin", [16, 2, 128], F32, sS)
                pS = pst("pS", [128, 8, 512], F32, sS)
                bank_r = [[Reg() for _ in range(4)] for _ in range(8)]
                uTg_r = [Reg() for _ in range(16)]
                LV2 = [(4 ** l, 4 ** l - 1, NJ // (4 ** l)) for l in range(5)]
                cls = [dict() for _ in range(2)]
                for ri in range(2):
                    for l in range(4):
                        for kk in range(3):
                            cls[ri][(l, kk)] = Reg()
                    cls[ri][(4, 0)] = Reg()
                allcls = [list(cls[ri].values()) for ri in range(2)]

                def deeper(ri, l):
                    return [r for (ll, kk), r in cls[ri].items() if ll > l]

                uD_r = [Reg() for _ in range(4)]
                uTq_r = [Reg() for _ in range(4)]
                for gq in range(4):
                    src_v = YU.t[:, 4 + gq, 0:SEQ].rearrange("p (j k) -> p k j", k=TC)
                    if gq % 2 == 0:
                        P.op("act", (lambda gq, src_v: lambda e: e.copy(uD[:, gq, :, :], src_v))(gq, src_v), reads=[uTq_r[gq]], writes=[uD_r[gq]])
                    else:
                        P.op("pool", (lambda gq, src_v: lambda e: e.tensor_copy(uD[:, gq, :, :], src_v))(gq, src_v), reads=[uTq_r[gq]], writes=[uD_r[gq]])
                bbts_r = [Reg()]
                wg2_v = BbTs.t[:].rearrange("p a b c d -> p (a b c d)").rearrange("p (k c) -> p k c", c=1024)
                idx = 0
                for gq in range(4):
                    for ri in range(2):
                        bks = [(idx * 4 + a4) % 8 for a4 in range(4)]
                        idx += 1

                        def emm(e, ri=ri, bks=bks, gq=gq):
                            ins = None
                            for s_ in range(TC):
                                for a4 in range(4):
                                    R0, R1 = a4 * 32, a4 * 32 + 32
                                    ins = e.matmul(pS[:, bks[a4], 0:NJ], BbTs[R0:R1, ri, gq, s_, :], uD[R0:R1, gq, s_, :],
                                                   start=(s_ == 0), stop=(s_ == TC - 1), tile_position=(R0, 0))
                            return ins
                        wr = []
                        for a4 in range(4):
                            wr += bank_r[bks[a4]]
                        P.op("pe", emm, reads=[uD_r[gq]] + bbts_r, writes=wr)
                        for a4 in range(4):
                            gp = gq * 4 + a4
                            P.op("act" if a4 % 2 == 0 else "dve", (lambda ri, gp, bk: lambda e: (
                                e.copy(Eall[:, ri, gp, :].rearrange("p (k jj) -> p jj k", k=4),
                                       pS[:, bk, 0:NJ].rearrange("p (jj k) -> p jj k", k=4)) if hasattr(e, "activation")
                                else e.tensor_copy(Eall[:, ri, gp, :].rearrange("p (k jj) -> p jj k", k=4),
                                                   pS[:, bk, 0:NJ].rearrange("p (jj k) -> p jj k", k=4))))(ri, gp, bks[a4]),
                                reads=[], writes=allcls[ri] + bank_r[bks[a4]])

                it_g_ = [0]
                gate_mm_regs = [Reg()]

                def gate_half(half):
                    gs_, gs_r = ((g1s, g1s_r), (g2s, g2s_r))[half]
                    wsel = wg1 if half == 0 else None
                    for ci, (c0, cn) in enumerate(CHUNKS):
                        for j in range(8):
                            bk = 6 + it_g_[0] % 2
                            sb_ = stg[it_g_[0] % 2]
                            it_g_[0] += 1

                            def g1mm(e, j=j, c0=c0, cn=cn, bk=bk, half=half):
                                ins = None
                                for kk in range(8):
                                    wv = wg1[:, kk, j * 128:(j + 1) * 128] if half == 0 else wg2_v[:, kk, j * 128:(j + 1) * 128]
                                    ins = e.matmul(pS[:, bk, 0:cn], wv, hT[:, kk, c0:c0 + cn],
                                                   start=(kk == 0), stop=(kk == 7))
                                return ins
                            P.op("pe", g1mm, reads=(wg1_r if half == 0 else bbts_r) + gate_mm_regs, writes=bank_r[bk])
                            P.op("act", (lambda cn, bk, sb_: lambda e: e.activation(out=sb_[:, 0:cn], in_=pS[:, bk, 0:cn], func=AF.Sigmoid))(cn, bk, sb_),
                                 reads=[], writes=bank_r[bk] + [sb_.r])
                            P.op("sp", (lambda j, c0, cn, sb_, gs_: lambda e: e.dma_start(out=gs_[:, j, c0:c0 + cn], in_=sb_[:, 0:cn]))(j, c0, cn, sb_, gs_),
                                 reads=[sb_.r], writes=[gs_r[ci]], dma=True)

                for kk in range(8):
                    P.op("pool", (lambda kk: lambda e: e.dma_start(out=wg2_v[:, kk, :], in_=w_in[kk * 128:(kk + 1) * 128, 2048:3072]))(kk),
                         writes=bbts_r, dma=True)
                gate_half(0)
                gate_half(1)
                tk = [0]

                def cstepb(m, tgt, srcv, nel, treg, sreg):
                    mi = MI[m]
                    prods = ((0, 0, 0), (1, 2, 0), (0, 1, 1), (1, 0, 1))
                    for (sri, lc, tri) in prods:
                        if nel <= 16:
                            tb_ = (tmpS + tmpSs)[tk[0] % 4]
                        else:
                            tb_ = tmpS[tk[0] % 2]
                        tk[0] += 1
                        Sv = Eall[:, sri, :, srcv]
                        Tv = Eall[:, tri, :, tgt]
                        lv = lam[:, :, mi, lc].unsqueeze(2).to_broadcast([128, 16, nel])
                        tv = tb_[:, :, 0:nel]
                        if nel >= 16 and ENG_M == "pool":
                            GS = 10
                            Sv1, Sv2 = Eall[:, sri, 0:GS, srcv], Eall[:, sri, GS:16, srcv]
                            lv1 = lam[:, 0:GS, mi, lc].unsqueeze(2).to_broadcast([128, GS, nel])
                            lv2 = lam[:, GS:16, mi, lc].unsqueeze(2).to_broadcast([128, 16 - GS, nel])
                            tv1, tv2 = tb_[:, 0:GS, 0:nel], tb_[:, GS:16, 0:nel]
                            r1, r2 = Reg(), Reg()
                            P.op("pool", (lambda tv1, Sv1, lv1: lambda e: e.tensor_tensor(out=tv1, in0=Sv1, in1=lv1, op=ALU.mult))(tv1, Sv1, lv1),
                                 reads=sreg(sri) + [tb_.r], writes=[r1])
                            P.op("dve", (lambda tv2, Sv2, lv2: lambda e: e.tensor_tensor(out=tv2, in0=Sv2, in1=lv2, op=ALU.mult))(tv2, Sv2, lv2),
                                 reads=sreg(sri) + [tb_.r], writes=[r2])
                            P.op("dve", (lambda tv, Tv: lambda e: e.tensor_add(Tv, Tv, tv))(tv, Tv),
                                 reads=[r1, r2] + treg(tri), writes=treg(tri) + [tb_.r])
                            continue
                        P.op(ENG_M, (lambda tv, Sv, lv: lambda e: e.tensor_tensor(out=tv, in0=Sv, in1=lv, op=ALU.mult))(tv, Sv, lv),
                             reads=sreg(sri), writes=[tb_.r])
                        P.op("dve", (lambda tv, Tv: lambda e: e.tensor_add(Tv, Tv, tv))(tv, Tv),
                             reads=[tb_.r] + treg(tri), writes=treg(tri))

                QB = NJ // 4

                def sl(l, kk, j0, cnt):
                    if l == 0:
                        return slice(kk * QB + j0, kk * QB + j0 + cnt)
                    sg = 4 ** (l - 1)
                    first = 3 * QB + (sg - 1) + sg * (4 * j0 + kk)
                    return slice(first, first + 4 * sg * (cnt - 1) + 1, 4 * sg)

                def scan_level(l):
                    sig, off, n = LV2[l]
                    if n == 1:
                        return
                    nch = n // 4
                    for kk in (1, 2, 3):
                        tgt = sl(l, kk, 0, nch)
                        srcv = sl(l, kk - 1, 0, nch)
                        if kk < 3:
                            treg = (lambda kk: lambda ri: [cls[ri][(l, kk)]])(kk)
                        else:
                            treg = lambda ri: deeper(ri, l)
                        sreg = (lambda kk: lambda ri: [cls[ri][(l, kk - 1)]])(kk)
                        cstepb(TC * sig, tgt, srcv, nch, treg, sreg)
                    scan_level(l + 1)
                    if nch > 1:
                        for kk in (0, 1, 2):
                            tgt = sl(l, kk, 1, nch - 1)
                            srcv = sl(l, 3, 0, nch - 1)
                            treg = (lambda kk: lambda ri: [cls[ri][(l, kk)]])(kk)
                            sreg = lambda ri: deeper(ri, l)
                            cstepb(TC * sig * (kk + 1), tgt, srcv, nch - 1, treg, sreg)
                scan_level(0)

                P.op("pool", lambda e: e.memset(Sbf[:, :, :, 0:1], 0.0), writes=[Sbf.r])
                for ri in range(2):
                    P.op("act", (lambda ri: lambda e: e.copy(Hfin[:, ri, :], Eall[:, ri, :, NJ - 1]))(ri),
                         reads=allcls[ri], writes=[Hfin.r])
                    for k4 in range(4):
                        cnt = QB if k4 < 3 else QB - 1
                        P.op("act", (lambda ri, k4, cnt: lambda e: e.copy(
                            Sbf[:, ri, :, 1 + k4:1 + k4 + 4 * (cnt - 1) + 1:4], Eall[:, ri, :, k4 * QB:k4 * QB + cnt]))(ri, k4, cnt),
                            reads=allcls[ri], writes=[Sbf.r])
                for ri in range(2):
                    bk = ri
                    P.op("pe", (lambda ri, bk: lambda e: e.transpose(pS[0:16, bk, 0:128], Hfin[:, ri, :], ident_f[:, :]))(ri, bk),
                         reads=[Hfin.r], writes=bank_r[bk])
                    P.op("act", (lambda ri, bk: lambda e: e.copy(fin[0:16, ri, :], pS[0:16, bk, 0:128]))(ri, bk),
                         reads=bank_r[bk], writes=[fin.r])
                P.op("sp", lambda e: e.dma_start(out=o_re_p[:, :], in_=fin[0:16, 0, :]), reads=[fin.r], dma=True, out=True)
                P.op("sp", lambda e: e.dma_start(out=o_im_p[:, :], in_=fin[0:16, 1, :]), reads=[fin.r], dma=True, out=True)

                for gq in range(4):
                    bb = (gq % 2) * 4

                    def ymm2(e, gq=gq, bb=bb):
                        ins = None
                        for kk in range(TC):
                            ov = pS[:, bb + kk // 2, (kk % 2) * NJ:(kk % 2) * NJ + NJ]
                            for a4 in range(4):
                                gp = gq * 4 + a4
                                R0, R1 = a4 * 32, a4 * 32 + 32
                                ovq = pS[R0:R1, bb + kk // 2, (kk % 2) * NJ:(kk % 2) * NJ + NJ]
                                e.matmul(ovq, Wt[:, 0, kk, gp, :], Sbf[:, 0, gp, :], start=True, stop=False,
                                         tile_position=(0, R0))
                                e.matmul(ovq, Wt[:, 1, kk, gp, :], Sbf[:, 1, gp, :], start=False, stop=False,
                                         tile_position=(0, R0))
                            for s_ in range(kk + 1):
                                ins = e.matmul(ov, Ktf[:, gq, kk - s_, :], uD[:, gq, s_, :], start=False, stop=(s_ == kk))
                        return ins
                    wr = []
                    for b4 in range(4):
                        wr += bank_r[bb + b4]
                    P.op("pe", ymm2, reads=[Sbf.r, uD_r[gq], fin.r], writes=wr)
                    yo_v = YU.t[:, 4 + gq, 0:SEQ].rearrange("p (j k) -> p k j", k=TC)
                    pv_ = pS[:, bb:bb + 4, :].rearrange("p b (h j) -> p (b h) j", j=NJ)
                    P.op("act", (lambda yo_v, pv_: lambda e: e.copy(yo_v, pv_))(yo_v, pv_),
                         reads=[], writes=wr + [uTg_r[gq * 4 + a4] for a4 in range(4)] + [uTq_r[gq]])
                for gp in range(16):
                    gq, a4 = gp // 4, gp % 4
                    R0, R1 = a4 * 32, a4 * 32 + 32
                    bk = gp % 8

                    def ysm(e, gp=gp, bk=bk, R0=R0, R1=R1):
                        e.matmul(pS[R0:R1, bk, 0:NS], CT[:, 0, gp, :], Hnb[:, 0, gp, :], start=True, stop=False, tile_position=(0, R0))
                        return e.matmul(pS[R0:R1, bk, 0:NS], CT[:, 1, gp, :], Hnb[:, 1, gp, :], start=False, stop=True,
                                        tile_position=(0, R0))
                    P.op("pe", ysm, reads=[uTg_r[gp]], writes=[bank_r[bk][a4]])
                    P.op("dve", (lambda bk, gq=gq, R0=R0, R1=R1: lambda e: e.scalar_tensor_tensor(
                        out=uT[R0:R1, gq, SEQ:NCOL], in0=uT[R0:R1, gq, SEQ:NCOL], scalar=dcol[R0:R1, gq:gq + 1],
                        in1=pS[R0:R1, bk, 0:NS], op0=ALU.mult, op1=ALU.add))(bk),
                        reads=[bank_r[bk][a4], uTg_r[gp]], writes=[uTg_r[gp]])
                hTflat = hT.t[:].rearrange("p a b -> p (a b)")
                po_v = hTflat[:, 0:4096].rearrange("p (k c) -> p k c", c=1024)
                gl_v = hTflat[:, 4096:12288].rearrange("p (k c) -> p k c", c=2048)
                hT_all = [Reg()]
                for kk in range(4):
                    P.op("pool", (lambda kk: lambda e: e.dma_start(out=po_v[:, kk, :], in_=pool_out[kk * 128:(kk + 1) * 128, :]))(kk),
                         writes=hT_all + gate_mm_regs, dma=True)
                for kk in range(4):
                    P.op("pool", (lambda kk: lambda e: e.dma_start(out=gl_v[:, kk, :], in_=glu[kk * 128:(kk + 1) * 128, :]))(kk),
                         writes=hT_all + gate_mm_regs, dma=True)
                P.flush()
            sT.close()

            if dbg == "B":
                dbg_outs["d_YU"] = (YU, [128, 8 * NCOL], BF16)
                _emit_dbg(nc, P, dbg_outs, es)
                P.flush(final=True)
                return nc, list(dbg_outs.keys()) + ["o_pool_p", "o_re_p", "o_im_p", "o_pool_s", "o_re_s", "o_im_s"]
            mT = YU
            mT_r = [[(yp_r[j][ci] if j < 4 else uT_r[j - 4][ci]) for ci in range(len(CHUNKS))] for j in range(8)]
            wo = sb("wo", [128, 8, D], BF16)
            wo_r = [Reg() for _ in range(8)]
            with ExitStack() as sC:
                sg1 = [sb("sg1_%d" % i, [128, 8, 512], BF16, sC) for i in range(2)]
                sg2 = [sb("sg2_%d" % i, [128, 8, 512], BF16, sC) for i in range(2)]
                class _V:
                    def __init__(self, ap):
                        self.ap = ap

                    def __getitem__(self, k):
                        return self.ap[k]
                po = _V(po_v)
                gl = _V(gl_v)
                mtmp = [sb("mtmp%d" % i, [128, 8, 512], BF16, sC) for i in range(2)]
                sz2 = [sb("sz2%d" % i, [128, 512], BF16, sC) for i in range(4)]
                ta = [sb("ta%d" % i, [128, 512], F32, sC) for i in range(4)]
                tb = [sb("tb%d" % i, [128, 512], F32, sC) for i in range(4)]
                pC = [pst("pC%d" % i, [128, 512], F32, sC) for i in range(8)]
                pck = [0]

                def nextc():
                    t = pC[pck[0] % 8]
                    pck[0] += 1
                    return t
                wg_r = [Reg() for _ in range(8)]
                po_r = hT_all
                gl_r = hT_all
                it = 0

                def load_sg1(ci):
                    c0, cn = CHUNKS[ci]
                    t_ = sg1[ci % 2]
                    P.op("sp", (lambda c0, cn, t_: lambda e: e.dma_start(out=t_[:, :, 0:cn], in_=g1s[:, :, c0:c0 + cn]))(c0, cn, t_),
                         reads=[g1s_r[ci]], writes=[t_.r], dma=True)
                    t2_ = sg2[ci % 2]
                    P.op("sp", (lambda c0, cn, t2_: lambda e: e.dma_start(out=t2_[:, :, 0:cn], in_=g2s[:, :, c0:c0 + cn]))(c0, cn, t2_),
                         reads=[g2s_r[ci]], writes=[t2_.r], dma=True)
                load_sg1(0)
                for ci, (c0, cn) in enumerate(CHUNKS):
                    tiles = [ti for ti, (t0, tn) in enumerate(TTILES) if t0 >= c0 and t0 < c0 + cn]
                    hreg = [hT_r[ti] for ti in tiles]
                    mt = mtmp[ci % 2]
                    if ci + 1 < len(CHUNKS):
                        load_sg1(ci + 1)
                    sg_c = sg1[ci % 2]
                    sg2_c = sg2[ci % 2]
                    for j in range(8):
                        b = it % 4
                        it += 1

                        def grp(wt, nk, col0, rhsT, rsel, pt, c0=c0, cn=cn):
                            def f(e):
                                ins = None
                                for kk in range(nk):
                                    ins = e.matmul(pt[:, 0:cn], wt[:, kk, col0:col0 + 128], rhsT[:, rsel + kk, c0:c0 + cn],
                                                   start=(kk == 0), stop=(kk == nk - 1))
                                return ins
                            return f
                        p3, p4, p5 = nextc(), nextc(), nextc()
                        P.op("pe", grp(po, 4, j * 128, YU, 0, p3), reads=[yp_r[kk][ci] for kk in range(4)] + po_r, writes=[p3.r])
                        P.op("pe", grp(gl, 4, j * 128, YU, 4, p4), reads=[uT_r[kk][ci] for kk in range(4)] + gl_r, writes=[p4.r])
                        P.op("pe", grp(gl, 4, 1024 + j * 128, YU, 4, p5), reads=[uT_r[kk][ci] for kk in range(4)] + gl_r, writes=[p5.r])
                        P.op("act", (lambda b, cn, p5: lambda e: e.activation(out=sz2[b][:, 0:cn], in_=p5[:, 0:cn], func=AF.Sigmoid))(b, cn, p5),
                             reads=[p5.r], writes=[sz2[b].r])
                        P.op("dve", (lambda b, cn, p3, j, sg_c: lambda e: e.tensor_mul(ta[b][:, 0:cn], sg_c[:, j, 0:cn], p3[:, 0:cn]))(b, cn, p3, j, sg_c),
                             reads=[sg_c.r, p3.r], writes=[ta[b].r, p3.r])
                        P.op("dve", (lambda b, cn, p4: lambda e: e.tensor_mul(tb[b][:, 0:cn], sz2[b][:, 0:cn], p4[:, 0:cn]))(b, cn, p4),
                             reads=[sz2[b].r, p4.r], writes=[tb[b].r])
                        P.op("pool", (lambda b, cn, j, sg2_c: lambda e: e.tensor_mul(tb[b][:, 0:cn], tb[b][:, 0:cn], sg2_c[:, j, 0:cn]))(b, cn, j, sg2_c),
                             reads=[tb[b].r, sg2_c.r], writes=[tb[b].r])
                        P.op("dve", (lambda b, cn, j, mt: lambda e: e.tensor_add(mt[:, j, 0:cn], ta[b][:, 0:cn], tb[b][:, 0:cn]))(b, cn, j, mt),
                             reads=[ta[b].r, tb[b].r], writes=[mt.r])
                    if ci == 1:
                        for kk in range(8):
                            P.op("pool", (lambda kk: lambda e: e.dma_start(out=wo[:, kk, :], in_=w_out[kk * 128:(kk + 1) * 128, :]))(kk),
                                 writes=[wo_r[kk]], dma=True)
                    P.op("dve", (lambda c0, cn, mt: lambda e: e.tensor_copy(YU[:, :, c0:c0 + cn], mt[:, :, 0:cn]))(c0, cn, mt),
                         reads=[mt.r], writes=[mT_r[j][ci] for j in range(8)])
                P.flush()

            if dbg == "C":
                dbg_outs["d_mT"] = (YU, [128, 8 * NCOL], BF16)
                _emit_dbg(nc, P, dbg_outs, es)
                P.flush(final=True)
                return nc, list(dbg_outs.keys())

            with ExitStack() as sDE:
                x1 = sb("x1", [128, 17, D], F32, sDE)
                wu0 = sb("wu0", [128, 8, 512], BF16, sDE)
                wd0 = sb("wd0", [128, 4, D], BF16, sDE)
                x1_r = [Reg() for _ in TTILES]
                h2_r = [Reg() for _ in TTILES]
                with ExitStack() as sD:
                    g2b = sb("g2b", [128, D], F32, sD)
                    xt = [sb("xtd%d" % i, [128, D], F32, sD) for i in range(2)]
                    sq = [sb("sqd%d" % i, [128, D], BF16, sD) for i in range(2)]
                    hb = [sb("hbd%d" % i, [128, D], BF16, sD) for i in range(2)]
                    ssq = [sb("ssqd%d" % i, [128, 1], F32, sD) for i in range(2)]
                    rstd = [sb("rstdd%d" % i, [128, 1], F32, sD) for i in range(2)]
                    ptr = [pst("ptrd%d" % i, [128, 8, 128], BF16, sD) for i in range(2)]
                    pD = [pst("pD%d" % i, [128, 512], F32, sD) for i in range(4)]
                    P.op("sp", lambda e: e.dma_start(out=g2b[:, :], in_=n2g.partition_broadcast(128)), writes=[g2b.r], dma=True)
                    pend_d = []
                    P.op("pool", lambda e: e.dma_start(
                        out=wu0[:, :, :], in_=dram_ap(w_up, 0, [[DFF, 128], [128 * DFF, 8], [1, 512]])), writes=[wu0.r], dma=True)
                    P.op("pool", lambda e: e.dma_start(
                        out=wd0[:, :, :], in_=dram_ap(w_down, 0, [[D, 128], [128 * D, 4], [1, D]])), writes=[wd0.r], dma=True)
                    for ti, (t0, tn) in enumerate(TTILES):
                        b = ti % 2
                        ci = min(ti // 4, 4)
                        src = x[t0:t0 + tn, :] if ti < 16 else xs[:, :]
                        P.op("sp", (lambda b, tn, src: lambda e: e.dma_start(out=xt[b][0:tn, :], in_=src))(b, tn, src),
                             writes=[xt[b].r], dma=True)
                        while len(pend_d) > 1:
                            pend_d.pop(0)()
                        for hf in range(2):
                            pt = pD[(2 * ti + hf) % 4]

                            def mo(e, t0=t0, tn=tn, hf=hf, pt=pt):
                                ins = None
                                for kk in range(8):
                                    ins = e.matmul(pt[0:tn, :], YU[:, kk, t0:t0 + tn], wo[:, kk, hf * 512:(hf + 1) * 512],
                                                   start=(kk == 0), stop=(kk == 7))
                                return ins
                            P.op("pe", mo, reads=[mT_r[j][ci] for j in range(8)] + wo_r, writes=[pt.r])
                            P.op("dve", (lambda b, tn, hf, pt, ti: lambda e: e.tensor_add(
                                x1[0:tn, ti, hf * 512:(hf + 1) * 512], pt[0:tn, :], xt[b][0:tn, hf * 512:(hf + 1) * 512]))(b, tn, hf, pt, ti),
                                reads=[pt.r, xt[b].r], writes=[x1_r[ti], pt.r])
                        while pend_d:
                            pend_d.pop(0)()
                        P.op("act", (lambda b, tn, ti: lambda e: e.activation(
                            out=sq[b][0:tn, :], in_=x1[0:tn, ti, :], func=AF.Square, accum_out=ssq[b][0:tn, :]))(b, tn, ti),
                            reads=[x1_r[ti]], writes=[sq[b].r, ssq[b].r])
                        P.op("dve", (lambda b, tn: lambda e: e.tensor_scalar(
                            rstd[b][0:tn, :], ssq[b][0:tn, :], 1.0 / D, EPS, ALU.mult, ALU.add))(b, tn),
                            reads=[ssq[b].r], writes=[rstd[b].r])
                        P.op("act", (lambda b, tn: lambda e: e.activation(
                            out=rstd[b][0:tn, :], in_=rstd[b][0:tn, :], func=AF.Sqrt))(b, tn),
                            reads=[rstd[b].r], writes=[rstd[b].r])
                        P.op("dve", (lambda b, tn: lambda e: e.reciprocal(rstd[b][0:tn, :], rstd[b][0:tn, :]))(b, tn),
                             reads=[rstd[b].r], writes=[rstd[b].r])
                        P.op("dve", (lambda b, tn, ti: lambda e: e.scalar_tensor_tensor(
                            out=hb[b][0:tn, :], in0=x1[0:tn, ti, :], scalar=rstd[b][0:tn, 0:1], in1=g2b[0:tn, :],
                            op0=ALU.mult, op1=ALU.mult))(b, tn, ti),
                            reads=[x1_r[ti], rstd[b].r, g2b.r], writes=[hb[b].r])

                        def tr2(e, b=b, tn=tn):
                            ins = None
                            for kk in range(8):
                                ins = e.transpose(ptr[b][:, kk, 0:tn], hb[b][0:tn, kk * 128:(kk + 1) * 128], ident_b[0:tn, 0:tn])
                            return ins
                        def late(tr2=tr2, b=b, t0=t0, tn=tn, ti=ti):
                            P.op("pe", tr2, reads=[hb[b].r], writes=[ptr[b].r])
                            P.op("act", (lambda b, t0, tn: lambda e: e.copy(hT[:, :, t0:t0 + tn], ptr[b][:, :, 0:tn]))(b, t0, tn),
                                 reads=[ptr[b].r], writes=[h2_r[ti], ptr[b].r])
                        pend_d.append(late)
                    for f_ in pend_d:
                        f_()
                    P.flush()

                if dbg == "D":
                    dbg_outs["d_x1"] = (x1, [128, 17 * D], F32)
                    dbg_outs["d_h2T"] = (hT, [128, 8 * NCOL], BF16)
                    _emit_dbg(nc, P, dbg_outs, es)
                    P.flush(final=True)
                    return nc, list(dbg_outs.keys())

                with ExitStack() as sE:
                    NE = 8
                    wu = [wu0, sb("wu1", [128, 8, 512], BF16, sE)]
                    wd = [wd0, sb("wd1", [128, 4, D], BF16, sE)]
                    gfb = sb("gfb", [128, D], F32, sE)
                    rt = [sb("rt%d" % i, [128, 512], BF16, sE) for i in range(2)]
                    aT = [sb("aT%d" % i, [128, 4, 512], BF16, sE) for i in range(2)]
                    aT_r = [[Reg() for _ in range(4)] for _ in range(2)]
                    yo = [sb("yo%d" % i, [128, D], F32, sE) for i in range(1)] * 2
                    sq = [sb("sqe%d" % i, [128, D], BF16, sE) for i in range(1)] * 2
                    ssq = [sb("ssqe%d" % i, [128, 1], F32, sE) for i in range(2)]
                    rstd = [sb("rstde%d" % i, [128, 1], F32, sE) for i in range(2)]
                    pE = [pst("pE%d" % i, [128, 512], F32, sE) for i in range(8)]
                    pek = [0]

                    def nexte():
                        t = pE[pek[0] % 8]
                        pek[0] += 1
                        return t
                    P.op("sp", lambda e: e.dma_start(out=gfb[:, :], in_=nfg.partition_broadcast(128)), writes=[gfb.r], dma=True)
                    def final_norm(ti):
                        t0, tn = TTILES[ti]
                        b = ti % 2
                        P.op("act", (lambda b, tn, ti: lambda e: e.activation(
                            out=sq[b][0:tn, :], in_=x1[0:tn, ti, :], func=AF.Square, accum_out=ssq[b][0:tn, :]))(b, tn, ti),
                            reads=[x1_r[ti]], writes=[sq[b].r, ssq[b].r])
                        P.op("dve", (lambda b, tn: lambda e: e.tensor_scalar(
                            rstd[b][0:tn, :], ssq[b][0:tn, :], 1.0 / D, EPS, ALU.mult, ALU.add))(b, tn),
                            reads=[ssq[b].r], writes=[rstd[b].r])
                        P.op("act", (lambda b, tn: lambda e: e.activation(
                            out=rstd[b][0:tn, :], in_=rstd[b][0:tn, :], func=AF.Sqrt))(b, tn),
                            reads=[rstd[b].r], writes=[rstd[b].r])
                        P.op("dve", (lambda b, tn: lambda e: e.reciprocal(rstd[b][0:tn, :], rstd[b][0:tn, :]))(b, tn),
                             reads=[rstd[b].r], writes=[rstd[b].r])
                        P.op("dve", (lambda b, tn, ti: lambda e: e.scalar_tensor_tensor(
                            out=yo[b][0:tn, :], in0=x1[0:tn, ti, :], scalar=rstd[b][0:tn, 0:1], in1=gfb[0:tn, :],
                            op0=ALU.mult, op1=ALU.mult))(b, tn, ti),
                            reads=[x1_r[ti], rstd[b].r, gfb.r], writes=[yo[b].r])
                        dst = y[t0:t0 + tn, :] if ti < 16 else ysamp[:, :]
                        P.op("sp", (lambda b, tn, dst: lambda e: e.dma_start(out=dst, in_=yo[b][0:tn, :]))(b, tn, dst),
                             reads=[yo[b].r], dma=True, out=True)
                    rk = 0
                    pend_fn = []
                    for e8 in range(NE):
                        b = e8 % 2
                        if e8 == 0:
                            pass
                        if e8 + 1 < NE:
                            nb_ = (e8 + 1) % 2
                            P.op("pool", (lambda nb_, e8: lambda e: e.dma_start(
                                out=wu[nb_][:, :, :], in_=dram_ap(w_up, (e8 + 1) * 512, [[DFF, 128], [128 * DFF, 8], [1, 512]])))(nb_, e8),
                                writes=[wu[nb_].r], dma=True)
                            P.op("pool", (lambda nb_, e8: lambda e: e.dma_start(
                                out=wd[nb_][:, :, :], in_=dram_ap(w_down, (e8 + 1) * 512 * D, [[D, 128], [128 * D, 4], [1, D]])))(nb_, e8),
                                writes=[wd[nb_].r], dma=True)
                        for ci, (c0, cn) in enumerate(CHUNKS):
                            tiles = [ti for ti, (t0, tn) in enumerate(TTILES) if t0 >= c0 and t0 < c0 + cn]
                            cb = ci % 2
                            for m in range(4):
                                pt = nexte()

                                def up(e, m=m, c0=c0, cn=cn, pt=pt, b=b):
                                    ins = None
                                    for kk in range(8):
                                        ins = e.matmul(pt[:, 0:cn], wu[b][:, kk, m * 128:(m + 1) * 128], hT[:, kk, c0:c0 + cn],
                                                       start=(kk == 0), stop=(kk == 7))
                                    return ins
                                P.op("pe", up, reads=[h2_r[ti] for ti in tiles] + [wu[b].r], writes=[pt.r])
                                rb = rk % 2
                                rk += 1
                                P.op("act", (lambda rb, cn, pt: lambda e: e.activation(out=rt[rb][:, 0:cn], in_=pt[:, 0:cn], func=AF.Relu))(rb, cn, pt),
                                     reads=[pt.r], writes=[rt[rb].r])
                                P.op("pool", (lambda rb, cn, cb, m: lambda e: e.tensor_mul(
                                    aT[cb][:, m, 0:cn], rt[rb][:, 0:cn], rt[rb][:, 0:cn]))(rb, cn, cb, m),
                                    reads=[rt[rb].r], writes=[aT_r[cb][m]])
                            for ti in tiles:
                                t0, tn = TTILES[ti]
                                for hf in range(2):
                                    pt = nexte()

                                    def dn(e, t0=t0, tn=tn, c0=c0, hf=hf, pt=pt, b=b, cb=cb):
                                        ins = None
                                        for kk in range(4):
                                            ins = e.matmul(pt[0:tn, :], aT[cb][:, kk, t0 - c0:t0 - c0 + tn], wd[b][:, kk, hf * 512:(hf + 1) * 512],
                                                           start=(kk == 0), stop=(kk == 3))
                                        return ins
                                    P.op("pe", dn, reads=aT_r[cb] + [wd[b].r], writes=[pt.r])
                                    P.op("dve", (lambda tn, ti, hf, pt: lambda e: e.tensor_add(
                                        x1[0:tn, ti, hf * 512:(hf + 1) * 512], x1[0:tn, ti, hf * 512:(hf + 1) * 512], pt[0:tn, :]))(tn, ti, hf, pt),
                                        reads=[pt.r, x1_r[ti]], writes=[x1_r[ti], pt.r])
                                if e8 == NE - 1 and hf == 1:
                                    pend_fn.append(ti)
                                    while len(pend_fn) > 2:
                                        final_norm(pend_fn.pop(0))
                    while pend_fn:
                        final_norm(pend_fn.pop(0))
                    P.flush(final=True)
    return nc, []


def _emit_dbg(nc, P, dbg_outs, es):
    for nm, (t, shp, dt) in dbg_outs.items():
        d = nc.dram_tensor(nm, list(shp), dt, kind="ExternalOutput").ap()
        nd = len(t.t.shape)
        flat = {2: lambda: t.t[:], 3: lambda: t.t[:].rearrange("p a b -> p (a b)"),
                4: lambda: t.t[:].rearrange("p a b c -> p (a b c)"),
                5: lambda: t.t[:].rearrange("p a b c d -> p (a b c d)")}[nd]()
        P.op("sp", (lambda d, flat: lambda e: e.dma_start(out=d[:, :], in_=flat))(d, flat),
             reads=[t.r], writes=[], dma=True, out=True)


def make_in_maps(inp):
    f = lambda a: np.ascontiguousarray(np.asarray(a, dtype=np.float32))
    shared = {
        "n1g": f(inp["norm1_g"]).reshape(1, D),
        "w_in": f(inp["w_in"]).reshape(D, 3072),
        "pool_w": f(inp["pool_w"]).reshape(512, 128),
        "pscale": f(inp["pool_scale"]).reshape(1, PW),
        "pool_out": f(inp["pool_out"]).reshape(PW, D),
        "a_re": f(inp["ssm_a_re"]).reshape(1, 2048),
        "a_im": f(inp["ssm_a_im"]).reshape(1, 2048),
        "log_dt": f(inp["ssm_log_dt"]).reshape(1, 32),
        "b_re": f(inp["ssm_b_re"]).reshape(1, -1),
        "b_im": f(inp["ssm_b_im"]).reshape(1, -1),
        "c_re": f(inp["ssm_c_re"]).reshape(512, 64),
        "c_im": f(inp["ssm_c_im"]).reshape(512, 64),
        "ssm_d": f(inp["ssm_d"]).reshape(1, 512),
        "glu": f(inp["ssm_glu"]).reshape(512, 2048),
        "w_out": f(inp["w_out"]).reshape(D, D),
        "n2g": f(inp["norm2_g"]).reshape(1, D),
        "w_up": f(inp["w_up"]).reshape(D, DFF),
        "w_down": f(inp["w_down"]).reshape(DFF, D),
        "nfg": f(inp["normf_g"]).reshape(1, D),
    }
    xp = f(inp["x_prompt"])
    xsm = f(inp["x_sample"]).reshape(128, D)
    cp = f(inp["cache_pool"]).reshape(128, PB, PW)
    sr = f(inp["state_ssm_re"]).reshape(128, 2048)
    si = f(inp["state_ssm_im"]).reshape(128, 2048)
    maps = []
    for c in range(NCORES):
        m = dict(shared)
        m["x"] = xp[c]
        m["xs"] = xsm[c * NS:(c + 1) * NS]
        m["cpool"] = cp[c * NS:(c + 1) * NS].reshape(NS * PB, PW)
        m["sre"] = sr[c * NS:(c + 1) * NS]
        m["sim"] = si[c * NS:(c + 1) * NS]
        maps.append(m)
    return maps


def kernel(**inp):
    nc, _ = build()
    maps = make_in_maps(inp)
    res = run_bass_kernel_spmd(nc, maps, core_ids=list(range(NCORES)))
    R = res.results
    y_prompt = np.stack([R[c]["y"] for c in range(NCORES)], 0).astype(np.float32)
    y_sample = np.concatenate([R[c]["ysamp"] for c in range(NCORES)], 0).reshape(128, 1, D).astype(np.float32)
    pool_p = np.stack([R[c]["o_pool_p"] for c in range(NCORES)], 0).reshape(1, 8, PB, PW).astype(np.float32)
    re_p = np.stack([R[c]["o_re_p"].reshape(32, 64) for c in range(NCORES)], 0).reshape(1, 8, 32, 64).astype(np.float32)
    im_p = np.stack([R[c]["o_im_p"].reshape(32, 64) for c in range(NCORES)], 0).reshape(1, 8, 32, 64).astype(np.float32)
    pool_s = np.concatenate([R[c]["o_pool_s"].reshape(NS, PB, PW) for c in range(NCORES)], 0).reshape(1, 128, PB, PW).astype(np.float32)
    re_s = np.concatenate([R[c]["o_re_s"].reshape(NS, 32, 64) for c in range(NCORES)], 0).reshape(1, 128, 32, 64).astype(np.float32)
    im_s = np.concatenate([R[c]["o_im_s"].reshape(NS, 32, 64) for c in range(NCORES)], 0).reshape(1, 128, 32, 64).astype(np.float32)
    return (y_prompt, y_sample, pool_p, re_p, im_p, pool_s, re_s, im_s)
```

```python
import math
SKIP = ''
ENG_M = 'pool'
from contextlib import ExitStack

import numpy as np
import concourse.bass as bass
import concourse.mybir as mybir
from concourse.bass_utils import run_bass_kernel_spmd

F32 = mybir.dt.float32
BF16 = mybir.dt.bfloat16
ALU = mybir.AluOpType
AF = mybir.ActivationFunctionType
AX = mybir.AxisListType

NCORES = 8
D = 1024
SEQ = 2048
NS = 16
NCOL = SEQ + NS
PB = 15
PW = 512
DFF = 4096
EPS = 1e-6
PAD = 16
CHUNKS = [(0, 512), (512, 512), (1024, 512), (1536, 512), (2048, NS)]
TTILES = [(i * 128, 128) for i in range(16)] + [(SEQ, NS)]
MLIST = [1, 2, 3, 4, 5, 6, 7, 8, 16, 24, 32, 64, 96, 128, 256, 384, 512, 1024, 1536]
NM = len(MLIST)
MI = {m: i for i, m in enumerate(MLIST)}
TC = 8
NJ = SEQ // TC
TWO_PI = 2.0 * math.pi


class Reg:
    __slots__ = ("w", "rs")

    def __init__(self):
        self.w = None
        self.rs = []


class Op:
    __slots__ = ("eng", "fn", "deps", "dma", "sig", "idx", "sem", "val", "phase")


class Prog:
    ND = 8

    def __init__(self, nc, es):
        self.nc = nc
        self.es = es
        self.ops = []
        self.phase = 0
        self.esem = {}
        self.ecount = {}
        for e in ("pe", "act", "dve", "pool", "sp"):
            self.esem[e] = es.enter_context(nc.semaphore("s_" + e))
            self.ecount[e] = 0
        self.dsem = {}
        self.dcount = {}
        for q in ("sp", "pool", "act"):
            self.dsem[q] = [es.enter_context(nc.semaphore("d_%s%d" % (q, i))) for i in range(self.ND)]
            self.dcount[q] = 0
        self.waited = {e: {} for e in ("pe", "act", "dve", "pool", "sp")}
        self.out_dmas = []

    def op(self, eng, fn, reads=(), writes=(), dma=False, out=False):
        o = Op()
        o.eng = eng
        o.fn = fn
        o.dma = dma
        o.sig = False
        o.idx = 0
        o.sem = None
        o.val = 0
        o.phase = self.phase
        deps = []
        for r in reads:
            if r.w is not None:
                deps.append(r.w)
        for w in writes:
            if w.w is not None:
                deps.append(w.w)
            deps.extend(w.rs)
        for r in reads:
            r.rs.append(o)
        for w in writes:
            w.w = o
            w.rs = []
        seen = set()
        o.deps = []
        for d in deps:
            if d is o or id(d) in seen:
                continue
            seen.add(id(d))
            o.deps.append(d)
        self.ops.append(o)
        if out:
            self.out_dmas.append(o)
        return o

    def interleave(self, n_head, n0):
        ops = self.ops
        head, a, b = ops[:n_head], ops[n_head:n0], ops[n0:]
        out = list(head)
        ia = 0
        for ib, o in enumerate(b):
            tgt = (ib + 1) * len(a) // max(len(b), 1)
            while ia < tgt:
                out.append(a[ia])
                ia += 1
            out.append(o)
        out.extend(a[ia:])
        self.ops = out

    def flush(self, final=False):
        nc = self.nc
        ops = self.ops
        self.ops = []
        if True:
            fence = Op()
            fence.eng = "sp"
            fence.fn = None
            fence.dma = False
            fence.sig = False
            fence.idx = 0
            fence.sem = None
            fence.val = 0
            fence.phase = self.phase
            fence.deps = [o for o in ops if o.dma] + (list(self.out_dmas) if final else [])
            ops.append(fence)
        for o in ops:
            o.deps = [d for d in o.deps if d.dma or d.phase == self.phase]
        for o in ops:
            for d in o.deps:
                if d.dma:
                    continue
                if d.eng == "pe" and o.eng == "pe":
                    continue
                d.sig = True
        for o in ops:
            if o.fn is None:
                continue
            if o.dma:
                q = o.eng
                j = self.dcount[q]
                self.dcount[q] = j + 1
                o.sem = self.dsem[q][j % self.ND]
                o.val = 16 * (j // self.ND + 1)
            elif o.sig:
                self.ecount[o.eng] += 1
                o.idx = self.ecount[o.eng]
        per = {e: [o for o in ops if o.eng == e] for e in ("pe", "act", "dve", "pool", "sp")}

        def emit(engname, eng):
            wd = self.waited[engname]
            for o in per[engname]:
                need = {}
                for d in o.deps:
                    if d.dma:
                        s, v = d.sem, d.val
                    else:
                        if d.eng == "pe" and o.eng == "pe":
                            continue
                        s, v = self.esem[d.eng], d.idx
                    key = id(s)
                    if key not in need or need[key][1] < v:
                        need[key] = (s, v)
                if o.dma and o.val > 16:
                    key = id(o.sem)
                    v = o.val - 16
                    if key not in need or need[key][1] < v:
                        need[key] = (o.sem, v)
                for key, (s, v) in need.items():
                    if wd.get(key, 0) >= v:
                        continue
                    eng.wait_ge(s, v)
                    wd[key] = v
                if o.fn is None:
                    continue
                ins = o.fn(eng)
                if o.dma:
                    ins.then_inc(o.sem, 16)
                elif o.sig:
                    ins.then_inc(self.esem[engname], 1)

        with nc.Block() as block:
            if per["pe"]:
                @block.tensor
                def _(eng):
                    emit("pe", eng)
            if per["act"]:
                @block.scalar
                def _(eng):
                    emit("act", eng)
            if per["dve"]:
                @block.vector
                def _(eng):
                    emit("dve", eng)
            if per["pool"]:
                @block.gpsimd
                def _(eng):
                    emit("pool", eng)
            if per["sp"]:
                @block.sync
                def _(eng):
                    emit("sp", eng)
        self.phase += 1


class Sub:
    def __init__(self, tl, lo):
        self.tl = tl
        self.lo = lo

    def __getitem__(self, k):
        p, q, c = k
        return self.tl.t[p, self.lo + q, c]


class Tl:
    def __init__(self, t):
        self.t = t
        self.r = Reg()

    def __getitem__(self, k):
        return self.t[k]


def build(dbg=None):
    nc = bass.Bass("TRN2", target_bir_lowering=False)

    def din(name, shape):
        return nc.dram_tensor(name, list(shape), F32, kind="ExternalInput").ap()

    def dout(name, shape):
        return nc.dram_tensor(name, list(shape), F32, kind="ExternalOutput").ap()

    x = din("x", [SEQ, D])
    xs = din("xs", [NS, D])
    cpool = din("cpool", [NS * PB, PW])
    sre = din("sre", [NS, 2048])
    sim = din("sim", [NS, 2048])
    n1g = din("n1g", [1, D])
    w_in = din("w_in", [D, 3072])
    pool_w = din("pool_w", [512, 128])
    pscale = din("pscale", [1, PW])
    pool_out = din("pool_out", [PW, D])
    a_re = din("a_re", [1, 2048])
    a_im = din("a_im", [1, 2048])
    log_dt = din("log_dt", [1, 32])
    b_re = din("b_re", [1, 32 * 64 * 16])
    b_im = din("b_im", [1, 32 * 64 * 16])
    c_re = din("c_re", [512, 64])
    c_im = din("c_im", [512, 64])
    ssm_d = din("ssm_d", [1, 512])
    glu = din("glu", [512, 2048])
    w_out = din("w_out", [D, D])
    n2g = din("n2g", [1, D])
    w_up = din("w_up", [D, DFF])
    w_down = din("w_down", [DFF, D])
    nfg = din("nfg", [1, D])

    y = dout("y", [SEQ, D])
    ysamp = dout("ysamp", [NS, D])
    o_pool_p = dout("o_pool_p", [PB, PW])
    o_re_p = dout("o_re_p", [16, 128])
    o_im_p = dout("o_im_p", [16, 128])
    o_pool_s = dout("o_pool_s", [NS * PB, PW])
    o_re_s = dout("o_re_s", [NS, 2048])
    o_im_s = dout("o_im_s", [NS, 2048])

    g1s = nc.dram_tensor("g1s", [128, 8, NCOL], BF16, kind="Internal").ap()
    g1s_r = [Reg() for _ in CHUNKS]
    g2s = nc.dram_tensor("g2s", [128, 8, NCOL], BF16, kind="Internal").ap()
    g2s_r = [Reg() for _ in CHUNKS]
    dbg_outs = {}

    with ExitStack() as es:
        P = Prog(nc, es)

        def sb(name, shape, dt, stack=None):
            return Tl((stack or es).enter_context(nc.sbuf_tensor(name, list(shape), dt)))

        def pst(name, shape, dt, stack=None):
            return Tl((stack or es).enter_context(nc.psum_tensor(name, list(shape), dt)))

        def dram_ap(base, offset, pattern):
            return bass.AP(base.tensor, offset, pattern)

        ident_f = sb("ident_f", [128, 128], F32)
        ident_b = sb("ident_b", [128, 128], BF16)
        dcol = sb("dcol", [128, 4], F32)
        pscol = sb("pscol", [128, 4], F32)
        rct = sb("rct", [128, 16], F32)
        hT = sb("hT", [128, 8, NCOL], BF16)
        YU = sb("YU", [128, 8, NCOL], BF16)
        sT = es.enter_context(ExitStack())
        lam = sb("lam", [128, 16, NM, 3], F32, sT)
        BbTs = sb("BbTs", [128, 2, 4, TC, 128], BF16, sT)
        CT = sb("CT", [128, 2, 16, 32], BF16, sT)
        Wt = sb("Wt", [128, 2, TC, 16, 32], BF16, sT)
        Kt = sb("Kt", [128, 4, TC, 32], BF16, sT)
        Hnb = sb("Hnb", [128, 2, 16, 16], BF16, sT)
        Ktf = sb("Ktf", [128, 4, TC, 128], BF16, sT)

        sW = es.enter_context(ExitStack())
        wvu = sb("wvu", [128, 8, 1024], BF16, sW)
        s0 = es.enter_context(ExitStack())
        if True:
            ar = sb("ar", [128, 16], F32, s0)
            ai = sb("ai", [128, 16], F32, s0)
            ldt = sb("ldt", [128, 16], F32, s0)
            dtt = sb("dtt", [128, 16], F32, s0)
            adt = sb("adt", [128, 16], F32, s0)
            wdt = sb("wdt", [128, 16], F32, s0)
            MT = sb("MT", [128, 16, NM], F32, s0)
            ARG = sb("ARG", [128, 16, NM], F32, s0)
            EXA = sb("EXA", [128, 16, NM], F32, s0)
            MAG = sb("MAG", [128, 16, NM], F32, s0)
            SA = sb("SA", [128, 16, NM], F32, s0)
            CA = sb("CA", [128, 16, NM], F32, s0)
            SN = sb("SN", [128, 16, NM], F32, s0)
            CS = sb("CS", [128, 16, NM], F32, s0)
            negpi = sb("negpi", [128, 1], F32, s0)
            t1 = sb("t1", [128, 16], F32, s0)
            t2 = sb("t2", [128, 16], F32, s0)
            t3 = sb("t3", [128, 16], F32, s0)
            t4 = sb("t4", [128, 16], F32, s0)
            cre = sb("cre", [128, 16], F32, s0)
            cim = sb("cim", [128, 16], F32, s0)
            Bn_re = sb("Bn_re", [128, 16, 16], F32, s0)
            Bn_im = sb("Bn_im", [128, 16, 16], F32, s0)
            Bb_re = sb("Bb_re", [128, 16, 16], F32, s0)
            Bb_im = sb("Bb_im", [128, 16, 16], F32, s0)
            Bt1 = sb("Bt1", [128, 16, 16], F32, s0)
            Bt2 = sb("Bt2", [128, 16, 16], F32, s0)
            Bpad = sb("Bpad", [128, 2, 16, 32], F32, s0)
            Cin = [sb("Cin%d" % i, [128, 4, 128], F32, s0) for i in range(2)]
            iot = sb("iot", [128, 16], F32, s0)
            ps0 = [pst("ps0_%d" % i, [128, 512], F32, s0) for i in range(4)]

            P.op("pool", lambda e: e.memset(ident_f[:, :], 0.0), writes=[ident_f.r])
            P.op("pool", lambda e: e.affine_select(
                out=ident_f[:, :], in_=ident_f[:, :], pattern=[[-1, 128]], compare_op=ALU.not_equal,
                fill=1.0, base=0, channel_multiplier=1), reads=[ident_f.r], writes=[ident_f.r])
            P.op("dve", lambda e: e.tensor_copy(ident_b[:, :], ident_f[:, :]), reads=[ident_f.r], writes=[ident_b.r])
            P.op("pool", lambda e: e.iota(iot[:, :], [[1, 16]], base=1, channel_multiplier=0,
                                          allow_small_or_imprecise_dtypes=True), writes=[iot.r])
            P.op("dve", lambda e: e.reciprocal(rct[:, :], iot[:, :]), reads=[iot.r], writes=[rct.r])
            P.op("dve", lambda e: e.memset(negpi[:, :], -math.pi), writes=[negpi.r])

            def bcast_row(dst, src_row, n):
                return lambda e: e.dma_start(out=dst[:, :], in_=src_row.partition_broadcast(128))

            def small_t(dst, src):
                def f(e):
                    with nc.allow_non_contiguous_dma(reason="tiny param load"):
                        return e.dma_start(out=dst, in_=src)
                return f
            prow = sb("prow", [16, 4, 128], F32, s0)
            Lb = sb("Lb", [128, 32], F32, s0)
            P.op("pool", lambda e: e.memset(prow[:, :, :], 0.0), writes=[prow.r])
            P.op("sp", lambda e: e.dma_start(out=prow[0:16, 0, :], in_=dram_ap(a_re, 0, [[128, 16], [1, 128]])), writes=[prow.r], dma=True)
            P.op("sp", lambda e: e.dma_start(out=prow[0:16, 1, :], in_=dram_ap(a_im, 0, [[128, 16], [1, 128]])), writes=[prow.r], dma=True)
            P.op("sp", lambda e: e.dma_start(out=prow[0:4, 2, :], in_=dram_ap(ssm_d, 0, [[128, 4], [1, 128]])), writes=[prow.r], dma=True)
            P.op("sp", lambda e: e.dma_start(out=prow[0:4, 3, :], in_=dram_ap(pscale, 0, [[128, 4], [1, 128]])), writes=[prow.r], dma=True)
            P.op("sp", lambda e: e.dma_start(out=Lb[:, :], in_=log_dt.partition_broadcast(128)), writes=[Lb.r], dma=True)
            ptp = ps0[3]

            def ptr_params(e):
                ins = None
                for w_ in range(4):
                    ins = e.transpose(ptp[:, w_ * 16:(w_ + 1) * 16], prow[0:16, w_, :], ident_f[0:16, 0:16])
                return ins
            P.op("pe", ptr_params, reads=[prow.r, ident_f.r], writes=[ptp.r])
            P.op("act", lambda e: e.copy(ar[:, :], ptp[:, 0:16]), reads=[ptp.r], writes=[ar.r, ptp.r])
            P.op("act", lambda e: e.copy(ai[:, :], ptp[:, 16:32]), reads=[ptp.r], writes=[ai.r, ptp.r])
            P.op("act", lambda e: e.copy(dcol[:, :], ptp[:, 32:36]), reads=[ptp.r], writes=[dcol.r, ptp.r])
            P.op("act", lambda e: e.copy(pscol[:, :], ptp[:, 48:52]), reads=[ptp.r], writes=[pscol.r, ptp.r])
            r_ldt = [Reg(), Reg()]
            for h in range(2):
                P.op("dve", (lambda h: lambda e: e.tensor_copy(ldt[h * 64:(h + 1) * 64, :], Lb[h * 64:(h + 1) * 64, h:32:2]))(h),
                     reads=[Lb.r], writes=[r_ldt[h]])
            P.op("sp", lambda e: e.dma_start(out=Bn_re[:, :, :], in_=dram_ap(b_re, 0, [[16, 128], [2048, 16], [1, 16]])),
                 writes=[Bn_re.r], dma=True)
            P.op("sp", lambda e: e.dma_start(out=Bn_im[:, :, :], in_=dram_ap(b_im, 0, [[16, 128], [2048, 16], [1, 16]])),
                 writes=[Bn_im.r], dma=True)

            P.op("act", lambda e: e.activation(out=dtt[:, :], in_=ldt[:, :], func=AF.Exp),
                 reads=r_ldt, writes=[dtt.r])
            P.op("dve", lambda e: e.tensor_mul(adt[:, :], ar[:, :], dtt[:, :]), reads=[ar.r, dtt.r], writes=[adt.r])
            P.op("dve", lambda e: e.tensor_mul(wdt[:, :], ai[:, :], dtt[:, :]), reads=[ai.r, dtt.r], writes=[wdt.r])
            for mi, m in enumerate(MLIST):
                P.op("pool", (lambda mi, m: lambda e: e.memset(MT[:, :, mi], float(m)))(mi, m), writes=[MT.r])
            P.op("dve", lambda e: e.tensor_mul(ARG[:, :, :], MT[:, :, :], wdt[:, :].unsqueeze(2).to_broadcast([128, 16, NM])),
                 reads=[MT.r, wdt.r], writes=[ARG.r])
            P.op("dve", lambda e: e.tensor_mul(EXA[:, :, :], MT[:, :, :], adt[:, :].unsqueeze(2).to_broadcast([128, 16, NM])),
                 reads=[MT.r, adt.r], writes=[EXA.r])
            P.op("act", lambda e: e.activation(out=MAG[:, :, :], in_=EXA[:, :, :], func=AF.Exp),
                 reads=[EXA.r], writes=[MAG.r])
            KI = sb("KI", [128, 16, NM], mybir.dt.int32, s0)
            KF = sb("KF", [128, 16, NM], F32, s0)
            MK = sb("MK", [128, 16, NM], F32, s0)

            def sin_of(dst, A, shift):
                P.op("dve", lambda e: e.tensor_scalar(A[:, :, :], ARG[:, :, :], float(shift), None, ALU.add),
                     reads=[ARG.r], writes=[A.r])
                P.op("dve", lambda e: e.tensor_scalar(KI[:, :, :], A[:, :, :], 1.0 / TWO_PI, None, ALU.mult),
                     reads=[A.r], writes=[KI.r])
                P.op("dve", lambda e: e.tensor_copy(KF[:, :, :], KI[:, :, :]), reads=[KI.r], writes=[KF.r])
                P.op("dve", lambda e: e.scalar_tensor_tensor(out=A[:, :, :], in0=KF[:, :, :], scalar=-TWO_PI, in1=A[:, :, :],
                                                             op0=ALU.mult, op1=ALU.add),
                     reads=[KF.r, A.r], writes=[A.r])
                P.op("dve", lambda e: e.tensor_scalar(MK[:, :, :], A[:, :, :], math.pi, TWO_PI, ALU.is_gt, ALU.mult),
                     reads=[A.r], writes=[MK.r])
                P.op("dve", lambda e: e.tensor_sub(A[:, :, :], A[:, :, :], MK[:, :, :]), reads=[A.r, MK.r], writes=[A.r])
                P.op("dve", lambda e: e.tensor_scalar(MK[:, :, :], A[:, :, :], -math.pi, TWO_PI, ALU.is_lt, ALU.mult),
                     reads=[A.r], writes=[MK.r])
                P.op("dve", lambda e: e.tensor_add(A[:, :, :], A[:, :, :], MK[:, :, :]), reads=[A.r, MK.r], writes=[A.r])
                P.op("dve", lambda e: e.tensor_scalar(A[:, :, :], A[:, :, :], -math.pi, math.pi, ALU.max, ALU.min),
                     reads=[A.r], writes=[A.r])
                P.op("act", lambda e: e.activation(out=dst[:, :, :], in_=A[:, :, :], func=AF.Sin),
                     reads=[A.r], writes=[dst.r])
            sin_of(SN, SA, 0.0)
            sin_of(CS, CA, 0.5 * math.pi)
            r_lre, r_lim, r_lni = Reg(), Reg(), Reg()
            P.op("dve", lambda e: e.tensor_mul(lam[:, :, :, 0], MAG[:, :, :], CS[:, :, :]), reads=[MAG.r, CS.r], writes=[r_lre])
            P.op("dve", lambda e: e.tensor_mul(lam[:, :, :, 1], MAG[:, :, :], SN[:, :, :]), reads=[MAG.r, SN.r], writes=[r_lim])
            P.op("dve", lambda e: e.tensor_scalar(lam[:, :, :, 2], lam[:, :, :, 1], -1.0, None, ALU.mult),
                 reads=[r_lim], writes=[r_lni])
            lam_regs = [r_lre, r_lim, r_lni]

            lr1 = lam[:, :, 0, 0]
            li1 = lam[:, :, 0, 1]
            P.op("dve", lambda e: e.tensor_scalar(t1[:, :], lr1, -1.0, None, ALU.add), reads=[r_lre], writes=[t1.r])
            P.op("dve", lambda e: e.tensor_mul(t2[:, :], ar[:, :], ar[:, :]), reads=[ar.r], writes=[t2.r])
            P.op("dve", lambda e: e.tensor_mul(t3[:, :], ai[:, :], ai[:, :]), reads=[ai.r], writes=[t3.r])
            P.op("dve", lambda e: e.tensor_add(t2[:, :], t2[:, :], t3[:, :]), reads=[t2.r, t3.r], writes=[t2.r])
            P.op("dve", lambda e: e.reciprocal(t2[:, :], t2[:, :]), reads=[t2.r], writes=[t2.r])
            P.op("dve", lambda e: e.tensor_mul(t3[:, :], t1[:, :], ar[:, :]), reads=[t1.r, ar.r], writes=[t3.r])
            P.op("dve", lambda e: e.tensor_mul(t4[:, :], li1, ai[:, :]), reads=[r_lim, ai.r], writes=[t4.r])
            P.op("dve", lambda e: e.tensor_add(t3[:, :], t3[:, :], t4[:, :]), reads=[t3.r, t4.r], writes=[t3.r])
            P.op("dve", lambda e: e.tensor_mul(cre[:, :], t3[:, :], t2[:, :]), reads=[t3.r, t2.r], writes=[cre.r])
            P.op("dve", lambda e: e.tensor_mul(t3[:, :], li1, ar[:, :]), reads=[r_lim, ar.r, cre.r], writes=[t3.r])
            P.op("dve", lambda e: e.tensor_mul(t4[:, :], t1[:, :], ai[:, :]), reads=[t1.r, ai.r, t3.r], writes=[t4.r])
            P.op("dve", lambda e: e.tensor_sub(t3[:, :], t3[:, :], t4[:, :]), reads=[t3.r, t4.r], writes=[t3.r])
            P.op("dve", lambda e: e.tensor_mul(cim[:, :], t3[:, :], t2[:, :]), reads=[t3.r, t2.r], writes=[cim.r])

            creb = cre[:, :].unsqueeze(2).to_broadcast([128, 16, 16])
            cimb = cim[:, :].unsqueeze(2).to_broadcast([128, 16, 16])
            P.op("dve", lambda e: e.tensor_mul(Bt1[:, :, :], Bn_re[:, :, :], creb), reads=[Bn_re.r, cre.r], writes=[Bt1.r])
            P.op("dve", lambda e: e.tensor_mul(Bt2[:, :, :], Bn_im[:, :, :], cimb), reads=[Bn_im.r, cim.r], writes=[Bt2.r])
            P.op("dve", lambda e: e.tensor_sub(Bb_re[:, :, :], Bt1[:, :, :], Bt2[:, :, :]), reads=[Bt1.r, Bt2.r], writes=[Bb_re.r])
            P.op("dve", lambda e: e.tensor_mul(Bt1[:, :, :], Bn_im[:, :, :], creb), reads=[Bn_im.r, cre.r, Bb_re.r], writes=[Bt1.r])
            P.op("dve", lambda e: e.tensor_mul(Bt2[:, :, :], Bn_re[:, :, :], cimb), reads=[Bn_re.r, cim.r, Bb_re.r], writes=[Bt2.r])
            P.op("dve", lambda e: e.tensor_add(Bb_im[:, :, :], Bt1[:, :, :], Bt2[:, :, :]), reads=[Bt1.r, Bt2.r], writes=[Bb_im.r])
            P.op("pool", lambda e: e.memset(Bpad[:, :, :, :], 0.0), writes=[Bpad.r])
            for ri, src in enumerate((Bb_re, Bb_im)):
                for h in range(2):
                    P.op("dve", (lambda ri, src, h: lambda e: e.tensor_copy(
                        Bpad[h * 64:(h + 1) * 64, ri, :, h * 16:(h + 1) * 16], src[h * 64:(h + 1) * 64, :, :]))(ri, src, h),
                        reads=[src.r], writes=[Bpad.r])
            CTf = sb("CTf", [128, 2, 16, 32], F32, s0)
            Bpb = sb("Bpb", [128, 2, 16, 32], BF16, s0)
            BL = [sb("BL%d" % i, [128, 2, 16, 32], F32, s0) for i in range(1)]
            Wtmp = [sb("Wtmp%d" % i, [128, 16, 32], F32, s0) for i in range(4)]
            I32 = sb("I32", [128, 32], F32, s0)
            wk = [0]

            def nxtw():
                t = Wtmp[wk[0] % 4]
                wk[0] += 1
                return t

            def lb(m, c):
                return lam[:, :, MI[m], c].unsqueeze(2).to_broadcast([128, 16, 32])
            k = 0
            for m in range(TC if 'b' not in SKIP else 1):
                if m == 0:
                    srcB = Bpad
                else:
                    srcB = BL[0]
                    w1, w2 = nxtw(), nxtw()
                    P.op("dve", (lambda m, srcB: lambda e: e.tensor_tensor(out=srcB[:, 0, :, :], in0=Bpad[:, 0, :, :], in1=lb(m, 0), op=ALU.mult))(m, srcB),
                         reads=[Bpad.r] + lam_regs, writes=[srcB.r])
                    P.op("dve", (lambda m, w1: lambda e: e.tensor_tensor(out=w1[:, :, :], in0=Bpad[:, 1, :, :], in1=lb(m, 2), op=ALU.mult))(m, w1),
                         reads=[Bpad.r] + lam_regs, writes=[w1.r])
                    P.op("dve", (lambda srcB, w1: lambda e: e.tensor_add(srcB[:, 0, :, :], srcB[:, 0, :, :], w1[:, :, :]))(srcB, w1),
                         reads=[srcB.r, w1.r], writes=[srcB.r])
                    P.op("dve", (lambda m, srcB: lambda e: e.tensor_tensor(out=srcB[:, 1, :, :], in0=Bpad[:, 0, :, :], in1=lb(m, 1), op=ALU.mult))(m, srcB),
                         reads=[Bpad.r] + lam_regs, writes=[srcB.r])
                    P.op("dve", (lambda m, w2: lambda e: e.tensor_tensor(out=w2[:, :, :], in0=Bpad[:, 1, :, :], in1=lb(m, 0), op=ALU.mult))(m, w2),
                         reads=[Bpad.r] + lam_regs, writes=[w2.r])
                    P.op("dve", (lambda srcB, w2: lambda e: e.tensor_add(srcB[:, 1, :, :], srcB[:, 1, :, :], w2[:, :, :]))(srcB, w2),
                         reads=[srcB.r, w2.r], writes=[srcB.r])
                for ri in range(2):
                    for gq in range(4):
                        pt = ps0[k % 4]
                        k += 1
                        P.op("pe", (lambda ri, gq, pt, srcB: lambda e: e.transpose(
                            pt[:, 0:128], srcB[:, ri, gq * 4:(gq + 1) * 4, :], ident_f[:, :]))(ri, gq, pt, srcB),
                            reads=[srcB.r, ident_f.r], writes=[pt.r])
                        P.op("act", (lambda ri, gq, pt, m: lambda e: e.copy(BbTs[:, ri, gq, TC - 1 - m, :], pt[:, 0:128]))(ri, gq, pt, m),
                             reads=[pt.r], writes=[BbTs.r])
            P.op("dve", lambda e: e.tensor_copy(Bpb[:, :, :, :], Bpad[:, :, :, :]), reads=[Bpad.r], writes=[Bpb.r])

            Cnat = [sb("Cnat%d" % i, [128, 4, 64], F32, s0) for i in range(2)]
            pidx = sb("pidx", [128, 1], mybir.dt.int32, s0)
            mko = sb("mko", [128, 1], F32, s0)
            mke = sb("mke", [128, 1], F32, s0)
            P.op("pool", lambda e: e.iota(pidx[:, :], [[0, 1]], base=0, channel_multiplier=1), writes=[pidx.r])
            P.op("dve", lambda e: e.tensor_scalar(pidx[:, :], pidx[:, :], 4, 1, ALU.arith_shift_right, ALU.bitwise_and),
                 reads=[pidx.r], writes=[pidx.r])
            P.op("dve", lambda e: e.tensor_copy(mko[:, :], pidx[:, :]), reads=[pidx.r], writes=[mko.r])
            P.op("dve", lambda e: e.tensor_scalar(mke[:, :], mko[:, :], -1.0, 1.0, ALU.mult, ALU.add), reads=[mko.r], writes=[mke.r])
            for ri, csrc in enumerate((c_re, c_im)):
                cin = Cin[ri]
                cn_ = Cnat[ri]
                P.op("sp", (lambda cn_, csrc: lambda e: e.dma_start(
                    out=cn_[:, :, :], in_=dram_ap(csrc, 0, [[64, 128], [128 * 64, 4], [1, 64]])))(cn_, csrc),
                    writes=[cn_.r], dma=True)
                P.op("dve", (lambda cin, cn_: lambda e: e.tensor_scalar(cin[:, :, 0:64], cn_[:, :, :], mke[:, 0:1], None, ALU.mult))(cin, cn_),
                     reads=[cn_.r, mke.r], writes=[cin.r])
                P.op("dve", (lambda cin, cn_: lambda e: e.tensor_scalar(cin[:, :, 64:128], cn_[:, :, :], mko[:, 0:1], None, ALU.mult))(cin, cn_),
                     reads=[cn_.r, mko.r], writes=[cin.r])
                for r in range(4):
                    pt = ps0[k % 4]
                    k += 1
                    P.op("pe", (lambda cin, r, pt: lambda e: e.transpose(pt[:, 0:128], cin[:, r, :], ident_f[:, :]))(cin, r, pt),
                         reads=[cin.r, ident_f.r], writes=[pt.r])
                    pv = pt[:, 0:128].rearrange("p (a c) -> p a c", c=32)
                    if ri == 0:
                        P.op("act", (lambda r, pv: lambda e: e.copy(CT[:, 0, r * 4:(r + 1) * 4, :], pv))(r, pv),
                             reads=[pt.r], writes=[CT.r, pt.r])
                    else:
                        P.op("act", (lambda r, pv: lambda e: e.mul(CT[:, 1, r * 4:(r + 1) * 4, :], pv, -1.0))(r, pv),
                             reads=[pt.r], writes=[CT.r, pt.r])
                    P.op("dve", (lambda ri, r, pv: lambda e: e.tensor_copy(CTf[:, ri, r * 4:(r + 1) * 4, :], pv))(ri, r, pv),
                         reads=[pt.r], writes=[CTf.r, pt.r])

            for kk in range(TC if 'c' not in SKIP else 0):
                m = kk + 1
                w1, w2, w3, w4 = nxtw(), nxtw(), nxtw(), nxtw()
                P.op("dve", (lambda m, w1: lambda e: e.tensor_tensor(out=w1[:, :, :], in0=CTf[:, 0, :, :], in1=lb(m, 0), op=ALU.mult))(m, w1),
                     reads=[CTf.r] + lam_regs, writes=[w1.r])
                P.op("dve", (lambda m, w2: lambda e: e.tensor_tensor(out=w2[:, :, :], in0=CTf[:, 1, :, :], in1=lb(m, 2), op=ALU.mult))(m, w2),
                     reads=[CTf.r] + lam_regs, writes=[w2.r])
                P.op("dve", (lambda kk, w1, w2: lambda e: e.tensor_add(Wt[:, 0, kk, :, :], w1[:, :, :], w2[:, :, :]))(kk, w1, w2),
                     reads=[w1.r, w2.r], writes=[Wt.r])
                P.op("dve", (lambda m, w3: lambda e: e.tensor_tensor(out=w3[:, :, :], in0=CTf[:, 0, :, :], in1=lb(m, 2), op=ALU.mult))(m, w3),
                     reads=[CTf.r] + lam_regs, writes=[w3.r])
                P.op("dve", (lambda m, w4: lambda e: e.tensor_tensor(out=w4[:, :, :], in0=CTf[:, 1, :, :], in1=lb(m, 0), op=ALU.mult))(m, w4),
                     reads=[CTf.r] + lam_regs, writes=[w4.r])
                P.op("dve", (lambda kk, w3, w4: lambda e: e.tensor_sub(Wt[:, 1, kk, :, :], w3[:, :, :], w4[:, :, :]))(kk, w3, w4),
                     reads=[w3.r, w4.r], writes=[Wt.r])

            for gq in range(4 if 't' not in SKIP else 0):
                pt = ps0[gq]

                def kmm(e, gq=gq, pt=pt):
                    ins = None
                    for a4 in range(4):
                        gp = gq * 4 + a4
                        R0, R1 = a4 * 32, a4 * 32 + 32
                        e.matmul(pt[R0:R1, 0:32], Bpb[:, 0, gp, :], CT[:, 0, gp, :], start=True, stop=False, tile_position=(0, R0))
                        e.matmul(pt[R0:R1, 0:32], Bpb[:, 1, gp, :], CT[:, 1, gp, :], start=False, stop=True, tile_position=(0, R0))
                        e.matmul(pt[R0:R1, 32:256], Bpb[:, 0, gp, :], Wt[:, 0, 0:TC - 1, gp, :], start=True, stop=False, tile_position=(0, R0))
                        ins = e.matmul(pt[R0:R1, 32:256], Bpb[:, 1, gp, :], Wt[:, 1, 0:TC - 1, gp, :], start=False, stop=True, tile_position=(0, R0))
                    return ins
                if "k" not in SKIP:
                    P.op("pe", kmm, reads=[Bpb.r, CT.r, Wt.r], writes=[pt.r])
                P.op("act", (lambda gq, pt: lambda e: e.copy(Kt[:, gq, :, :], pt[:, 0:256].rearrange("p (t c) -> p t c", c=32)))(gq, pt),
                     reads=[pt.r], writes=[Kt.r])
            P.op("dve", lambda e: e.tensor_add(I32[:, :], ident_f[:, 0:32], ident_f[:, 32:64]), reads=[ident_f.r], writes=[I32.r])
            P.op("dve", lambda e: e.tensor_add(I32[:, :], I32[:, :], ident_f[:, 64:96]), reads=[ident_f.r, I32.r], writes=[I32.r])
            P.op("dve", lambda e: e.tensor_add(I32[:, :], I32[:, :], ident_f[:, 96:128]), reads=[ident_f.r, I32.r], writes=[I32.r])
            for gq in range(4 if 't' not in SKIP else 0):
                P.op("dve", (lambda gq: lambda e: e.scalar_tensor_tensor(
                    out=Kt[:, gq, 0, :], in0=I32[:, :], scalar=dcol[:, gq:gq + 1], in1=Kt[:, gq, 0, :],
                    op0=ALU.mult, op1=ALU.add))(gq), reads=[I32.r, Kt.r, dcol.r], writes=[Kt.r])
            P.op("pool", lambda e: e.memset(Ktf[:, :, :, :], 0.0), writes=[Ktf.r])
            for a4 in range(4):
                P.op("dve", (lambda a4: lambda e: e.tensor_copy(
                    Ktf[a4 * 32:(a4 + 1) * 32, :, :, a4 * 32:(a4 + 1) * 32], Kt[a4 * 32:(a4 + 1) * 32, :, :, :]))(a4),
                    reads=[Kt.r], writes=[Ktf.r])
        n_ops_p0 = len(P.ops)

        with ExitStack() as sAC:
            ypT = Sub(YU, 0)
            uT = Sub(YU, 4)
            hT_r = [Reg() for _ in TTILES]
            yp_r = [[Reg() for _ in CHUNKS] for _ in range(4)]
            ys_r = [[Reg() for _ in CHUNKS] for _ in range(4)]

            with ExitStack() as sAB:
                vT_r = [[Reg() for _ in CHUNKS] for _ in range(4)]
                uT_r = [[Reg() for _ in CHUNKS] for _ in range(4)]
                vpad_r = [Reg() for _ in range(4)]
                vTs_r = [Reg() for _ in range(4)]

                with ExitStack() as sA:
                    g1b = sb("g1b", [128, D], F32, sA)
                    P.op("sp", lambda e: e.dma_start(out=g1b[:, :], in_=n1g.partition_broadcast(128)), writes=[g1b.r], dma=True)
                    xt = [sb("xt%d" % i, [128, D], F32, sA) for i in range(2)]
                    sq = [sb("sq%d" % i, [128, D], BF16, sA) for i in range(1)] * 2
                    hb = [sb("hb%d" % i, [128, D], BF16, sA) for i in range(2)]
                    ssq = [sb("ssq%d" % i, [128, 1], F32, sA) for i in range(2)]
                    rstd = [sb("rstd%d" % i, [128, 1], F32, sA) for i in range(2)]
                    ptr = [pst("ptr%d" % i, [128, 8, 128], BF16, sA) for i in range(2)]
                    pmm = [pst("pmm%d" % i, [128, 512], F32, sA) for i in range(2)]

                    wvu_r = [Reg() for _ in range(8)]
                    for kk in range(8):
                        P.op("pool", (lambda kk: lambda e: e.dma_start(
                            out=wvu[:, kk, :], in_=w_in[kk * 128:(kk + 1) * 128, 0:1024]))(kk),
                            writes=[wvu_r[kk]], dma=True)

                    pend_a = []
                    for ti, (t0, tn) in enumerate(TTILES):
                        b = ti % 2
                        src = x[t0:t0 + tn, :] if ti < 16 else xs[:, :]
                        P.op("sp", (lambda b, tn, src: lambda e: e.dma_start(out=xt[b][0:tn, :], in_=src))(b, tn, src),
                             writes=[xt[b].r], dma=True)
                        P.op("act", (lambda b, tn: lambda e: e.activation(
                            out=sq[b][0:tn, :], in_=xt[b][0:tn, :], func=AF.Square, accum_out=ssq[b][0:tn, :]))(b, tn),
                            reads=[xt[b].r], writes=[sq[b].r, ssq[b].r])
                        P.op("dve", (lambda b, tn: lambda e: e.tensor_scalar(
                            rstd[b][0:tn, :], ssq[b][0:tn, :], 1.0 / D, EPS, ALU.mult, ALU.add))(b, tn),
                            reads=[ssq[b].r], writes=[rstd[b].r])
                        P.op("act", (lambda b, tn: lambda e: e.activation(
                            out=rstd[b][0:tn, :], in_=rstd[b][0:tn, :], func=AF.Sqrt))(b, tn),
                            reads=[rstd[b].r], writes=[rstd[b].r])
                        P.op("dve", (lambda b, tn: lambda e: e.reciprocal(rstd[b][0:tn, :], rstd[b][0:tn, :]))(b, tn),
                             reads=[rstd[b].r], writes=[rstd[b].r])
                        P.op("dve", (lambda b, tn: lambda e: e.scalar_tensor_tensor(
                            out=hb[b][0:tn, :], in0=xt[b][0:tn, :], scalar=rstd[b][0:tn, 0:1], in1=g1b[0:tn, :],
                            op0=ALU.mult, op1=ALU.mult))(b, tn),
                            reads=[xt[b].r, rstd[b].r, g1b.r], writes=[hb[b].r])

                        def tr(e, b=b, tn=tn):
                            ins = None
                            for kk in range(8):
                                ins = e.transpose(ptr[b][:, kk, 0:tn], hb[b][0:tn, kk * 128:(kk + 1) * 128], ident_b[0:tn, 0:tn])
                            return ins
                        def late_a(tr=tr, b=b, t0=t0, tn=tn, ti=ti):
                            P.op("pe", tr, reads=[hb[b].r, ident_b.r], writes=[ptr[b].r])
                            P.op("act", (lambda b, t0, tn: lambda e: e.copy(hT[:, :, t0:t0 + tn], ptr[b][:, :, 0:tn]))(b, t0, tn),
                                 reads=[ptr[b].r], writes=[hT_r[ti], ptr[b].r])
                        while pend_a:
                            pend_a.pop(0)()
                        pend_a.append(late_a)
                    while pend_a:
                        pend_a.pop(0)()

                    k = 0
                    for ci, (c0, cn) in enumerate(CHUNKS):
                        tiles = [ti for ti, (t0, tn) in enumerate(TTILES) if t0 >= c0 and t0 < c0 + cn]
                        for m in range(4, 8):
                            pt = pmm[k % 2]
                            k += 1

                            def mmf(e, m=m, c0=c0, cn=cn, pt=pt):
                                ins = None
                                for kk in range(8):
                                    ins = e.matmul(pt[:, 0:cn], wvu[:, kk, m * 128:(m + 1) * 128], hT[:, kk, c0:c0 + cn],
                                                   start=(kk == 0), stop=(kk == 7))
                                return ins
                            P.op("pe", mmf, reads=[hT_r[ti] for ti in tiles] + wvu_r, writes=[pt.r])
                            P.op("dve", (lambda m, c0, cn, pt: lambda e: e.tensor_copy(
                                uT[:, m - 4, c0:c0 + cn], pt[:, 0:cn]))(m, c0, cn, pt),
                                reads=[pt.r], writes=[uT_r[m - 4][ci], pt.r])
                    P.interleave(8, n_ops_p0)
                    P.flush()

                s0.close()
                if dbg == "A":
                    dbg_outs["d_hT"] = (hT, [128, 8 * NCOL], BF16)
                    dbg_outs["d_YU"] = (YU, [128, 8 * NCOL], BF16)
                    dbg_outs["d_lam"] = (lam, [128, 16 * NM * 3], F32)
                    dbg_outs["d_BbTs"] = (BbTs, [128, 2 * 4 * TC * 128], BF16)
                    dbg_outs["d_CT"] = (CT, [128, 2 * 16 * 32], BF16)
                    dbg_outs["d_Wt"] = (Wt, [128, 2 * TC * 16 * 32], BF16)
                    dbg_outs["d_Kt"] = (Kt, [128, 4 * TC * 32], BF16)
                    _emit_dbg(nc, P, dbg_outs, es)
                    P.flush(final=True)
                    return nc, list(dbg_outs.keys())

                with ExitStack() as sB:
                    vT = sb("vT", [128, 4, PAD + SEQ], F32, sB)
                    vTs = sb("vTs", [128, 4, NS, 16], F32, sB)
                    pw_b = sb("pw_b", [128, 4, 128], BF16, sB)
                    Abuf = sb("Abuf", [128, PAD + SEQ], F32, sB)
                    Bbuf = sb("Bbuf", [128, PAD + SEQ], F32, sB)
                    ptmp = sb("ptmp", [128, 16], F32, sB)
                    wsum = sb("wsum", [128, 16], F32, sB)
                    cpl = [sb("cpl%d" % i, [120, 512], F32, sB) for i in range(2)]
                    tok15 = sb("tok15", [16, 512], F32, sB)
                    toks = sb("toks", [16, 512], F32, sB)
                    H0 = sb("H0", [128, 2, 16, 16], F32, sB)
                    Hn = sb("Hn", [128, 2, 16, 16], F32, sB)
                    BUs = sb("BUs", [128, 2, 16, 16], F32, sB)
                    T1 = sb("T1", [128, 16, 16], F32, sB)
                    T2 = sb("T2", [128, 16, 16], F32, sB)
                    srow = [Abuf, Bbuf]
                    pB = [pst("pB%d" % i, [128, 512], F32, sB) for i in range(8)]
                    pbk = [0]

                    def nextp():
                        t = pB[pbk[0] % 8]
                        pbk[0] += 1
                        return t

                    pw_r = [Reg() for _ in range(4)]
                    for q in range(4):
                        P.op("pool", (lambda q: lambda e: e.dma_start(out=pw_b[:, q, :], in_=pool_w[q * 128:(q + 1) * 128, :]))(q),
                             writes=[pw_r[q]], dma=True)
                    for j in range(2):
                        P.op("sp", (lambda j: lambda e: e.dma_start(out=cpl[j][:, :], in_=cpool[j * 120:(j + 1) * 120, :]))(j),
                             writes=[cpl[j].r], dma=True)
                    if 'd' not in SKIP:
                      P.op("sp", lambda e: e.dma_start(
                        out=dram_ap(o_pool_s, 0, [[PB * PW, NS], [1, 14 * PW]]),
                        in_=dram_ap(cpool, PW, [[PB * PW, NS], [1, 14 * PW]])), dma=True, out=True)
                    for j in range(2 if 't' not in SKIP else 0):
                        for q in range(4):
                            pt = nextp()
                            P.op("pe", (lambda j, q, pt: lambda e: e.transpose(
                                pt[:, 0:120], cpl[j][0:120, q * 128:(q + 1) * 128], ident_f[0:120, 0:120]))(j, q, pt),
                                reads=[cpl[j].r], writes=[pt.r])
                            P.op("act", (lambda j, q, pt: lambda e: e.copy(
                                vTs[:, q, j * 8:(j + 1) * 8, 0:15], pt[:, 0:120].rearrange("p (b r) -> p b r", r=15)))(j, q, pt),
                                reads=[pt.r], writes=[vTs_r[q]])

                    P.op("dve", lambda e: e.memset(vT[:, :, 0:PAD], 0.0), writes=vpad_r)
                    for m in range(4):
                        for ci, (c0, cn) in enumerate(CHUNKS):
                            pt = nextp()

                            def mmv(e, m=m, c0=c0, cn=cn, pt=pt):
                                ins = None
                                for kk in range(8):
                                    ins = e.matmul(pt[:, 0:cn], wvu[:, kk, m * 128:(m + 1) * 128], hT[:, kk, c0:c0 + cn],
                                                   start=(kk == 0), stop=(kk == 7))
                                return ins
                            P.op("pe", mmv, writes=[pt.r])
                            if ci < 4:
                                P.op("act", (lambda m, c0, cn, pt: lambda e: e.copy(
                                    vT[:, m, PAD + c0:PAD + c0 + cn], pt[:, 0:cn]))(m, c0, cn, pt),
                                    reads=[pt.r], writes=[vT_r[m][ci], pt.r])
                            else:
                                P.op("act", (lambda m, pt: lambda e: e.copy(vTs[:, m, :, 15], pt[:, 0:NS]))(m, pt),
                                     reads=[pt.r], writes=[vT_r[m][ci], pt.r])
                    P.op("dve", lambda e: e.memset(Abuf[:, 0:PAD], 0.0), writes=[Abuf.r])
                    P.op("dve", lambda e: e.memset(Bbuf[:, 0:PAD], 0.0), writes=[Bbuf.r])

                    for q in range(4):
                        w = 2 ** (q + 1)
                        allv = [vT_r[q][ci] for ci in range(4)] + [vpad_r[q]]
                        src = vT
                        P.op("dve", (lambda q: lambda e: e.tensor_add(
                            Abuf[:, PAD:], vT[:, q, PAD:], vT[:, q, PAD - 1:PAD + SEQ - 1]))(q),
                            reads=allv, writes=[Abuf.r])
                        cur, oth = Abuf, Bbuf
                        sh = 2
                        while sh < w:
                            P.op("dve", (lambda cur, oth, sh: lambda e: e.tensor_add(
                                oth[:, PAD:], cur[:, PAD:], cur[:, PAD - sh:PAD + SEQ - sh]))(cur, oth, sh),
                                reads=[cur.r], writes=[oth.r])
                            cur, oth = oth, cur
                            sh *= 2
                        P.op("dve", (lambda q, cur, w: lambda e: e.scalar_tensor_tensor(
                            out=ypT[:, q, 0:SEQ], in0=cur[:, PAD:], scalar=1.0 / w, in1=vT[:, q, PAD:],
                            op0=ALU.mult, op1=ALU.subtract))(q, cur, w),
                            reads=[cur.r] + allv, writes=[yp_r[q][ci] for ci in range(4)])
                        if w > 1:
                            P.op("dve", (lambda q, cur, w: lambda e: e.tensor_mul(
                                ptmp[:, 0:w - 1], cur[:, PAD:PAD + w - 1], rct[:, 0:w - 1]))(q, cur, w),
                                reads=[cur.r], writes=[ptmp.r])
                            P.op("dve", (lambda q, w: lambda e: e.tensor_sub(
                                ypT[:, q, 0:w - 1], ptmp[:, 0:w - 1], vT[:, q, PAD:PAD + w - 1]))(q, w),
                                reads=[ptmp.r] + allv, writes=[yp_r[q][0]])
                        P.op("dve", (lambda q, w: lambda e: e.tensor_reduce(
                            out=wsum[:, :], in_=vTs[:, q, :, 16 - w:16], axis=AX.X, op=ALU.add))(q, w),
                            reads=[vTs_r[q], vT_r[q][4]], writes=[wsum.r])
                        P.op("dve", (lambda q, w: lambda e: e.scalar_tensor_tensor(
                            out=ypT[:, q, SEQ:NCOL], in0=wsum[:, :], scalar=1.0 / w, in1=vTs[:, q, :, 15],
                            op0=ALU.mult, op1=ALU.subtract))(q, w),
                            reads=[wsum.r, vT_r[q][4]], writes=[yp_r[q][4]])
                        if 'o' in SKIP:
                            continue
                        pt = nextp()
                        P.op("pe", (lambda q, pt: lambda e: e.transpose(
                            pt[0:16, 0:128], vT[:, q, PAD + SEQ - 16:PAD + SEQ], ident_f[:, :]))(q, pt),
                            reads=[vT_r[q][3]], writes=[pt.r])
                        P.op("act", (lambda q, pt: lambda e: e.copy(tok15[0:16, q * 128:(q + 1) * 128], pt[0:16, 0:128]))(q, pt),
                             reads=[pt.r], writes=[tok15.r])
                        pt = nextp()
                        P.op("pe", (lambda q, pt: lambda e: e.transpose(
                            pt[0:16, 0:128], vTs[:, q, :, 15], ident_f[:, :]))(q, pt),
                            reads=[vT_r[q][4]], writes=[pt.r])
                        P.op("act", (lambda q, pt: lambda e: e.copy(toks[0:16, q * 128:(q + 1) * 128], pt[0:16, 0:128]))(q, pt),
                             reads=[pt.r], writes=[toks.r])
                        for ci, (c0, cn) in enumerate(CHUNKS):
                            pt = nextp()
                            P.op("pe", (lambda q, c0, cn, pt: lambda e: e.matmul(
                                pt[:, 0:cn], pw_b[:, q, :], ypT[:, q, c0:c0 + cn], start=True, stop=True))(q, c0, cn, pt),
                                reads=[pw_r[q], yp_r[q][ci]], writes=[pt.r])
                            P.op("act", (lambda q, c0, cn, pt: lambda e: e.mul(
                                ypT[:, q, c0:c0 + cn], pt[:, 0:cn], pscol[:, q:q + 1]))(q, c0, cn, pt),
                                reads=[pt.r], writes=[yp_r[q][ci]])
                    P.op("sp", lambda e: e.dma_start(out=o_pool_p[:, :], in_=tok15[1:16, :]), reads=[tok15.r], dma=True, out=True)
                    P.op("sp", lambda e: e.dma_start(out=dram_ap(o_pool_s, 14 * PW, [[PB * PW, NS], [1, PW]]), in_=toks[0:16, :]),
                         reads=[toks.r], dma=True, out=True)

                    if dbg == "B1":
                        P.flush()
                        dbg_outs["d_YU"] = (YU, [128, 8 * NCOL], BF16)
                        _emit_dbg(nc, P, dbg_outs, es)
                        P.flush(final=True)
                        return nc, list(dbg_outs.keys()) + ["o_pool_p", "o_pool_s"]
                    P.op("sp", lambda e: e.dma_start(out=srow[0][0:16, 0:2048], in_=sre[:, :]), writes=[srow[0].r], dma=True)
                    P.op("sp", lambda e: e.dma_start(out=srow[1][0:16, 0:2048], in_=sim[:, :]), writes=[srow[1].r], dma=True)
                    for ri in range(2):
                        pt = nextp()

                        def trs(e, ri=ri, pt=pt):
                            ins = None
                            for gp in range(16):
                                ins = e.transpose(pt[:, gp * 16:(gp + 1) * 16], srow[ri][0:16, gp * 128:(gp + 1) * 128], ident_f[0:16, 0:16])
                            return ins
                        if '1' not in SKIP:
                            P.op("pe", trs, reads=[srow[ri].r], writes=[pt.r])
                        P.op("act", (lambda ri, pt: lambda e: e.copy(
                            H0[:, ri, :, :], pt[:, 0:256].rearrange("p (g b) -> p g b", b=16)))(ri, pt),
                            reads=[pt.r], writes=[H0.r])
                    for ri in range(2):
                        for a4 in range(4):
                            pt = nextp()

                            def bus(e, ri=ri, pt=pt, a4=a4):
                                ins = None
                                for gq in range(4):
                                    ins = e.matmul(pt[:, gq * 16:(gq + 1) * 16], BbTs[a4 * 32:(a4 + 1) * 32, ri, gq, TC - 1, :],
                                                   uT[a4 * 32:(a4 + 1) * 32, gq, SEQ:NCOL], start=True, stop=True,
                                                   tile_position=(a4 * 32, 0))
                                return ins
                            if '2' not in SKIP:
                                P.op("pe", bus, reads=[uT_r[gq][4] for gq in range(4)], writes=[pt.r])
                            P.op("act", (lambda ri, pt, a4: lambda e: e.copy(
                                BUs[:, ri, a4:16:4, :], pt[:, 0:64].rearrange("p (g b) -> p g b", b=16)))(ri, pt, a4),
                                reads=[pt.r], writes=[BUs.r])
                    lreb = lam[:, :, 0, 0].unsqueeze(2).to_broadcast([128, 16, 16])
                    limb = lam[:, :, 0, 1].unsqueeze(2).to_broadcast([128, 16, 16])
                    P.op("dve", lambda e: e.tensor_mul(T1[:, :, :], H0[:, 0, :, :], lreb), reads=[H0.r], writes=[T1.r])
                    P.op("dve", lambda e: e.tensor_mul(T2[:, :, :], H0[:, 1, :, :], limb), reads=[H0.r], writes=[T2.r])
                    P.op("dve", lambda e: e.tensor_sub(T1[:, :, :], T1[:, :, :], T2[:, :, :]), reads=[T1.r, T2.r], writes=[T1.r])
                    P.op("dve", lambda e: e.tensor_add(Hn[:, 0, :, :], T1[:, :, :], BUs[:, 0, :, :]), reads=[T1.r, BUs.r], writes=[Hn.r])
                    P.op("dve", lambda e: e.tensor_mul(T1[:, :, :], H0[:, 1, :, :], lreb), reads=[H0.r, Hn.r], writes=[T1.r])
                    P.op("dve", lambda e: e.tensor_mul(T2[:, :, :], H0[:, 0, :, :], limb), reads=[H0.r, Hn.r], writes=[T2.r])
                    P.op("dve", lambda e: e.tensor_add(T1[:, :, :], T1[:, :, :], T2[:, :, :]), reads=[T1.r, T2.r], writes=[T1.r])
                    r_hn1 = Reg()
                    P.op("dve", lambda e: e.tensor_add(Hn[:, 1, :, :], T1[:, :, :], BUs[:, 1, :, :]), reads=[T1.r, BUs.r, Hn.r], writes=[r_hn1])
                    P.op("act", lambda e: e.copy(Hnb[:, :, :, :], Hn[:, :, :, :]), reads=[Hn.r, r_hn1], writes=[Hnb.r])
                    for ri in range(2):
                        for g4 in range(4):
                            pt = nextp()

                            def tro(e, ri=ri, g4=g4, pt=pt):
                                ins = None
                                for a in range(4):
                                    gp = g4 * 4 + a
                                    ins = e.transpose(pt[0:16, a * 128:(a + 1) * 128], Hn[:, ri, gp, :], ident_f[:, :])
                                return ins
                            if '3' not in SKIP:
                                P.op("pe", tro, reads=[Hn.r, r_hn1], writes=[pt.r])
                            P.op("act", (lambda ri, g4, pt: lambda e: e.copy(
                                srow[ri][0:16, g4 * 512:(g4 + 1) * 512], pt[0:16, 0:512]))(ri, g4, pt),
                                reads=[pt.r], writes=[srow[ri].r])
                    P.op("sp", lambda e: e.dma_start(out=o_re_s[:, :], in_=srow[0][0:16, 0:2048]), reads=[srow[0].r], dma=True, out=True)
                    P.op("sp", lambda e: e.dma_start(out=o_im_s[:, :], in_=srow[1][0:16, 0:2048]), reads=[srow[1].r], dma=True, out=True)

                    if dbg == "B2":
                        P.flush()
                        dbg_outs["d_YU"] = (YU, [128, 8 * NCOL], BF16)
                        _emit_dbg(nc, P, dbg_outs, es)
                        P.flush(final=True)
                        return nc, list(dbg_outs.keys()) + ["o_pool_p", "o_pool_s", "o_re_s", "o_im_s"]
                    P.flush()

            sW.close()
            with ExitStack() as sS:
                Eall = sb("Eall", [128, 2, 16, NJ], F32, sS)
                Sbf = sb("Sbf", [128, 2, 16, NJ], BF16, sS)
                uD = sb("uD", [128, 4, TC, NJ], BF16, sS)
                wg1 = sb("wg1", [128, 8, 1024], BF16, sS)
                stg = [sb("stg%d" % i, [128, 512], BF16, sS) for i in range(2)]
                wg1_r = [Reg() for _ in range(8)]
                for kk in range(8):
                    P.op("pool", (lambda kk: lambda e: e.dma_start(out=wg1[:, kk, :], in_=w_in[kk * 128:(kk + 1) * 128, 1024:2048]))(kk),
                         writes=[wg1_r[kk]], dma=True)
                tmpS = [sb("tmpS%d" % i, [128, 16, 64], F32, sS) for i in range(2)]
                tmpSs = [sb("tmpSs%d" % i, [128, 16, 16], F32, sS) for i in range(2)]
                Hfin = sb("Hfin", [128, 2, 16], F32, sS)
                fin = sb("fin", [16, 2, 128], F32, sS)
                pS = pst("pS", [128, 8, 512], F32, sS)
                bank_r = [[Reg() for _ in range(4)] for _ in range(8)]
                uTg_r = [Reg() for _ in range(16)]
                LV2 = [(4 ** l, 4 ** l - 1, NJ // (4 ** l)) for l in range(5)]
                cls = [dict() for _ in range(2)]
                for ri in range(2):
                    for l in range(4):
                        for kk in range(3):
                            cls[ri][(l, kk)] = Reg()
                    cls[ri][(4, 0)] = Reg()
                allcls = [list(cls[ri].values()) for ri in range(2)]

                def deeper(ri, l):
                    return [r for (ll, kk), r in cls[ri].items() if ll > l]

                uD_r = [Reg() for _ in range(4)]
                uTq_r = [Reg() for _ in range(4)]
                for gq in range(4):
                    src_v = YU.t[:, 4 + gq, 0:SEQ].rearrange("p (j k) -> p k j", k=TC)
                    if gq % 2 == 0:
                        P.op("act", (lambda gq, src_v: lambda e: e.copy(uD[:, gq, :, :], src_v))(gq, src_v), reads=[uTq_r[gq]], writes=[uD_r[gq]])
                    else:
                        P.op("pool", (lambda gq, src_v: lambda e: e.tensor_copy(uD[:, gq, :, :], src_v))(gq, src_v), reads=[uTq_r[gq]], writes=[uD_r[gq]])
                bbts_r = [Reg()]
                wg2_v = BbTs.t[:].rearrange("p a b c d -> p (a b c d)").rearrange("p (k c) -> p k c", c=1024)
                idx = 0
                for gq in range(4):
                    for ri in range(2):
                        bks = [(idx * 4 + a4) % 8 for a4 in range(4)]
                        idx += 1

                        def emm(e, ri=ri, bks=bks, gq=gq):
                            ins = None
                            for s_ in range(TC):
                                for a4 in range(4):
                                    R0, R1 = a4 * 32, a4 * 32 + 32
                                    ins = e.matmul(pS[:, bks[a4], 0:NJ], BbTs[R0:R1, ri, gq, s_, :], uD[R0:R1, gq, s_, :],
                                                   start=(s_ == 0), stop=(s_ == TC - 1), tile_position=(R0, 0))
                            return ins
                        wr = []
                        for a4 in range(4):
                            wr += bank_r[bks[a4]]
                        P.op("pe", emm, reads=[uD_r[gq]] + bbts_r, writes=wr)
                        for a4 in range(4):
                            gp = gq * 4 + a4
                            P.op("act" if a4 % 2 == 0 else "dve", (lambda ri, gp, bk: lambda e: (
                                e.copy(Eall[:, ri, gp, :].rearrange("p (k jj) -> p jj k", k=4),
                                       pS[:, bk, 0:NJ].rearrange("p (jj k) -> p jj k", k=4)) if hasattr(e, "activation")
                                else e.tensor_copy(Eall[:, ri, gp, :].rearrange("p (k jj) -> p jj k", k=4),
                                                   pS[:, bk, 0:NJ].rearrange("p (jj k) -> p jj k", k=4))))(ri, gp, bks[a4]),
                                reads=[], writes=allcls[ri] + bank_r[bks[a4]])

                it_g_ = [0]
                gate_mm_regs = [Reg()]

                def gate_half(half):
                    gs_, gs_r = ((g1s, g1s_r), (g2s, g2s_r))[half]
                    wsel = wg1 if half == 0 else None
                    for ci, (c0, cn) in enumerate(CHUNKS):
                        for j in range(8):
                            bk = 6 + it_g_[0] % 2
                            sb_ = stg[it_g_[0] % 2]
                            it_g_[0] += 1

                            def g1mm(e, j=j, c0=c0, cn=cn, bk=bk, half=half):
                                ins = None
                                for kk in range(8):
                                    wv = wg1[:, kk, j * 128:(j + 1) * 128] if half == 0 else wg2_v[:, kk, j * 128:(j + 1) * 128]
                                    ins = e.matmul(pS[:, bk, 0:cn], wv, hT[:, kk, c0:c0 + cn],
                                                   start=(kk == 0), stop=(kk == 7))
                                return ins
                            P.op("pe", g1mm, reads=(wg1_r if half == 0 else bbts_r) + gate_mm_regs, writes=bank_r[bk])
                            P.op("act", (lambda cn, bk, sb_: lambda e: e.activation(out=sb_[:, 0:cn], in_=pS[:, bk, 0:cn], func=AF.Sigmoid))(cn, bk, sb_),
                                 reads=[], writes=bank_r[bk] + [sb_.r])
                            P.op("sp", (lambda j, c0, cn, sb_, gs_: lambda e: e.dma_start(out=gs_[:, j, c0:c0 + cn], in_=sb_[:, 0:cn]))(j, c0, cn, sb_, gs_),
                                 reads=[sb_.r], writes=[gs_r[ci]], dma=True)

                for kk in range(8):
                    P.op("pool", (lambda kk: lambda e: e.dma_start(out=wg2_v[:, kk, :], in_=w_in[kk * 128:(kk + 1) * 128, 2048:3072]))(kk),
                         writes=bbts_r, dma=True)
                gate_half(0)
                gate_half(1)
                tk = [0]

                def cstepb(m, tgt, srcv, nel, treg, sreg):
                    mi = MI[m]
                    prods = ((0, 0, 0), (1, 2, 0), (0, 1, 1), (1, 0, 1))
                    for (sri, lc, tri) in prods:
                        if nel <= 16:
                            tb_ = (tmpS + tmpSs)[tk[0] % 4]
                        else:
                            tb_ = tmpS[tk[0] % 2]
                        tk[0] += 1
                        Sv = Eall[:, sri, :, srcv]
                        Tv = Eall[:, tri, :, tgt]
                        lv = lam[:, :, mi, lc].unsqueeze(2).to_broadcast([128, 16, nel])
                        tv = tb_[:, :, 0:nel]
                        if nel >= 16 and ENG_M == "pool":
                            GS = 10
                            Sv1, Sv2 = Eall[:, sri, 0:GS, srcv], Eall[:, sri, GS:16, srcv]
                            lv1 = lam[:, 0:GS, mi, lc].unsqueeze(2).to_broadcast([128, GS, nel])
                            lv2 = lam[:, GS:16, mi, lc].unsqueeze(2).to_broadcast([128, 16 - GS, nel])
                            tv1, tv2 = tb_[:, 0:GS, 0:nel], tb_[:, GS:16, 0:nel]
                            r1, r2 = Reg(), Reg()
                            P.op("pool", (lambda tv1, Sv1, lv1: lambda e: e.tensor_tensor(out=tv1, in0=Sv1, in1=lv1, op=ALU.mult))(tv1, Sv1, lv1),
                                 reads=sreg(sri) + [tb_.r], writes=[r1])
                            P.op("dve", (lambda tv2, Sv2, lv2: lambda e: e.tensor_tensor(out=tv2, in0=Sv2, in1=lv2, op=ALU.mult))(tv2, Sv2, lv2),
                                 reads=sreg(sri) + [tb_.r], writes=[r2])
                            P.op("dve", (lambda tv, Tv: lambda e: e.tensor_add(Tv, Tv, tv))(tv, Tv),
                                 reads=[r1, r2] + treg(tri), writes=treg(tri) + [tb_.r])
                            continue
                        P.op(ENG_M, (lambda tv, Sv, lv: lambda e: e.tensor_tensor(out=tv, in0=Sv, in1=lv, op=ALU.mult))(tv, Sv, lv),
                             reads=sreg(sri), writes=[tb_.r])
                        P.op("dve", (lambda tv, Tv: lambda e: e.tensor_add(Tv, Tv, tv))(tv, Tv),
                             reads=[tb_.r] + treg(tri), writes=treg(tri))

                QB = NJ // 4

                def sl(l, kk, j0, cnt):
                    if l == 0:
                        return slice(kk * QB + j0, kk * QB + j0 + cnt)
                    sg = 4 ** (l - 1)
                    first = 3 * QB + (sg - 1) + sg * (4 * j0 + kk)
                    return slice(first, first + 4 * sg * (cnt - 1) + 1, 4 * sg)

                def scan_level(l):
                    sig, off, n = LV2[l]
                    if n == 1:
                        return
                    nch = n // 4
                    for kk in (1, 2, 3):
                        tgt = sl(l, kk, 0, nch)
                        srcv = sl(l, kk - 1, 0, nch)
                        if kk < 3:
                            treg = (lambda kk: lambda ri: [cls[ri][(l, kk)]])(kk)
                        else:
                            treg = lambda ri: deeper(ri, l)
                        sreg = (lambda kk: lambda ri: [cls[ri][(l, kk - 1)]])(kk)
                        cstepb(TC * sig, tgt, srcv, nch, treg, sreg)
                    scan_level(l + 1)
                    if nch > 1:
                        for kk in (0, 1, 2):
                            tgt = sl(l, kk, 1, nch - 1)
                            srcv = sl(l, 3, 0, nch - 1)
                            treg = (lambda kk: lambda ri: [cls[ri][(l, kk)]])(kk)
                            sreg = lambda ri: deeper(ri, l)
                            cstepb(TC * sig * (kk + 1), tgt, srcv, nch - 1, treg, sreg)
                scan_level(0)

                P.op("pool", lambda e: e.memset(Sbf[:, :, :, 0:1], 0.0), writes=[Sbf.r])
                for ri in range(2):
                    P.op("act", (lambda ri: lambda e: e.copy(Hfin[:, ri, :], Eall[:, ri, :, NJ - 1]))(ri),
                         reads=allcls[ri], writes=[Hfin.r])
                    for k4 in range(4):
                        cnt = QB if k4 < 3 else QB - 1
                        P.op("act", (lambda ri, k4, cnt: lambda e: e.copy(
                            Sbf[:, ri, :, 1 + k4:1 + k4 + 4 * (cnt - 1) + 1:4], Eall[:, ri, :, k4 * QB:k4 * QB + cnt]))(ri, k4, cnt),
                            reads=allcls[ri], writes=[Sbf.r])
                for ri in range(2):
                    bk = ri
                    P.op("pe", (lambda ri, bk: lambda e: e.transpose(pS[0:16, bk, 0:128], Hfin[:, ri, :], ident_f[:, :]))(ri, bk),
                         reads=[Hfin.r], writes=bank_r[bk])
                    P.op("act", (lambda ri, bk: lambda e: e.copy(fin[0:16, ri, :], pS[0:16, bk, 0:128]))(ri, bk),
                         reads=bank_r[bk], writes=[fin.r])
                P.op("sp", lambda e: e.dma_start(out=o_re_p[:, :], in_=fin[0:16, 0, :]), reads=[fin.r], dma=True, out=True)
                P.op("sp", lambda e: e.dma_start(out=o_im_p[:, :], in_=fin[0:16, 1, :]), reads=[fin.r], dma=True, out=True)

                for gq in range(4):
                    bb = (gq % 2) * 4

                    def ymm2(e, gq=gq, bb=bb):
                        ins = None
                        for kk in range(TC):
                            ov = pS[:, bb + kk // 2, (kk % 2) * NJ:(kk % 2) * NJ + NJ]
                            for a4 in range(4):
                                gp = gq * 4 + a4
                                R0, R1 = a4 * 32, a4 * 32 + 32
                                ovq = pS[R0:R1, bb + kk // 2, (kk % 2) * NJ:(kk % 2) * NJ + NJ]
                                e.matmul(ovq, Wt[:, 0, kk, gp, :], Sbf[:, 0, gp, :], start=True, stop=False,
                                         tile_position=(0, R0))
                                e.matmul(ovq, Wt[:, 1, kk, gp, :], Sbf[:, 1, gp, :], start=False, stop=False,
                                         tile_position=(0, R0))
                            for s_ in range(kk + 1):
                                ins = e.matmul(ov, Ktf[:, gq, kk - s_, :], uD[:, gq, s_, :], start=False, stop=(s_ == kk))
                        return ins
                    wr = []
                    for b4 in range(4):
                        wr += bank_r[bb + b4]
                    P.op("pe", ymm2, reads=[Sbf.r, uD_r[gq], fin.r], writes=wr)
                    yo_v = YU.t[:, 4 + gq, 0:SEQ].rearrange("p (j k) -> p k j", k=TC)
                    pv_ = pS[:, bb:bb + 4, :].rearrange("p b (h j) -> p (b h) j", j=NJ)
                    P.op("act", (lambda yo_v, pv_: lambda e: e.copy(yo_v, pv_))(yo_v, pv_),
                         reads=[], writes=wr + [uTg_r[gq * 4 + a4] for a4 in range(4)] + [uTq_r[gq]])
                for gp in range(16):
                    gq, a4 = gp // 4, gp % 4
                    R0, R1 = a4 * 32, a4 * 32 + 32
                    bk = gp % 8

                    def ysm(e, gp=gp, bk=bk, R0=R0, R1=R1):
                        e.matmul(pS[R0:R1, bk, 0:NS], CT[:, 0, gp, :], Hnb[:, 0, gp, :], start=True, stop=False, tile_position=(0, R0))
                        return e.matmul(pS[R0:R1, bk, 0:NS], CT[:, 1, gp, :], Hnb[:, 1, gp, :], start=False, stop=True,
                                        tile_position=(0, R0))
                    P.op("pe", ysm, reads=[uTg_r[gp]], writes=[bank_r[bk][a4]])
                    P.op("dve", (lambda bk, gq=gq, R0=R0, R1=R1: lambda e: e.scalar_tensor_tensor(
                        out=uT[R0:R1, gq, SEQ:NCOL], in0=uT[R0:R1, gq, SEQ:NCOL], scalar=dcol[R0:R1, gq:gq + 1],
                        in1=pS[R0:R1, bk, 0:NS], op0=ALU.mult, op1=ALU.add))(bk),
                        reads=[bank_r[bk][a4], uTg_r[gp]], writes=[uTg_r[gp]])
                hTflat = hT.t[:].rearrange("p a b -> p (a b)")
                po_v = hTflat[:, 0:4096].rearrange("p (k c) -> p k c", c=1024)
                gl_v = hTflat[:, 4096:12288].rearrange("p (k c) -> p k c", c=2048)
                hT_all = [Reg()]
                for kk in range(4):
                    P.op("pool", (lambda kk: lambda e: e.dma_start(out=po_v[:, kk, :], in_=pool_out[kk * 128:(kk + 1) * 128, :]))(kk),
                         writes=hT_all + gate_mm_regs, dma=True)
                for kk in range(4):
                    P.op("pool", (lambda kk: lambda e: e.dma_start(out=gl_v[:, kk, :], in_=glu[kk * 128:(kk + 1) * 128, :]))(kk),
                         writes=hT_all + gate_mm_regs, dma=True)
                P.flush()
            sT.close()

            if dbg == "B":
                dbg_outs["d_YU"] = (YU, [128, 8 * NCOL], BF16)
                _emit_dbg(nc, P, dbg_outs, es)
                P.flush(final=True)
                return nc, list(dbg_outs.keys()) + ["o_pool_p", "o_re_p", "o_im_p", "o_pool_s", "o_re_s", "o_im_s"]
            mT = YU
            mT_r = [[(yp_r[j][ci] if j < 4 else uT_r[j - 4][ci]) for ci in range(len(CHUNKS))] for j in range(8)]
            with ExitStack() as sC:
                sg1 = [sb("sg1_%d" % i, [128, 8, 512], BF16, sC) for i in range(2)]
                sg2 = [sb("sg2_%d" % i, [128, 8, 512], BF16, sC) for i in range(2)]
                class _V:
                    def __init__(self, ap):
                        self.ap = ap

                    def __getitem__(self, k):
                        return self.ap[k]
                po = _V(po_v)
                gl = _V(gl_v)
                mtmp = [sb("mtmp%d" % i, [128, 8, 512], BF16, sC) for i in range(2)]
                sz2 = [sb("sz2%d" % i, [128, 512], BF16, sC) for i in range(4)]
                ta = [sb("ta%d" % i, [128, 512], F32, sC) for i in range(4)]
                tb = [sb("tb%d" % i, [128, 512], F32, sC) for i in range(4)]
                pC = [pst("pC%d" % i, [128, 512], F32, sC) for i in range(8)]
                pck = [0]

                def nextc():
                    t = pC[pck[0] % 8]
                    pck[0] += 1
                    return t
                wg_r = [Reg() for _ in range(8)]
                po_r = hT_all
                gl_r = hT_all
                it = 0

                def load_sg1(ci):
                    c0, cn = CHUNKS[ci]
                    t_ = sg1[ci % 2]
                    P.op("sp", (lambda c0, cn, t_: lambda e: e.dma_start(out=t_[:, :, 0:cn], in_=g1s[:, :, c0:c0 + cn]))(c0, cn, t_),
                         reads=[g1s_r[ci]], writes=[t_.r], dma=True)
                    t2_ = sg2[ci % 2]
                    P.op("sp", (lambda c0, cn, t2_: lambda e: e.dma_start(out=t2_[:, :, 0:cn], in_=g2s[:, :, c0:c0 + cn]))(c0, cn, t2_),
                         reads=[g2s_r[ci]], writes=[t2_.r], dma=True)
                load_sg1(0)
                for ci, (c0, cn) in enumerate(CHUNKS):
                    tiles = [ti for ti, (t0, tn) in enumerate(TTILES) if t0 >= c0 and t0 < c0 + cn]
                    hreg = [hT_r[ti] for ti in tiles]
                    mt = mtmp[ci % 2]
                    if ci + 1 < len(CHUNKS):
                        load_sg1(ci + 1)
                    sg_c = sg1[ci % 2]
                    sg2_c = sg2[ci % 2]
                    for j in range(8):
                        b = it % 4
                        it += 1

                        def grp(wt, nk, col0, rhsT, rsel, pt, c0=c0, cn=cn):
                            def f(e):
                                ins = None
                                for kk in range(nk):
                                    ins = e.matmul(pt[:, 0:cn], wt[:, kk, col0:col0 + 128], rhsT[:, rsel + kk, c0:c0 + cn],
                                                   start=(kk == 0), stop=(kk == nk - 1))
                                return ins
                            return f
                        p3, p4, p5 = nextc(), nextc(), nextc()
                        P.op("pe", grp(po, 4, j * 128, YU, 0, p3), reads=[yp_r[kk][ci] for kk in range(4)] + po_r, writes=[p3.r])
                        P.op("pe", grp(gl, 4, j * 128, YU, 4, p4), reads=[uT_r[kk][ci] for kk in range(4)] + gl_r, writes=[p4.r])
                        P.op("pe", grp(gl, 4, 1024 + j * 128, YU, 4, p5), reads=[uT_r[kk][ci] for kk in range(4)] + gl_r, writes=[p5.r])
                        P.op("act", (lambda b, cn, p5: lambda e: e.activation(out=sz2[b][:, 0:cn], in_=p5[:, 0:cn], func=AF.Sigmoid))(b, cn, p5),
                             reads=[p5.r], writes=[sz2[b].r])
                        P.op("dve", (lambda b, cn, p3, j, sg_c: lambda e: e.tensor_mul(ta[b][:, 0:cn], sg_c[:, j, 0:cn], p3[:, 0:cn]))(b, cn, p3, j, sg_c),
                             reads=[sg_c.r, p3.r], writes=[ta[b].r, p3.r])
                        P.op("dve", (lambda b, cn, p4: lambda e: e.tensor_mul(tb[b][:, 0:cn], sz2[b][:, 0:cn], p4[:, 0:cn]))(b, cn, p4),
                             reads=[sz2[b].r, p4.r], writes=[tb[b].r])
                        P.op("pool", (lambda b, cn, j, sg2_c: lambda e: e.tensor_mul(tb[b][:, 0:cn], tb[b][:, 0:cn], sg2_c[:, j, 0:cn]))(b, cn, j, sg2_c),
                             reads=[tb[b].r, sg2_c.r], writes=[tb[b].r])
                        P.op("dve", (lambda b, cn, j, mt: lambda e: e.tensor_add(mt[:, j, 0:cn], ta[b][:, 0:cn], tb[b][:, 0:cn]))(b, cn, j, mt),
                             reads=[ta[b].r, tb[b].r], writes=[mt.r])
                    P.op("dve", (lambda c0, cn, mt: lambda e: e.tensor_copy(YU[:, :, c0:c0 + cn], mt[:, :, 0:cn]))(c0, cn, mt),
                         reads=[mt.r], writes=[mT_r[j][ci] for j in range(8)])
                P.flush()

            if dbg == "C":
                dbg_outs["d_mT"] = (YU, [128, 8 * NCOL], BF16)
                _emit_dbg(nc, P, dbg_outs, es)
                P.flush(final=True)
                return nc, list(dbg_outs.keys())

            with ExitStack() as sDE:
                x1 = sb("x1", [128, 17, D], F32, sDE)
                wu0 = sb("wu0", [128, 8, 512], BF16, sDE)
                wd0 = sb("wd0", [128, 4, D], BF16, sDE)
                x1_r = [Reg() for _ in TTILES]
                h2_r = [Reg() for _ in TTILES]
                with ExitStack() as sD:
                    wo = sb("wo", [128, 8, D], BF16, sD)
                    g2b = sb("g2b", [128, D], F32, sD)
                    xt = [sb("xtd%d" % i, [128, D], F32, sD) for i in range(2)]
                    sq = [sb("sqd%d" % i, [128, D], BF16, sD) for i in range(2)]
                    hb = [sb("hbd%d" % i, [128, D], BF16, sD) for i in range(2)]
                    ssq = [sb("ssqd%d" % i, [128, 1], F32, sD) for i in range(2)]
                    rstd = [sb("rstdd%d" % i, [128, 1], F32, sD) for i in range(2)]
                    ptr = [pst("ptrd%d" % i, [128, 8, 128], BF16, sD) for i in range(2)]
                    pD = [pst("pD%d" % i, [128, 512], F32, sD) for i in range(4)]
                    wo_r = [Reg() for _ in range(8)]
                    for kk in range(8):
                        P.op("pool", (lambda kk: lambda e: e.dma_start(out=wo[:, kk, :], in_=w_out[kk * 128:(kk + 1) * 128, :]))(kk),
                             writes=[wo_r[kk]], dma=True)
                    P.op("sp", lambda e: e.dma_start(out=g2b[:, :], in_=n2g.partition_broadcast(128)), writes=[g2b.r], dma=True)
                    pend_d = []
                    P.op("pool", lambda e: e.dma_start(
                        out=wu0[:, :, :], in_=dram_ap(w_up, 0, [[DFF, 128], [128 * DFF, 8], [1, 512]])), writes=[wu0.r], dma=True)
                    P.op("pool", lambda e: e.dma_start(
                        out=wd0[:, :, :], in_=dram_ap(w_down, 0, [[D, 128], [128 * D, 4], [1, D]])), writes=[wd0.r], dma=True)
                    for ti, (t0, tn) in enumerate(TTILES):
                        b = ti % 2
                        ci = min(ti // 4, 4)
                        src = x[t0:t0 + tn, :] if ti < 16 else xs[:, :]
                        P.op("sp", (lambda b, tn, src: lambda e: e.dma_start(out=xt[b][0:tn, :], in_=src))(b, tn, src),
                             writes=[xt[b].r], dma=True)
                        while len(pend_d) > 1:
                            pend_d.pop(0)()
                        for hf in range(2):
                            pt = pD[(2 * ti + hf) % 4]

                            def mo(e, t0=t0, tn=tn, hf=hf, pt=pt):
                                ins = None
                                for kk in range(8):
                                    ins = e.matmul(pt[0:tn, :], YU[:, kk, t0:t0 + tn], wo[:, kk, hf * 512:(hf + 1) * 512],
                                                   start=(kk == 0), stop=(kk == 7))
                                return ins
                            P.op("pe", mo, reads=[mT_r[j][ci] for j in range(8)] + wo_r, writes=[pt.r])
                            P.op("dve", (lambda b, tn, hf, pt, ti: lambda e: e.tensor_add(
                                x1[0:tn, ti, hf * 512:(hf + 1) * 512], pt[0:tn, :], xt[b][0:tn, hf * 512:(hf + 1) * 512]))(b, tn, hf, pt, ti),
                                reads=[pt.r, xt[b].r], writes=[x1_r[ti], pt.r])
                        while pend_d:
                            pend_d.pop(0)()
                        P.op("act", (lambda b, tn, ti: lambda e: e.activation(
                            out=sq[b][0:tn, :], in_=x1[0:tn, ti, :], func=AF.Square, accum_out=ssq[b][0:tn, :]))(b, tn, ti),
                            reads=[x1_r[ti]], writes=[sq[b].r, ssq[b].r])
                        P.op("dve", (lambda b, tn: lambda e: e.tensor_scalar(
                            rstd[b][0:tn, :], ssq[b][0:tn, :], 1.0 / D, EPS, ALU.mult, ALU.add))(b, tn),
                            reads=[ssq[b].r], writes=[rstd[b].r])
                        P.op("act", (lambda b, tn: lambda e: e.activation(
                            out=rstd[b][0:tn, :], in_=rstd[b][0:tn, :], func=AF.Sqrt))(b, tn),
                            reads=[rstd[b].r], writes=[rstd[b].r])
                        P.op("dve", (lambda b, tn: lambda e: e.reciprocal(rstd[b][0:tn, :], rstd[b][0:tn, :]))(b, tn),
                             reads=[rstd[b].r], writes=[rstd[b].r])
                        P.op("dve", (lambda b, tn, ti: lambda e: e.scalar_tensor_tensor(
                            out=hb[b][0:tn, :], in0=x1[0:tn, ti, :], scalar=rstd[b][0:tn, 0:1], in1=g2b[0:tn, :],
                            op0=ALU.mult, op1=ALU.mult))(b, tn, ti),
                            reads=[x1_r[ti], rstd[b].r, g2b.r], writes=[hb[b].r])

                        def tr2(e, b=b, tn=tn):
                            ins = None
                            for kk in range(8):
                                ins = e.transpose(ptr[b][:, kk, 0:tn], hb[b][0:tn, kk * 128:(kk + 1) * 128], ident_b[0:tn, 0:tn])
                            return ins
                        def late(tr2=tr2, b=b, t0=t0, tn=tn, ti=ti):
                            P.op("pe", tr2, reads=[hb[b].r], writes=[ptr[b].r])
                            P.op("act", (lambda b, t0, tn: lambda e: e.copy(hT[:, :, t0:t0 + tn], ptr[b][:, :, 0:tn]))(b, t0, tn),
                                 reads=[ptr[b].r], writes=[h2_r[ti], ptr[b].r])
                        pend_d.append(late)
                    for f_ in pend_d:
                        f_()
                    P.flush()

                if dbg == "D":
                    dbg_outs["d_x1"] = (x1, [128, 17 * D], F32)
                    dbg_outs["d_h2T"] = (hT, [128, 8 * NCOL], BF16)
                    _emit_dbg(nc, P, dbg_outs, es)
                    P.flush(final=True)
                    return nc, list(dbg_outs.keys())

                with ExitStack() as sE:
                    NE = 8
                    wu = [wu0, sb("wu1", [128, 8, 512], BF16, sE)]
                    wd = [wd0, sb("wd1", [128, 4, D], BF16, sE)]
                    gfb = sb("gfb", [128, D], F32, sE)
                    rt = [sb("rt%d" % i, [128, 512], BF16, sE) for i in range(2)]
                    aT = [sb("aT%d" % i, [128, 4, 512], BF16, sE) for i in range(2)]
                    aT_r = [[Reg() for _ in range(4)] for _ in range(2)]
                    yo = [sb("yo%d" % i, [128, D], F32, sE) for i in range(2)]
                    sq = [sb("sqe%d" % i, [128, D], BF16, sE) for i in range(2)]
                    ssq = [sb("ssqe%d" % i, [128, 1], F32, sE) for i in range(2)]
                    rstd = [sb("rstde%d" % i, [128, 1], F32, sE) for i in range(2)]
                    pE = [pst("pE%d" % i, [128, 512], F32, sE) for i in range(8)]
                    pek = [0]

                    def nexte():
                        t = pE[pek[0] % 8]
                        pek[0] += 1
                        return t
                    P.op("sp", lambda e: e.dma_start(out=gfb[:, :], in_=nfg.partition_broadcast(128)), writes=[gfb.r], dma=True)
                    def final_norm(ti):
                        t0, tn = TTILES[ti]
                        b = ti % 2
                        P.op("act", (lambda b, tn, ti: lambda e: e.activation(
                            out=sq[b][0:tn, :], in_=x1[0:tn, ti, :], func=AF.Square, accum_out=ssq[b][0:tn, :]))(b, tn, ti),
                            reads=[x1_r[ti]], writes=[sq[b].r, ssq[b].r])
                        P.op("dve", (lambda b, tn: lambda e: e.tensor_scalar(
                            rstd[b][0:tn, :], ssq[b][0:tn, :], 1.0 / D, EPS, ALU.mult, ALU.add))(b, tn),
                            reads=[ssq[b].r], writes=[rstd[b].r])
                        P.op("act", (lambda b, tn: lambda e: e.activation(
                            out=rstd[b][0:tn, :], in_=rstd[b][0:tn, :], func=AF.Sqrt))(b, tn),
                            reads=[rstd[b].r], writes=[rstd[b].r])
                        P.op("dve", (lambda b, tn: lambda e: e.reciprocal(rstd[b][0:tn, :], rstd[b][0:tn, :]))(b, tn),
                             reads=[rstd[b].r], writes=[rstd[b].r])
                        P.op("dve", (lambda b, tn, ti: lambda e: e.scalar_tensor_tensor(
                            out=yo[b][0:tn, :], in0=x1[0:tn, ti, :], scalar=rstd[b][0:tn, 0:1], in1=gfb[0:tn, :],
                            op0=ALU.mult, op1=ALU.mult))(b, tn, ti),
                            reads=[x1_r[ti], rstd[b].r, gfb.r], writes=[yo[b].r])
                        dst = y[t0:t0 + tn, :] if ti < 16 else ysamp[:, :]
                        P.op("sp", (lambda b, tn, dst: lambda e: e.dma_start(out=dst, in_=yo[b][0:tn, :]))(b, tn, dst),
                             reads=[yo[b].r], dma=True, out=True)
                    rkk = [0]
                    pend_fn = []

                    def prefetch(e8):
                        if e8 + 1 < NE:
                            nb_ = (e8 + 1) % 2
                            P.op("pool", (lambda nb_, e8: lambda e: e.dma_start(
                                out=wu[nb_][:, :, :], in_=dram_ap(w_up, (e8 + 1) * 512, [[DFF, 128], [128 * DFF, 8], [1, 512]])))(nb_, e8),
                                writes=[wu[nb_].r], dma=True)
                            P.op("pool", (lambda nb_, e8: lambda e: e.dma_start(
                                out=wd[nb_][:, :, :], in_=dram_ap(w_down, (e8 + 1) * 512 * D, [[D, 128], [128 * D, 4], [1, D]])))(nb_, e8),
                                writes=[wd[nb_].r], dma=True)

                    def emit_up(e8, ci, cb):
                        b = e8 % 2
                        c0, cn = CHUNKS[ci]
                        tiles = [ti for ti, (t0, tn) in enumerate(TTILES) if t0 >= c0 and t0 < c0 + cn]
                        for m in range(4):
                            pt = nexte()

                            def up(e, m=m, c0=c0, cn=cn, pt=pt, b=b):
                                ins = None
                                for kk in range(8):
                                    ins = e.matmul(pt[:, 0:cn], wu[b][:, kk, m * 128:(m + 1) * 128], hT[:, kk, c0:c0 + cn],
                                                   start=(kk == 0), stop=(kk == 7))
                                return ins
                            P.op("pe", up, reads=[h2_r[ti] for ti in tiles] + [wu[b].r], writes=[pt.r])
                            rb = rkk[0] % 2
                            rkk[0] += 1
                            P.op("act", (lambda rb, cn, pt: lambda e: e.activation(out=rt[rb][:, 0:cn], in_=pt[:, 0:cn], func=AF.Relu))(rb, cn, pt),
                                 reads=[pt.r], writes=[rt[rb].r, pt.r])
                            P.op("pool", (lambda rb, cn, cb, m: lambda e: e.tensor_mul(
                                aT[cb][:, m, 0:cn], rt[rb][:, 0:cn], rt[rb][:, 0:cn]))(rb, cn, cb, m),
                                reads=[rt[rb].r], writes=[aT_r[cb][m]])

                    def emit_down(e8, ci, cb):
                        b = e8 % 2
                        c0, cn = CHUNKS[ci]
                        tiles = [ti for ti, (t0, tn) in enumerate(TTILES) if t0 >= c0 and t0 < c0 + cn]
                        for ti in tiles:
                            t0, tn = TTILES[ti]
                            for hf in range(2):
                                pt = nexte()

                                def dn(e, t0=t0, tn=tn, c0=c0, hf=hf, pt=pt, b=b, cb=cb):
                                    ins = None
                                    for kk in range(4):
                                        ins = e.matmul(pt[0:tn, :], aT[cb][:, kk, t0 - c0:t0 - c0 + tn], wd[b][:, kk, hf * 512:(hf + 1) * 512],
                                                       start=(kk == 0), stop=(kk == 3))
                                    return ins
                                P.op("pe", dn, reads=aT_r[cb] + [wd[b].r], writes=[pt.r])
                                P.op("dve", (lambda tn, ti, hf, pt: lambda e: e.tensor_add(
                                    x1[0:tn, ti, hf * 512:(hf + 1) * 512], x1[0:tn, ti, hf * 512:(hf + 1) * 512], pt[0:tn, :]))(tn, ti, hf, pt),
                                    reads=[pt.r, x1_r[ti]], writes=[x1_r[ti], pt.r])
                            if e8 == NE - 1:
                                pend_fn.append(ti)
                                while len(pend_fn) > 2:
                                    final_norm(pend_fn.pop(0))

                    items = [(e8, ci) for e8 in range(NE) for ci in range(len(CHUNKS))]
                    emit_up(items[0][0], items[0][1], 0)
                    for i_, (e8, ci) in enumerate(items):
                        if ci == 0:
                            prefetch(e8)
                        if i_ + 1 < len(items):
                            emit_up(items[i_ + 1][0], items[i_ + 1][1], (i_ + 1) % 2)
                        emit_down(e8, ci, i_ % 2)
                    while pend_fn:
                        final_norm(pend_fn.pop(0))
                    P.flush(final=True)
    return nc, []


def _emit_dbg(nc, P, dbg_outs, es):
    for nm, (t, shp, dt) in dbg_outs.items():
        d = nc.dram_tensor(nm, list(shp), dt, kind="ExternalOutput").ap()
        nd = len(t.t.shape)
        flat = {2: lambda: t.t[:], 3: lambda: t.t[:].rearrange("p a b -> p (a b)"),
                4: lambda: t.t[:].rearrange("p a b c -> p (a b c)"),
                5: lambda: t.t[:].rearrange("p a b c d -> p (a b c d)")}[nd]()
        P.op("sp", (lambda d, flat: lambda e: e.dma_start(out=d[:, :], in_=flat))(d, flat),
             reads=[t.r], writes=[], dma=True, out=True)


def make_in_maps(inp):
    f = lambda a: np.ascontiguousarray(np.asarray(a, dtype=np.float32))
    shared = {
        "n1g": f(inp["norm1_g"]).reshape(1, D),
        "w_in": f(inp["w_in"]).reshape(D, 3072),
        "pool_w": f(inp["pool_w"]).reshape(512, 128),
        "pscale": f(inp["pool_scale"]).reshape(1, PW),
        "pool_out": f(inp["pool_out"]).reshape(PW, D),
        "a_re": f(inp["ssm_a_re"]).reshape(1, 2048),
        "a_im": f(inp["ssm_a_im"]).reshape(1, 2048),
        "log_dt": f(inp["ssm_log_dt"]).reshape(1, 32),
        "b_re": f(inp["ssm_b_re"]).reshape(1, -1),
        "b_im": f(inp["ssm_b_im"]).reshape(1, -1),
        "c_re": f(inp["ssm_c_re"]).reshape(512, 64),
        "c_im": f(inp["ssm_c_im"]).reshape(512, 64),
        "ssm_d": f(inp["ssm_d"]).reshape(1, 512),
        "glu": f(inp["ssm_glu"]).reshape(512, 2048),
        "w_out": f(inp["w_out"]).reshape(D, D),
        "n2g": f(inp["norm2_g"]).reshape(1, D),
        "w_up": f(inp["w_up"]).reshape(D, DFF),
        "w_down": f(inp["w_down"]).reshape(DFF, D),
        "nfg": f(inp["normf_g"]).reshape(1, D),
    }
    xp = f(inp["x_prompt"])
    xsm = f(inp["x_sample"]).reshape(128, D)
    cp = f(inp["cache_pool"]).reshape(128, PB, PW)
    sr = f(inp["state_ssm_re"]).reshape(128, 2048)
    si = f(inp["state_ssm_im"]).reshape(128, 2048)
    maps = []
    for c in range(NCORES):
        m = dict(shared)
        m["x"] = xp[c]
        m["xs"] = xsm[c * NS:(c + 1) * NS]
        m["cpool"] = cp[c * NS:(c + 1) * NS].reshape(NS * PB, PW)
        m["sre"] = sr[c * NS:(c + 1) * NS]
        m["sim"] = si[c * NS:(c + 1) * NS]
        maps.append(m)
    return maps


def kernel(**inp):
    nc, _ = build()
    maps = make_in_maps(inp)
    res = run_bass_kernel_spmd(nc, maps, core_ids=list(range(NCORES)))
    R = res.results
    y_prompt = np.stack([R[c]["y"] for c in range(NCORES)], 0).astype(np.float32)
    y_sample = np.concatenate([R[c]["ysamp"] for c in range(NCORES)], 0).reshape(128, 1, D).astype(np.float32)
    pool_p = np.stack([R[c]["o_pool_p"] for c in range(NCORES)], 0).reshape(1, 8, PB, PW).astype(np.float32)
    re_p = np.stack([R[c]["o_re_p"].reshape(32, 64) for c in range(NCORES)], 0).reshape(1, 8, 32, 64).astype(np.float32)
    im_p = np.stack([R[c]["o_im_p"].reshape(32, 64) for c in range(NCORES)], 0).reshape(1, 8, 32, 64).astype(np.float32)
    pool_s = np.concatenate([R[c]["o_pool_s"].reshape(NS, PB, PW) for c in range(NCORES)], 0).reshape(1, 128, PB, PW).astype(np.float32)
    re_s = np.concatenate([R[c]["o_re_s"].reshape(NS, 32, 64) for c in range(NCORES)], 0).reshape(1, 128, 32, 64).astype(np.float32)
    im_s = np.concatenate([R[c]["o_im_s"].reshape(NS, 32, 64) for c in range(NCORES)], 0).reshape(1, 128, 32, 64).astype(np.float32)
    return (y_prompt, y_sample, pool_p, re_p, im_p, pool_s, re_s, im_s)
```

```python
import math
SKIP = ''
ENG_M = 'pool'
from contextlib import ExitStack

import numpy as np
import concourse.bass as bass
import concourse.mybir as mybir
from concourse.bass_utils import run_bass_kernel_spmd

F32 = mybir.dt.float32
BF16 = mybir.dt.bfloat16
ALU = mybir.AluOpType
AF = mybir.ActivationFunctionType
AX = mybir.AxisListType

NCORES = 8
D = 1024
SEQ = 2048
NS = 16
NCOL = SEQ + NS
PB = 15
PW = 512
DFF = 4096
EPS = 1e-6
PAD = 16
CHUNKS = [(0, 512), (512, 512), (1024, 512), (1536, 512), (2048, NS)]
TTILES = [(i * 128, 128) for i in range(16)] + [(SEQ, NS)]
MLIST = [1, 2, 3, 4, 5, 6, 7, 8, 16, 24, 32, 64, 96, 128, 256, 384, 512, 1024, 1536]
NM = len(MLIST)
MI = {m: i for i, m in enumerate(MLIST)}
TC = 8
NJ = SEQ // TC
TWO_PI = 2.0 * math.pi


class Reg:
    __slots__ = ("w", "rs")

    def __init__(self):
        self.w = None
        self.rs = []


class Op:
    __slots__ = ("eng", "fn", "deps", "dma", "sig", "idx", "sem", "val", "phase")


class Prog:
    ND = 8

    def __init__(self, nc, es):
        self.nc = nc
        self.es = es
        self.ops = []
        self.phase = 0
        self.esem = {}
        self.ecount = {}
        for e in ("pe", "act", "dve", "pool", "sp"):
            self.esem[e] = es.enter_context(nc.semaphore("s_" + e))
            self.ecount[e] = 0
        self.dsem = {}
        self.dcount = {}
        for q in ("sp", "pool", "act"):
            self.dsem[q] = [es.enter_context(nc.semaphore("d_%s%d" % (q, i))) for i in range(self.ND)]
            self.dcount[q] = 0
        self.waited = {e: {} for e in ("pe", "act", "dve", "pool", "sp")}
        self.out_dmas = []

    def op(self, eng, fn, reads=(), writes=(), dma=False, out=False):
        o = Op()
        o.eng = eng
        o.fn = fn
        o.dma = dma
        o.sig = False
        o.idx = 0
        o.sem = None
        o.val = 0
        o.phase = self.phase
        deps = []
        for r in reads:
            if r.w is not None:
                deps.append(r.w)
        for w in writes:
            if w.w is not None:
                deps.append(w.w)
            deps.extend(w.rs)
        for r in reads:
            r.rs.append(o)
        for w in writes:
            w.w = o
            w.rs = []
        seen = set()
        o.deps = []
        for d in deps:
            if d is o or id(d) in seen:
                continue
            seen.add(id(d))
            o.deps.append(d)
        self.ops.append(o)
        if out:
            self.out_dmas.append(o)
        return o

    def interleave(self, n_head, n0):
        ops = self.ops
        head, a, b = ops[:n_head], ops[n_head:n0], ops[n0:]
        out = list(head)
        ia = 0
        for ib, o in enumerate(b):
            tgt = (ib + 1) * len(a) // max(len(b), 1)
            while ia < tgt:
                out.append(a[ia])
                ia += 1
            out.append(o)
        out.extend(a[ia:])
        self.ops = out

    def flush(self, final=False):
        nc = self.nc
        ops = self.ops
        self.ops = []
        if True:
            fence = Op()
            fence.eng = "sp"
            fence.fn = None
            fence.dma = False
            fence.sig = False
            fence.idx = 0
            fence.sem = None
            fence.val = 0
            fence.phase = self.phase
            fence.deps = [o for o in ops if o.dma] + (list(self.out_dmas) if final else [])
            ops.append(fence)
        for o in ops:
            o.deps = [d for d in o.deps if d.dma or d.phase == self.phase]
        for o in ops:
            for d in o.deps:
                if d.dma:
                    continue
                if d.eng == "pe" and o.eng == "pe":
                    continue
                d.sig = True
        for o in ops:
            if o.fn is None:
                continue
            if o.dma:
                q = o.eng
                j = self.dcount[q]
                self.dcount[q] = j + 1
                o.sem = self.dsem[q][j % self.ND]
                o.val = 16 * (j // self.ND + 1)
            elif o.sig:
                self.ecount[o.eng] += 1
                o.idx = self.ecount[o.eng]
        per = {e: [o for o in ops if o.eng == e] for e in ("pe", "act", "dve", "pool", "sp")}

        def emit(engname, eng):
            wd = self.waited[engname]
            for o in per[engname]:
                need = {}
                for d in o.deps:
                    if d.dma:
                        s, v = d.sem, d.val
                    else:
                        if d.eng == "pe" and o.eng == "pe":
                            continue
                        s, v = self.esem[d.eng], d.idx
                    key = id(s)
                    if key not in need or need[key][1] < v:
                        need[key] = (s, v)
                if o.dma and o.val > 16:
                    key = id(o.sem)
                    v = o.val - 16
                    if key not in need or need[key][1] < v:
                        need[key] = (o.sem, v)
                for key, (s, v) in need.items():
                    if wd.get(key, 0) >= v:
                        continue
                    eng.wait_ge(s, v)
                    wd[key] = v
                if o.fn is None:
                    continue
                ins = o.fn(eng)
                if o.dma:
                    ins.then_inc(o.sem, 16)
                elif o.sig:
                    ins.then_inc(self.esem[engname], 1)

        with nc.Block() as block:
            if per["pe"]:
                @block.tensor
                def _(eng):
                    emit("pe", eng)
            if per["act"]:
                @block.scalar
                def _(eng):
                    emit("act", eng)
            if per["dve"]:
                @block.vector
                def _(eng):
                    emit("dve", eng)
            if per["pool"]:
                @block.gpsimd
                def _(eng):
                    emit("pool", eng)
            if per["sp"]:
                @block.sync
                def _(eng):
                    emit("sp", eng)
        self.phase += 1


class Sub:
    def __init__(self, tl, lo):
        self.tl = tl
        self.lo = lo

    def __getitem__(self, k):
        p, q, c = k
        return self.tl.t[p, self.lo + q, c]


class Tl:
    def __init__(self, t):
        self.t = t
        self.r = Reg()

    def __getitem__(self, k):
        return self.t[k]


def build(dbg=None):
    nc = bass.Bass("TRN2", target_bir_lowering=False)

    def din(name, shape):
        return nc.dram_tensor(name, list(shape), F32, kind="ExternalInput").ap()

    def dout(name, shape):
        return nc.dram_tensor(name, list(shape), F32, kind="ExternalOutput").ap()

    x = din("x", [SEQ, D])
    xs = din("xs", [NS, D])
    cpool = din("cpool", [NS * PB, PW])
    sre = din("sre", [NS, 2048])
    sim = din("sim", [NS, 2048])
    n1g = din("n1g", [1, D])
    w_in = din("w_in", [D, 3072])
    pool_w = din("pool_w", [512, 128])
    pscale = din("pscale", [1, PW])
    pool_out = din("pool_out", [PW, D])
    a_re = din("a_re", [1, 2048])
    a_im = din("a_im", [1, 2048])
    log_dt = din("log_dt", [1, 32])
    b_re = din("b_re", [1, 32 * 64 * 16])
    b_im = din("b_im", [1, 32 * 64 * 16])
    c_re = din("c_re", [512, 64])
    c_im = din("c_im", [512, 64])
    ssm_d = din("ssm_d", [1, 512])
    glu = din("glu", [512, 2048])
    w_out = din("w_out", [D, D])
    n2g = din("n2g", [1, D])
    w_up = din("w_up", [D, DFF])
    w_down = din("w_down", [DFF, D])
    nfg = din("nfg", [1, D])

    y = dout("y", [SEQ, D])
    ysamp = dout("ysamp", [NS, D])
    o_pool_p = dout("o_pool_p", [PB, PW])
    o_re_p = dout("o_re_p", [16, 128])
    o_im_p = dout("o_im_p", [16, 128])
    o_pool_s = dout("o_pool_s", [NS * PB, PW])
    o_re_s = dout("o_re_s", [NS, 2048])
    o_im_s = dout("o_im_s", [NS, 2048])

    g1s = nc.dram_tensor("g1s", [128, 8, NCOL], BF16, kind="Internal").ap()
    g1s_r = [Reg() for _ in CHUNKS]
    g2s = nc.dram_tensor("g2s", [128, 8, NCOL], BF16, kind="Internal").ap()
    g2s_r = [Reg() for _ in CHUNKS]
    dbg_outs = {}

    with ExitStack() as es:
        P = Prog(nc, es)

        def sb(name, shape, dt, stack=None):
            return Tl((stack or es).enter_context(nc.sbuf_tensor(name, list(shape), dt)))

        def pst(name, shape, dt, stack=None):
            return Tl((stack or es).enter_context(nc.psum_tensor(name, list(shape), dt)))

        def dram_ap(base, offset, pattern):
            return bass.AP(base.tensor, offset, pattern)

        ident_f = sb("ident_f", [128, 128], F32)
        ident_b = sb("ident_b", [128, 128], BF16)
        dcol = sb("dcol", [128, 4], F32)
        pscol = sb("pscol", [128, 4], F32)
        rct = sb("rct", [128, 16], F32)
        hT = sb("hT", [128, 8, NCOL], BF16)
        YU = sb("YU", [128, 8, NCOL], BF16)
        sT = es.enter_context(ExitStack())
        lam = sb("lam", [128, 16, NM, 3], F32, sT)
        BbTs = sb("BbTs", [128, 2, 4, TC, 128], BF16, sT)
        CT = sb("CT", [128, 2, 16, 32], BF16, sT)
        Wt = sb("Wt", [128, 2, TC, 16, 32], BF16, sT)
        Kt = sb("Kt", [128, 4, TC, 32], BF16, sT)
        Hnb = sb("Hnb", [128, 2, 16, 16], BF16, sT)
        Ktf = sb("Ktf", [128, 4, TC, 128], BF16, sT)

        sW = es.enter_context(ExitStack())
        wvu = sb("wvu", [128, 8, 1024], BF16, sW)
        s0 = es.enter_context(ExitStack())
        if True:
            ar = sb("ar", [128, 16], F32, s0)
            ai = sb("ai", [128, 16], F32, s0)
            ldt = sb("ldt", [128, 16], F32, s0)
            dtt = sb("dtt", [128, 16], F32, s0)
            adt = sb("adt", [128, 16], F32, s0)
            wdt = sb("wdt", [128, 16], F32, s0)
            MT = sb("MT", [128, 16, NM], F32, s0)
            ARG = sb("ARG", [128, 16, NM], F32, s0)
            EXA = sb("EXA", [128, 16, NM], F32, s0)
            MAG = sb("MAG", [128, 16, NM], F32, s0)
            SA = sb("SA", [128, 16, NM], F32, s0)
            CA = sb("CA", [128, 16, NM], F32, s0)
            SN = sb("SN", [128, 16, NM], F32, s0)
            CS = sb("CS", [128, 16, NM], F32, s0)
            negpi = sb("negpi", [128, 1], F32, s0)
            t1 = sb("t1", [128, 16], F32, s0)
            t2 = sb("t2", [128, 16], F32, s0)
            t3 = sb("t3", [128, 16], F32, s0)
            t4 = sb("t4", [128, 16], F32, s0)
            cre = sb("cre", [128, 16], F32, s0)
            cim = sb("cim", [128, 16], F32, s0)
            Bn_re = sb("Bn_re", [128, 16, 16], F32, s0)
            Bn_im = sb("Bn_im", [128, 16, 16], F32, s0)
            Bb_re = sb("Bb_re", [128, 16, 16], F32, s0)
            Bb_im = sb("Bb_im", [128, 16, 16], F32, s0)
            Bt1 = sb("Bt1", [128, 16, 16], F32, s0)
            Bt2 = sb("Bt2", [128, 16, 16], F32, s0)
            Bpad = sb("Bpad", [128, 2, 16, 32], F32, s0)
            Cin = [sb("Cin%d" % i, [128, 4, 128], F32, s0) for i in range(2)]
            iot = sb("iot", [128, 16], F32, s0)
            ps0 = [pst("ps0_%d" % i, [128, 512], F32, s0) for i in range(4)]

            P.op("pool", lambda e: e.memset(ident_f[:, :], 0.0), writes=[ident_f.r])
            P.op("pool", lambda e: e.affine_select(
                out=ident_f[:, :], in_=ident_f[:, :], pattern=[[-1, 128]], compare_op=ALU.not_equal,
                fill=1.0, base=0, channel_multiplier=1), reads=[ident_f.r], writes=[ident_f.r])
            P.op("dve", lambda e: e.tensor_copy(ident_b[:, :], ident_f[:, :]), reads=[ident_f.r], writes=[ident_b.r])
            P.op("pool", lambda e: e.iota(iot[:, :], [[1, 16]], base=1, channel_multiplier=0,
                                          allow_small_or_imprecise_dtypes=True), writes=[iot.r])
            P.op("dve", lambda e: e.reciprocal(rct[:, :], iot[:, :]), reads=[iot.r], writes=[rct.r])
            P.op("dve", lambda e: e.memset(negpi[:, :], -math.pi), writes=[negpi.r])

            def bcast_row(dst, src_row, n):
                return lambda e: e.dma_start(out=dst[:, :], in_=src_row.partition_broadcast(128))

            def small_t(dst, src):
                def f(e):
                    with nc.allow_non_contiguous_dma(reason="tiny param load"):
                        return e.dma_start(out=dst, in_=src)
                return f
            prow = sb("prow", [16, 4, 128], F32, s0)
            Lb = sb("Lb", [128, 32], F32, s0)
            P.op("pool", lambda e: e.memset(prow[:, :, :], 0.0), writes=[prow.r])
            P.op("sp", lambda e: e.dma_start(out=prow[0:16, 0, :], in_=dram_ap(a_re, 0, [[128, 16], [1, 128]])), writes=[prow.r], dma=True)
            P.op("sp", lambda e: e.dma_start(out=prow[0:16, 1, :], in_=dram_ap(a_im, 0, [[128, 16], [1, 128]])), writes=[prow.r], dma=True)
            P.op("sp", lambda e: e.dma_start(out=prow[0:4, 2, :], in_=dram_ap(ssm_d, 0, [[128, 4], [1, 128]])), writes=[prow.r], dma=True)
            P.op("sp", lambda e: e.dma_start(out=prow[0:4, 3, :], in_=dram_ap(pscale, 0, [[128, 4], [1, 128]])), writes=[prow.r], dma=True)
            P.op("sp", lambda e: e.dma_start(out=Lb[:, :], in_=log_dt.partition_broadcast(128)), writes=[Lb.r], dma=True)
            ptp = ps0[3]

            def ptr_params(e):
                ins = None
                for w_ in range(4):
                    ins = e.transpose(ptp[:, w_ * 16:(w_ + 1) * 16], prow[0:16, w_, :], ident_f[0:16, 0:16])
                return ins
            P.op("pe", ptr_params, reads=[prow.r, ident_f.r], writes=[ptp.r])
            P.op("act", lambda e: e.copy(ar[:, :], ptp[:, 0:16]), reads=[ptp.r], writes=[ar.r, ptp.r])
            P.op("act", lambda e: e.copy(ai[:, :], ptp[:, 16:32]), reads=[ptp.r], writes=[ai.r, ptp.r])
            P.op("act", lambda e: e.copy(dcol[:, :], ptp[:, 32:36]), reads=[ptp.r], writes=[dcol.r, ptp.r])
            P.op("act", lambda e: e.copy(pscol[:, :], ptp[:, 48:52]), reads=[ptp.r], writes=[pscol.r, ptp.r])
            r_ldt = [Reg(), Reg()]
            for h in range(2):
                P.op("dve", (lambda h: lambda e: e.tensor_copy(ldt[h * 64:(h + 1) * 64, :], Lb[h * 64:(h + 1) * 64, h:32:2]))(h),
                     reads=[Lb.r], writes=[r_ldt[h]])
            P.op("sp", lambda e: e.dma_start(out=Bn_re[:, :, :], in_=dram_ap(b_re, 0, [[16, 128], [2048, 16], [1, 16]])),
                 writes=[Bn_re.r], dma=True)
            P.op("sp", lambda e: e.dma_start(out=Bn_im[:, :, :], in_=dram_ap(b_im, 0, [[16, 128], [2048, 16], [1, 16]])),
                 writes=[Bn_im.r], dma=True)

            P.op("act", lambda e: e.activation(out=dtt[:, :], in_=ldt[:, :], func=AF.Exp),
                 reads=r_ldt, writes=[dtt.r])
            P.op("dve", lambda e: e.tensor_mul(adt[:, :], ar[:, :], dtt[:, :]), reads=[ar.r, dtt.r], writes=[adt.r])
            P.op("dve", lambda e: e.tensor_mul(wdt[:, :], ai[:, :], dtt[:, :]), reads=[ai.r, dtt.r], writes=[wdt.r])
            for mi, m in enumerate(MLIST):
                P.op("pool", (lambda mi, m: lambda e: e.memset(MT[:, :, mi], float(m)))(mi, m), writes=[MT.r])
            P.op("dve", lambda e: e.tensor_mul(ARG[:, :, :], MT[:, :, :], wdt[:, :].unsqueeze(2).to_broadcast([128, 16, NM])),
                 reads=[MT.r, wdt.r], writes=[ARG.r])
            P.op("dve", lambda e: e.tensor_mul(EXA[:, :, :], MT[:, :, :], adt[:, :].unsqueeze(2).to_broadcast([128, 16, NM])),
                 reads=[MT.r, adt.r], writes=[EXA.r])
            P.op("act", lambda e: e.activation(out=MAG[:, :, :], in_=EXA[:, :, :], func=AF.Exp),
                 reads=[EXA.r], writes=[MAG.r])
            KI = sb("KI", [128, 16, NM], mybir.dt.int32, s0)
            KF = sb("KF", [128, 16, NM], F32, s0)
            MK = sb("MK", [128, 16, NM], F32, s0)

            def sin_of(dst, A, shift):
                P.op("dve", lambda e: e.tensor_scalar(A[:, :, :], ARG[:, :, :], float(shift), None, ALU.add),
                     reads=[ARG.r], writes=[A.r])
                P.op("dve", lambda e: e.tensor_scalar(KI[:, :, :], A[:, :, :], 1.0 / TWO_PI, None, ALU.mult),
                     reads=[A.r], writes=[KI.r])
                P.op("dve", lambda e: e.tensor_copy(KF[:, :, :], KI[:, :, :]), reads=[KI.r], writes=[KF.r])
                P.op("dve", lambda e: e.scalar_tensor_tensor(out=A[:, :, :], in0=KF[:, :, :], scalar=-TWO_PI, in1=A[:, :, :],
                                                             op0=ALU.mult, op1=ALU.add),
                     reads=[KF.r, A.r], writes=[A.r])
                P.op("dve", lambda e: e.tensor_scalar(MK[:, :, :], A[:, :, :], math.pi, TWO_PI, ALU.is_gt, ALU.mult),
                     reads=[A.r], writes=[MK.r])
                P.op("dve", lambda e: e.tensor_sub(A[:, :, :], A[:, :, :], MK[:, :, :]), reads=[A.r, MK.r], writes=[A.r])
                P.op("dve", lambda e: e.tensor_scalar(MK[:, :, :], A[:, :, :], -math.pi, TWO_PI, ALU.is_lt, ALU.mult),
                     reads=[A.r], writes=[MK.r])
                P.op("dve", lambda e: e.tensor_add(A[:, :, :], A[:, :, :], MK[:, :, :]), reads=[A.r, MK.r], writes=[A.r])
                P.op("dve", lambda e: e.tensor_scalar(A[:, :, :], A[:, :, :], -math.pi, math.pi, ALU.max, ALU.min),
                     reads=[A.r], writes=[A.r])
                P.op("act", lambda e: e.activation(out=dst[:, :, :], in_=A[:, :, :], func=AF.Sin),
                     reads=[A.r], writes=[dst.r])
            sin_of(SN, SA, 0.0)
            sin_of(CS, CA, 0.5 * math.pi)
            r_lre, r_lim, r_lni = Reg(), Reg(), Reg()
            P.op("dve", lambda e: e.tensor_mul(lam[:, :, :, 0], MAG[:, :, :], CS[:, :, :]), reads=[MAG.r, CS.r], writes=[r_lre])
            P.op("dve", lambda e: e.tensor_mul(lam[:, :, :, 1], MAG[:, :, :], SN[:, :, :]), reads=[MAG.r, SN.r], writes=[r_lim])
            P.op("dve", lambda e: e.tensor_scalar(lam[:, :, :, 2], lam[:, :, :, 1], -1.0, None, ALU.mult),
                 reads=[r_lim], writes=[r_lni])
            lam_regs = [r_lre, r_lim, r_lni]

            lr1 = lam[:, :, 0, 0]
            li1 = lam[:, :, 0, 1]
            P.op("dve", lambda e: e.tensor_scalar(t1[:, :], lr1, -1.0, None, ALU.add), reads=[r_lre], writes=[t1.r])
            P.op("dve", lambda e: e.tensor_mul(t2[:, :], ar[:, :], ar[:, :]), reads=[ar.r], writes=[t2.r])
            P.op("dve", lambda e: e.tensor_mul(t3[:, :], ai[:, :], ai[:, :]), reads=[ai.r], writes=[t3.r])
            P.op("dve", lambda e: e.tensor_add(t2[:, :], t2[:, :], t3[:, :]), reads=[t2.r, t3.r], writes=[t2.r])
            P.op("dve", lambda e: e.reciprocal(t2[:, :], t2[:, :]), reads=[t2.r], writes=[t2.r])
            P.op("dve", lambda e: e.tensor_mul(t3[:, :], t1[:, :], ar[:, :]), reads=[t1.r, ar.r], writes=[t3.r])
            P.op("dve", lambda e: e.tensor_mul(t4[:, :], li1, ai[:, :]), reads=[r_lim, ai.r], writes=[t4.r])
            P.op("dve", lambda e: e.tensor_add(t3[:, :], t3[:, :], t4[:, :]), reads=[t3.r, t4.r], writes=[t3.r])
            P.op("dve", lambda e: e.tensor_mul(cre[:, :], t3[:, :], t2[:, :]), reads=[t3.r, t2.r], writes=[cre.r])
            P.op("dve", lambda e: e.tensor_mul(t3[:, :], li1, ar[:, :]), reads=[r_lim, ar.r, cre.r], writes=[t3.r])
            P.op("dve", lambda e: e.tensor_mul(t4[:, :], t1[:, :], ai[:, :]), reads=[t1.r, ai.r, t3.r], writes=[t4.r])
            P.op("dve", lambda e: e.tensor_sub(t3[:, :], t3[:, :], t4[:, :]), reads=[t3.r, t4.r], writes=[t3.r])
            P.op("dve", lambda e: e.tensor_mul(cim[:, :], t3[:, :], t2[:, :]), reads=[t3.r, t2.r], writes=[cim.r])

            creb = cre[:, :].unsqueeze(2).to_broadcast([128, 16, 16])
            cimb = cim[:, :].unsqueeze(2).to_broadcast([128, 16, 16])
            P.op("dve", lambda e: e.tensor_mul(Bt1[:, :, :], Bn_re[:, :, :], creb), reads=[Bn_re.r, cre.r], writes=[Bt1.r])
            P.op("dve", lambda e: e.tensor_mul(Bt2[:, :, :], Bn_im[:, :, :], cimb), reads=[Bn_im.r, cim.r], writes=[Bt2.r])
            P.op("dve", lambda e: e.tensor_sub(Bb_re[:, :, :], Bt1[:, :, :], Bt2[:, :, :]), reads=[Bt1.r, Bt2.r], writes=[Bb_re.r])
            P.op("dve", lambda e: e.tensor_mul(Bt1[:, :, :], Bn_im[:, :, :], creb), reads=[Bn_im.r, cre.r, Bb_re.r], writes=[Bt1.r])
            P.op("dve", lambda e: e.tensor_mul(Bt2[:, :, :], Bn_re[:, :, :], cimb), reads=[Bn_re.r, cim.r, Bb_re.r], writes=[Bt2.r])
            P.op("dve", lambda e: e.tensor_add(Bb_im[:, :, :], Bt1[:, :, :], Bt2[:, :, :]), reads=[Bt1.r, Bt2.r], writes=[Bb_im.r])
            P.op("pool", lambda e: e.memset(Bpad[:, :, :, :], 0.0), writes=[Bpad.r])
            for ri, src in enumerate((Bb_re, Bb_im)):
                for h in range(2):
                    P.op("dve", (lambda ri, src, h: lambda e: e.tensor_copy(
                        Bpad[h * 64:(h + 1) * 64, ri, :, h * 16:(h + 1) * 16], src[h * 64:(h + 1) * 64, :, :]))(ri, src, h),
                        reads=[src.r], writes=[Bpad.r])
            CTf = sb("CTf", [128, 2, 16, 32], F32, s0)
            Bpb = sb("Bpb", [128, 2, 16, 32], BF16, s0)
            BL = [sb("BL%d" % i, [128, 2, 16, 32], F32, s0) for i in range(1)]
            Wtmp = [sb("Wtmp%d" % i, [128, 16, 32], F32, s0) for i in range(4)]
            I32 = sb("I32", [128, 32], F32, s0)
            wk = [0]

            def nxtw():
                t = Wtmp[wk[0] % 4]
                wk[0] += 1
                return t

            def lb(m, c):
                return lam[:, :, MI[m], c].unsqueeze(2).to_broadcast([128, 16, 32])
            k = 0
            for m in range(TC if 'b' not in SKIP else 1):
                if m == 0:
                    srcB = Bpad
                else:
                    srcB = BL[0]
                    w1, w2 = nxtw(), nxtw()
                    P.op("dve", (lambda m, srcB: lambda e: e.tensor_tensor(out=srcB[:, 0, :, :], in0=Bpad[:, 0, :, :], in1=lb(m, 0), op=ALU.mult))(m, srcB),
                         reads=[Bpad.r] + lam_regs, writes=[srcB.r])
                    P.op("dve", (lambda m, w1: lambda e: e.tensor_tensor(out=w1[:, :, :], in0=Bpad[:, 1, :, :], in1=lb(m, 2), op=ALU.mult))(m, w1),
                         reads=[Bpad.r] + lam_regs, writes=[w1.r])
                    P.op("dve", (lambda srcB, w1: lambda e: e.tensor_add(srcB[:, 0, :, :], srcB[:, 0, :, :], w1[:, :, :]))(srcB, w1),
                         reads=[srcB.r, w1.r], writes=[srcB.r])
                    P.op("dve", (lambda m, srcB: lambda e: e.tensor_tensor(out=srcB[:, 1, :, :], in0=Bpad[:, 0, :, :], in1=lb(m, 1), op=ALU.mult))(m, srcB),
                         reads=[Bpad.r] + lam_regs, writes=[srcB.r])
                    P.op("dve", (lambda m, w2: lambda e: e.tensor_tensor(out=w2[:, :, :], in0=Bpad[:, 1, :, :], in1=lb(m, 0), op=ALU.mult))(m, w2),
                         reads=[Bpad.r] + lam_regs, writes=[w2.r])
                    P.op("dve", (lambda srcB, w2: lambda e: e.tensor_add(srcB[:, 1, :, :], srcB[:, 1, :, :], w2[:, :, :]))(srcB, w2),
                         reads=[srcB.r, w2.r], writes=[srcB.r])
                for ri in range(2):
                    for gq in range(4):
                        pt = ps0[k % 4]
                        k += 1
                        P.op("pe", (lambda ri, gq, pt, srcB: lambda e: e.transpose(
                            pt[:, 0:128], srcB[:, ri, gq * 4:(gq + 1) * 4, :], ident_f[:, :]))(ri, gq, pt, srcB),
                            reads=[srcB.r, ident_f.r], writes=[pt.r])
                        P.op("act", (lambda ri, gq, pt, m: lambda e: e.copy(BbTs[:, ri, gq, TC - 1 - m, :], pt[:, 0:128]))(ri, gq, pt, m),
                             reads=[pt.r], writes=[BbTs.r])
            P.op("dve", lambda e: e.tensor_copy(Bpb[:, :, :, :], Bpad[:, :, :, :]), reads=[Bpad.r], writes=[Bpb.r])

            Cnat = [sb("Cnat%d" % i, [128, 4, 64], F32, s0) for i in range(2)]
            pidx = sb("pidx", [128, 1], mybir.dt.int32, s0)
            mko = sb("mko", [128, 1], F32, s0)
            mke = sb("mke", [128, 1], F32, s0)
            P.op("pool", lambda e: e.iota(pidx[:, :], [[0, 1]], base=0, channel_multiplier=1), writes=[pidx.r])
            P.op("dve", lambda e: e.tensor_scalar(pidx[:, :], pidx[:, :], 4, 1, ALU.arith_shift_right, ALU.bitwise_and),
                 reads=[pidx.r], writes=[pidx.r])
            P.op("dve", lambda e: e.tensor_copy(mko[:, :], pidx[:, :]), reads=[pidx.r], writes=[mko.r])
            P.op("dve", lambda e: e.tensor_scalar(mke[:, :], mko[:, :], -1.0, 1.0, ALU.mult, ALU.add), reads=[mko.r], writes=[mke.r])
            for ri, csrc in enumerate((c_re, c_im)):
                cin = Cin[ri]
                cn_ = Cnat[ri]
                P.op("sp", (lambda cn_, csrc: lambda e: e.dma_start(
                    out=cn_[:, :, :], in_=dram_ap(csrc, 0, [[64, 128], [128 * 64, 4], [1, 64]])))(cn_, csrc),
                    writes=[cn_.r], dma=True)
                P.op("dve", (lambda cin, cn_: lambda e: e.tensor_scalar(cin[:, :, 0:64], cn_[:, :, :], mke[:, 0:1], None, ALU.mult))(cin, cn_),
                     reads=[cn_.r, mke.r], writes=[cin.r])
                P.op("dve", (lambda cin, cn_: lambda e: e.tensor_scalar(cin[:, :, 64:128], cn_[:, :, :], mko[:, 0:1], None, ALU.mult))(cin, cn_),
                     reads=[cn_.r, mko.r], writes=[cin.r])
                for r in range(4):
                    pt = ps0[k % 4]
                    k += 1
                    P.op("pe", (lambda cin, r, pt: lambda e: e.transpose(pt[:, 0:128], cin[:, r, :], ident_f[:, :]))(cin, r, pt),
                         reads=[cin.r, ident_f.r], writes=[pt.r])
                    pv = pt[:, 0:128].rearrange("p (a c) -> p a c", c=32)
                    if ri == 0:
                        P.op("act", (lambda r, pv: lambda e: e.copy(CT[:, 0, r * 4:(r + 1) * 4, :], pv))(r, pv),
                             reads=[pt.r], writes=[CT.r, pt.r])
                    else:
                        P.op("act", (lambda r, pv: lambda e: e.mul(CT[:, 1, r * 4:(r + 1) * 4, :], pv, -1.0))(r, pv),
                             reads=[pt.r], writes=[CT.r, pt.r])
                    P.op("dve", (lambda ri, r, pv: lambda e: e.tensor_copy(CTf[:, ri, r * 4:(r + 1) * 4, :], pv))(ri, r, pv),
                         reads=[pt.r], writes=[CTf.r, pt.r])

            for kk in range(TC if 'c' not in SKIP else 0):
                m = kk + 1
                w1, w2, w3, w4 = nxtw(), nxtw(), nxtw(), nxtw()
                P.op("dve", (lambda m, w1: lambda e: e.tensor_tensor(out=w1[:, :, :], in0=CTf[:, 0, :, :], in1=lb(m, 0), op=ALU.mult))(m, w1),
                     reads=[CTf.r] + lam_regs, writes=[w1.r])
                P.op("dve", (lambda m, w2: lambda e: e.tensor_tensor(out=w2[:, :, :], in0=CTf[:, 1, :, :], in1=lb(m, 2), op=ALU.mult))(m, w2),
                     reads=[CTf.r] + lam_regs, writes=[w2.r])
                P.op("dve", (lambda kk, w1, w2: lambda e: e.tensor_add(Wt[:, 0, kk, :, :], w1[:, :, :], w2[:, :, :]))(kk, w1, w2),
                     reads=[w1.r, w2.r], writes=[Wt.r])
                P.op("dve", (lambda m, w3: lambda e: e.tensor_tensor(out=w3[:, :, :], in0=CTf[:, 0, :, :], in1=lb(m, 2), op=ALU.mult))(m, w3),
                     reads=[CTf.r] + lam_regs, writes=[w3.r])
                P.op("dve", (lambda m, w4: lambda e: e.tensor_tensor(out=w4[:, :, :], in0=CTf[:, 1, :, :], in1=lb(m, 0), op=ALU.mult))(m, w4),
                     reads=[CTf.r] + lam_regs, writes=[w4.r])
                P.op("dve", (lambda kk, w3, w4: lambda e: e.tensor_sub(Wt[:, 1, kk, :, :], w3[:, :, :], w4[:, :, :]))(kk, w3, w4),
                     reads=[w3.r, w4.r], writes=[Wt.r])

            for gq in range(4 if 't' not in SKIP else 0):
                pt = ps0[gq]

                def kmm(e, gq=gq, pt=pt):
                    ins = None
                    for a4 in range(4):
                        gp = gq * 4 + a4
                        R0, R1 = a4 * 32, a4 * 32 + 32
                        e.matmul(pt[R0:R1, 0:32], Bpb[:, 0, gp, :], CT[:, 0, gp, :], start=True, stop=False, tile_position=(0, R0))
                        e.matmul(pt[R0:R1, 0:32], Bpb[:, 1, gp, :], CT[:, 1, gp, :], start=False, stop=True, tile_position=(0, R0))
                        e.matmul(pt[R0:R1, 32:256], Bpb[:, 0, gp, :], Wt[:, 0, 0:TC - 1, gp, :], start=True, stop=False, tile_position=(0, R0))
                        ins = e.matmul(pt[R0:R1, 32:256], Bpb[:, 1, gp, :], Wt[:, 1, 0:TC - 1, gp, :], start=False, stop=True, tile_position=(0, R0))
                    return ins
                if "k" not in SKIP:
                    P.op("pe", kmm, reads=[Bpb.r, CT.r, Wt.r], writes=[pt.r])
                P.op("act", (lambda gq, pt: lambda e: e.copy(Kt[:, gq, :, :], pt[:, 0:256].rearrange("p (t c) -> p t c", c=32)))(gq, pt),
                     reads=[pt.r], writes=[Kt.r])
            P.op("dve", lambda e: e.tensor_add(I32[:, :], ident_f[:, 0:32], ident_f[:, 32:64]), reads=[ident_f.r], writes=[I32.r])
            P.op("dve", lambda e: e.tensor_add(I32[:, :], I32[:, :], ident_f[:, 64:96]), reads=[ident_f.r, I32.r], writes=[I32.r])
            P.op("dve", lambda e: e.tensor_add(I32[:, :], I32[:, :], ident_f[:, 96:128]), reads=[ident_f.r, I32.r], writes=[I32.r])
            for gq in range(4 if 't' not in SKIP else 0):
                P.op("dve", (lambda gq: lambda e: e.scalar_tensor_tensor(
                    out=Kt[:, gq, 0, :], in0=I32[:, :], scalar=dcol[:, gq:gq + 1], in1=Kt[:, gq, 0, :],
                    op0=ALU.mult, op1=ALU.add))(gq), reads=[I32.r, Kt.r, dcol.r], writes=[Kt.r])
            P.op("pool", lambda e: e.memset(Ktf[:, :, :, :], 0.0), writes=[Ktf.r])
            for a4 in range(4):
                P.op("dve", (lambda a4: lambda e: e.tensor_copy(
                    Ktf[a4 * 32:(a4 + 1) * 32, :, :, a4 * 32:(a4 + 1) * 32], Kt[a4 * 32:(a4 + 1) * 32, :, :, :]))(a4),
                    reads=[Kt.r], writes=[Ktf.r])
        n_ops_p0 = len(P.ops)

        with ExitStack() as sAC:
            ypT = Sub(YU, 0)
            uT = Sub(YU, 4)
            hT_r = [Reg() for _ in TTILES]
            yp_r = [[Reg() for _ in CHUNKS] for _ in range(4)]
            ys_r = [[Reg() for _ in CHUNKS] for _ in range(4)]

            with ExitStack() as sAB:
                vT_r = [[Reg() for _ in CHUNKS] for _ in range(4)]
                uT_r = [[Reg() for _ in CHUNKS] for _ in range(4)]
                vpad_r = [Reg() for _ in range(4)]
                vTs_r = [Reg() for _ in range(4)]

                with ExitStack() as sA:
                    g1b = sb("g1b", [128, D], F32, sA)
                    P.op("sp", lambda e: e.dma_start(out=g1b[:, :], in_=n1g.partition_broadcast(128)), writes=[g1b.r], dma=True)
                    xt = [sb("xt%d" % i, [128, D], F32, sA) for i in range(2)]
                    sq = [sb("sq%d" % i, [128, D], BF16, sA) for i in range(1)] * 2
                    hb = [sb("hb%d" % i, [128, D], BF16, sA) for i in range(2)]
                    ssq = [sb("ssq%d" % i, [128, 1], F32, sA) for i in range(2)]
                    rstd = [sb("rstd%d" % i, [128, 1], F32, sA) for i in range(2)]
                    ptr = [pst("ptr%d" % i, [128, 8, 128], BF16, sA) for i in range(2)]
                    pmm = [pst("pmm%d" % i, [128, 512], F32, sA) for i in range(2)]

                    wvu_r = [Reg() for _ in range(8)]
                    for kk in range(8):
                        P.op("pool", (lambda kk: lambda e: e.dma_start(
                            out=wvu[:, kk, :], in_=w_in[kk * 128:(kk + 1) * 128, 0:1024]))(kk),
                            writes=[wvu_r[kk]], dma=True)

                    pend_a = []
                    for ti, (t0, tn) in enumerate(TTILES):
                        b = ti % 2
                        src = x[t0:t0 + tn, :] if ti < 16 else xs[:, :]
                        P.op("sp", (lambda b, tn, src: lambda e: e.dma_start(out=xt[b][0:tn, :], in_=src))(b, tn, src),
                             writes=[xt[b].r], dma=True)
                        P.op("act", (lambda b, tn: lambda e: e.activation(
                            out=sq[b][0:tn, :], in_=xt[b][0:tn, :], func=AF.Square, accum_out=ssq[b][0:tn, :]))(b, tn),
                            reads=[xt[b].r], writes=[sq[b].r, ssq[b].r])
                        P.op("dve", (lambda b, tn: lambda e: e.tensor_scalar(
                            rstd[b][0:tn, :], ssq[b][0:tn, :], 1.0 / D, EPS, ALU.mult, ALU.add))(b, tn),
                            reads=[ssq[b].r], writes=[rstd[b].r])
                        P.op("act", (lambda b, tn: lambda e: e.activation(
                            out=rstd[b][0:tn, :], in_=rstd[b][0:tn, :], func=AF.Sqrt))(b, tn),
                            reads=[rstd[b].r], writes=[rstd[b].r])
                        P.op("dve", (lambda b, tn: lambda e: e.reciprocal(rstd[b][0:tn, :], rstd[b][0:tn, :]))(b, tn),
                             reads=[rstd[b].r], writes=[rstd[b].r])
                        P.op("dve", (lambda b, tn: lambda e: e.scalar_tensor_tensor(
                            out=hb[b][0:tn, :], in0=xt[b][0:tn, :], scalar=rstd[b][0:tn, 0:1], in1=g1b[0:tn, :],
                            op0=ALU.mult, op1=ALU.mult))(b, tn),
                            reads=[xt[b].r, rstd[b].r, g1b.r], writes=[hb[b].r])

                        def tr(e, b=b, tn=tn):
                            ins = None
                            for kk in range(8):
                                ins = e.transpose(ptr[b][:, kk, 0:tn], hb[b][0:tn, kk * 128:(kk + 1) * 128], ident_b[0:tn, 0:tn])
                            return ins
                        def late_a(tr=tr, b=b, t0=t0, tn=tn, ti=ti):
                            P.op("pe", tr, reads=[hb[b].r, ident_b.r], writes=[ptr[b].r])
                            P.op("act", (lambda b, t0, tn: lambda e: e.copy(hT[:, :, t0:t0 + tn], ptr[b][:, :, 0:tn]))(b, t0, tn),
                                 reads=[ptr[b].r], writes=[hT_r[ti], ptr[b].r])
                        while pend_a:
                            pend_a.pop(0)()
                        pend_a.append(late_a)
                    while pend_a:
                        pend_a.pop(0)()

                    k = 0
                    for ci, (c0, cn) in enumerate(CHUNKS):
                        tiles = [ti for ti, (t0, tn) in enumerate(TTILES) if t0 >= c0 and t0 < c0 + cn]
                        for m in range(4, 8):
                            pt = pmm[k % 2]
                            k += 1

                            def mmf(e, m=m, c0=c0, cn=cn, pt=pt):
                                ins = None
                                for kk in range(8):
                                    ins = e.matmul(pt[:, 0:cn], wvu[:, kk, m * 128:(m + 1) * 128], hT[:, kk, c0:c0 + cn],
                                                   start=(kk == 0), stop=(kk == 7))
                                return ins
                            P.op("pe", mmf, reads=[hT_r[ti] for ti in tiles] + wvu_r, writes=[pt.r])
                            P.op("dve", (lambda m, c0, cn, pt: lambda e: e.tensor_copy(
                                uT[:, m - 4, c0:c0 + cn], pt[:, 0:cn]))(m, c0, cn, pt),
                                reads=[pt.r], writes=[uT_r[m - 4][ci], pt.r])
                    P.interleave(8, n_ops_p0)
                    P.flush()

                s0.close()
                if dbg == "A":
                    dbg_outs["d_hT"] = (hT, [128, 8 * NCOL], BF16)
                    dbg_outs["d_YU"] = (YU, [128, 8 * NCOL], BF16)
                    dbg_outs["d_lam"] = (lam, [128, 16 * NM * 3], F32)
                    dbg_outs["d_BbTs"] = (BbTs, [128, 2 * 4 * TC * 128], BF16)
                    dbg_outs["d_CT"] = (CT, [128, 2 * 16 * 32], BF16)
                    dbg_outs["d_Wt"] = (Wt, [128, 2 * TC * 16 * 32], BF16)
                    dbg_outs["d_Kt"] = (Kt, [128, 4 * TC * 32], BF16)
                    _emit_dbg(nc, P, dbg_outs, es)
                    P.flush(final=True)
                    return nc, list(dbg_outs.keys())

                with ExitStack() as sB:
                    vT = sb("vT", [128, 4, PAD + SEQ], F32, sB)
                    vTs = sb("vTs", [128, 4, NS, 16], F32, sB)
                    pw_b = sb("pw_b", [128, 4, 128], BF16, sB)
                    Abuf = sb("Abuf", [128, PAD + SEQ], F32, sB)
                    Bbuf = sb("Bbuf", [128, PAD + SEQ], F32, sB)
                    ptmp = sb("ptmp", [128, 16], F32, sB)
                    wsum = sb("wsum", [128, 16], F32, sB)
                    cpl = [sb("cpl%d" % i, [120, 512], F32, sB) for i in range(2)]
                    tok15 = sb("tok15", [16, 512], F32, sB)
                    toks = sb("toks", [16, 512], F32, sB)
                    H0 = sb("H0", [128, 2, 16, 16], F32, sB)
                    Hn = sb("Hn", [128, 2, 16, 16], F32, sB)
                    BUs = sb("BUs", [128, 2, 16, 16], F32, sB)
                    T1 = sb("T1", [128, 16, 16], F32, sB)
                    T2 = sb("T2", [128, 16, 16], F32, sB)
                    srow = [Abuf, Bbuf]
                    pB = [pst("pB%d" % i, [128, 512], F32, sB) for i in range(8)]
                    pbk = [0]

                    def nextp():
                        t = pB[pbk[0] % 8]
                        pbk[0] += 1
                        return t

                    pw_r = [Reg() for _ in range(4)]
                    for q in range(4):
                        P.op("pool", (lambda q: lambda e: e.dma_start(out=pw_b[:, q, :], in_=pool_w[q * 128:(q + 1) * 128, :]))(q),
                             writes=[pw_r[q]], dma=True)
                    for j in range(2):
                        P.op("sp", (lambda j: lambda e: e.dma_start(out=cpl[j][:, :], in_=cpool[j * 120:(j + 1) * 120, :]))(j),
                             writes=[cpl[j].r], dma=True)
                    if 'd' not in SKIP:
                      P.op("sp", lambda e: e.dma_start(
                        out=dram_ap(o_pool_s, 0, [[PB * PW, NS], [1, 14 * PW]]),
                        in_=dram_ap(cpool, PW, [[PB * PW, NS], [1, 14 * PW]])), dma=True, out=True)
                    for j in range(2 if 't' not in SKIP else 0):
                        for q in range(4):
                            pt = nextp()
                            P.op("pe", (lambda j, q, pt: lambda e: e.transpose(
                                pt[:, 0:120], cpl[j][0:120, q * 128:(q + 1) * 128], ident_f[0:120, 0:120]))(j, q, pt),
                                reads=[cpl[j].r], writes=[pt.r])
                            P.op("act", (lambda j, q, pt: lambda e: e.copy(
                                vTs[:, q, j * 8:(j + 1) * 8, 0:15], pt[:, 0:120].rearrange("p (b r) -> p b r", r=15)))(j, q, pt),
                                reads=[pt.r], writes=[vTs_r[q]])

                    P.op("dve", lambda e: e.memset(vT[:, :, 0:PAD], 0.0), writes=vpad_r)
                    for m in range(4):
                        for ci, (c0, cn) in enumerate(CHUNKS):
                            pt = nextp()

                            def mmv(e, m=m, c0=c0, cn=cn, pt=pt):
                                ins = None
                                for kk in range(8):
                                    ins = e.matmul(pt[:, 0:cn], wvu[:, kk, m * 128:(m + 1) * 128], hT[:, kk, c0:c0 + cn],
                                                   start=(kk == 0), stop=(kk == 7))
                                return ins
                            P.op("pe", mmv, writes=[pt.r])
                            if ci < 4:
                                P.op("act", (lambda m, c0, cn, pt: lambda e: e.copy(
                                    vT[:, m, PAD + c0:PAD + c0 + cn], pt[:, 0:cn]))(m, c0, cn, pt),
                                    reads=[pt.r], writes=[vT_r[m][ci], pt.r])
                            else:
                                P.op("act", (lambda m, pt: lambda e: e.copy(vTs[:, m, :, 15], pt[:, 0:NS]))(m, pt),
                                     reads=[pt.r], writes=[vT_r[m][ci], pt.r])
                    P.op("dve", lambda e: e.memset(Abuf[:, 0:PAD], 0.0), writes=[Abuf.r])
                    P.op("dve", lambda e: e.memset(Bbuf[:, 0:PAD], 0.0), writes=[Bbuf.r])

                    for q in range(4):
                        w = 2 ** (q + 1)
                        allv = [vT_r[q][ci] for ci in range(4)] + [vpad_r[q]]
                        src = vT
                        P.op("dve", (lambda q: lambda e: e.tensor_add(
                            Abuf[:, PAD:], vT[:, q, PAD:], vT[:, q, PAD - 1:PAD + SEQ - 1]))(q),
                            reads=allv, writes=[Abuf.r])
                        cur, oth = Abuf, Bbuf
                        sh = 2
                        while sh < w:
                            P.op("dve", (lambda cur, oth, sh: lambda e: e.tensor_add(
                                oth[:, PAD:], cur[:, PAD:], cur[:, PAD - sh:PAD + SEQ - sh]))(cur, oth, sh),
                                reads=[cur.r], writes=[oth.r])
                            cur, oth = oth, cur
                            sh *= 2
                        P.op("dve", (lambda q, cur, w: lambda e: e.scalar_tensor_tensor(
                            out=ypT[:, q, 0:SEQ], in0=cur[:, PAD:], scalar=1.0 / w, in1=vT[:, q, PAD:],
                            op0=ALU.mult, op1=ALU.subtract))(q, cur, w),
                            reads=[cur.r] + allv, writes=[yp_r[q][ci] for ci in range(4)])
                        if w > 1:
                            P.op("dve", (lambda q, cur, w: lambda e: e.tensor_mul(
                                ptmp[:, 0:w - 1], cur[:, PAD:PAD + w - 1], rct[:, 0:w - 1]))(q, cur, w),
                                reads=[cur.r], writes=[ptmp.r])
                            P.op("dve", (lambda q, w: lambda e: e.tensor_sub(
                                ypT[:, q, 0:w - 1], ptmp[:, 0:w - 1], vT[:, q, PAD:PAD + w - 1]))(q, w),
                                reads=[ptmp.r] + allv, writes=[yp_r[q][0]])
                        P.op("dve", (lambda q, w: lambda e: e.tensor_reduce(
                            out=wsum[:, :], in_=vTs[:, q, :, 16 - w:16], axis=AX.X, op=ALU.add))(q, w),
                            reads=[vTs_r[q], vT_r[q][4]], writes=[wsum.r])
                        P.op("dve", (lambda q, w: lambda e: e.scalar_tensor_tensor(
                            out=ypT[:, q, SEQ:NCOL], in0=wsum[:, :], scalar=1.0 / w, in1=vTs[:, q, :, 15],
                            op0=ALU.mult, op1=ALU.subtract))(q, w),
                            reads=[wsum.r, vT_r[q][4]], writes=[yp_r[q][4]])
                        if 'o' in SKIP:
                            continue
                        pt = nextp()
                        P.op("pe", (lambda q, pt: lambda e: e.transpose(
                            pt[0:16, 0:128], vT[:, q, PAD + SEQ - 16:PAD + SEQ], ident_f[:, :]))(q, pt),
                            reads=[vT_r[q][3]], writes=[pt.r])
                        P.op("act", (lambda q, pt: lambda e: e.copy(tok15[0:16, q * 128:(q + 1) * 128], pt[0:16, 0:128]))(q, pt),
                             reads=[pt.r], writes=[tok15.r])
                        pt = nextp()
                        P.op("pe", (lambda q, pt: lambda e: e.transpose(
                            pt[0:16, 0:128], vTs[:, q, :, 15], ident_f[:, :]))(q, pt),
                            reads=[vT_r[q][4]], writes=[pt.r])
                        P.op("act", (lambda q, pt: lambda e: e.copy(toks[0:16, q * 128:(q + 1) * 128], pt[0:16, 0:128]))(q, pt),
                             reads=[pt.r], writes=[toks.r])
                        for ci, (c0, cn) in enumerate(CHUNKS):
                            pt = nextp()
                            P.op("pe", (lambda q, c0, cn, pt: lambda e: e.matmul(
                                pt[:, 0:cn], pw_b[:, q, :], ypT[:, q, c0:c0 + cn], start=True, stop=True))(q, c0, cn, pt),
                                reads=[pw_r[q], yp_r[q][ci]], writes=[pt.r])
                            P.op("act", (lambda q, c0, cn, pt: lambda e: e.mul(
                                ypT[:, q, c0:c0 + cn], pt[:, 0:cn], pscol[:, q:q + 1]))(q, c0, cn, pt),
                                reads=[pt.r], writes=[yp_r[q][ci]])
                    P.op("sp", lambda e: e.dma_start(out=o_pool_p[:, :], in_=tok15[1:16, :]), reads=[tok15.r], dma=True, out=True)
                    P.op("sp", lambda e: e.dma_start(out=dram_ap(o_pool_s, 14 * PW, [[PB * PW, NS], [1, PW]]), in_=toks[0:16, :]),
                         reads=[toks.r], dma=True, out=True)

                    if dbg == "B1":
                        P.flush()
                        dbg_outs["d_YU"] = (YU, [128, 8 * NCOL], BF16)
                        _emit_dbg(nc, P, dbg_outs, es)
                        P.flush(final=True)
                        return nc, list(dbg_outs.keys()) + ["o_pool_p", "o_pool_s"]
                    P.op("sp", lambda e: e.dma_start(out=srow[0][0:16, 0:2048], in_=sre[:, :]), writes=[srow[0].r], dma=True)
                    P.op("sp", lambda e: e.dma_start(out=srow[1][0:16, 0:2048], in_=sim[:, :]), writes=[srow[1].r], dma=True)
                    for ri in range(2):
                        pt = nextp()

                        def trs(e, ri=ri, pt=pt):
                            ins = None
                            for gp in range(16):
                                ins = e.transpose(pt[:, gp * 16:(gp + 1) * 16], srow[ri][0:16, gp * 128:(gp + 1) * 128], ident_f[0:16, 0:16])
                            return ins
                        if '1' not in SKIP:
                            P.op("pe", trs, reads=[srow[ri].r], writes=[pt.r])
                        P.op("act", (lambda ri, pt: lambda e: e.copy(
                            H0[:, ri, :, :], pt[:, 0:256].rearrange("p (g b) -> p g b", b=16)))(ri, pt),
                            reads=[pt.r], writes=[H0.r])
                    for ri in range(2):
                        for a4 in range(4):
                            pt = nextp()

                            def bus(e, ri=ri, pt=pt, a4=a4):
                                ins = None
                                for gq in range(4):
                                    ins = e.matmul(pt[:, gq * 16:(gq + 1) * 16], BbTs[a4 * 32:(a4 + 1) * 32, ri, gq, TC - 1, :],
                                                   uT[a4 * 32:(a4 + 1) * 32, gq, SEQ:NCOL], start=True, stop=True,
                                                   tile_position=(a4 * 32, 0))
                                return ins
                            if '2' not in SKIP:
                                P.op("pe", bus, reads=[uT_r[gq][4] for gq in range(4)], writes=[pt.r])
                            P.op("act", (lambda ri, pt, a4: lambda e: e.copy(
                                BUs[:, ri, a4:16:4, :], pt[:, 0:64].rearrange("p (g b) -> p g b", b=16)))(ri, pt, a4),
                                reads=[pt.r], writes=[BUs.r])
                    lreb = lam[:, :, 0, 0].unsqueeze(2).to_broadcast([128, 16, 16])
                    limb = lam[:, :, 0, 1].unsqueeze(2).to_broadcast([128, 16, 16])
                    P.op("dve", lambda e: e.tensor_mul(T1[:, :, :], H0[:, 0, :, :], lreb), reads=[H0.r], writes=[T1.r])
                    P.op("dve", lambda e: e.tensor_mul(T2[:, :, :], H0[:, 1, :, :], limb), reads=[H0.r], writes=[T2.r])
                    P.op("dve", lambda e: e.tensor_sub(T1[:, :, :], T1[:, :, :], T2[:, :, :]), reads=[T1.r, T2.r], writes=[T1.r])
                    P.op("dve", lambda e: e.tensor_add(Hn[:, 0, :, :], T1[:, :, :], BUs[:, 0, :, :]), reads=[T1.r, BUs.r], writes=[Hn.r])
                    P.op("dve", lambda e: e.tensor_mul(T1[:, :, :], H0[:, 1, :, :], lreb), reads=[H0.r, Hn.r], writes=[T1.r])
                    P.op("dve", lambda e: e.tensor_mul(T2[:, :, :], H0[:, 0, :, :], limb), reads=[H0.r, Hn.r], writes=[T2.r])
                    P.op("dve", lambda e: e.tensor_add(T1[:, :, :], T1[:, :, :], T2[:, :, :]), reads=[T1.r, T2.r], writes=[T1.r])
                    r_hn1 = Reg()
                    P.op("dve", lambda e: e.tensor_add(Hn[:, 1, :, :], T1[:, :, :], BUs[:, 1, :, :]), reads=[T1.r, BUs.r, Hn.r], writes=[r_hn1])
                    P.op("act", lambda e: e.copy(Hnb[:, :, :, :], Hn[:, :, :, :]), reads=[Hn.r, r_hn1], writes=[Hnb.r])
                    for ri in range(2):
                        for g4 in range(4):
                            pt = nextp()

                            def tro(e, ri=ri, g4=g4, pt=pt):
                                ins = None
                                for a in range(4):
                                    gp = g4 * 4 + a
                                    ins = e.transpose(pt[0:16, a * 128:(a + 1) * 128], Hn[:, ri, gp, :], ident_f[:, :])
                                return ins
                            if '3' not in SKIP:
                                P.op("pe", tro, reads=[Hn.r, r_hn1], writes=[pt.r])
                            P.op("act", (lambda ri, g4, pt: lambda e: e.copy(
                                srow[ri][0:16, g4 * 512:(g4 + 1) * 512], pt[0:16, 0:512]))(ri, g4, pt),
                                reads=[pt.r], writes=[srow[ri].r])
                    P.op("sp", lambda e: e.dma_start(out=o_re_s[:, :], in_=srow[0][0:16, 0:2048]), reads=[srow[0].r], dma=True, out=True)
                    P.op("sp", lambda e: e.dma_start(out=o_im_s[:, :], in_=srow[1][0:16, 0:2048]), reads=[srow[1].r], dma=True, out=True)

                    if dbg == "B2":
                        P.flush()
                        dbg_outs["d_YU"] = (YU, [128, 8 * NCOL], BF16)
                        _emit_dbg(nc, P, dbg_outs, es)
                        P.flush(final=True)
                        return nc, list(dbg_outs.keys()) + ["o_pool_p", "o_pool_s", "o_re_s", "o_im_s"]
                    P.flush()

            sW.close()
            with ExitStack() as sS:
                Eall = sb("Eall", [128, 2, 16, NJ], F32, sS)
                Sbf = sb("Sbf", [128, 2, 16, NJ], BF16, sS)
                uD = sb("uD", [128, 4, TC, NJ], BF16, sS)
                wg1 = sb("wg1", [128, 8, 1024], BF16, sS)
                stg = [sb("stg%d" % i, [128, 512], BF16, sS) for i in range(2)]
                wg1_r = [Reg() for _ in range(8)]
                for kk in range(8):
                    P.op("pool", (lambda kk: lambda e: e.dma_start(out=wg1[:, kk, :], in_=w_in[kk * 128:(kk + 1) * 128, 1024:2048]))(kk),
                         writes=[wg1_r[kk]], dma=True)
                tmpS = [sb("tmpS%d" % i, [128, 16, 64], F32, sS) for i in range(2)]
                tmpSs = [sb("tmpSs%d" % i, [128, 16, 16], F32, sS) for i in range(2)]
                Hfin = sb("Hfin", [128, 2, 16], F32, sS)
                fin = sb("fin", [16, 2, 128], F32, sS)
                pS = pst("pS", [128, 8, 512], F32, sS)
                bank_r = [[Reg() for _ in range(4)] for _ in range(8)]
                uTg_r = [Reg() for _ in range(16)]
                LV2 = [(4 ** l, 4 ** l - 1, NJ // (4 ** l)) for l in range(5)]
                cls = [dict() for _ in range(2)]
                for ri in range(2):
                    for l in range(4):
                        for kk in range(3):
                            cls[ri][(l, kk)] = Reg()
                    cls[ri][(4, 0)] = Reg()
                allcls = [list(cls[ri].values()) for ri in range(2)]

                def deeper(ri, l):
                    return [r for (ll, kk), r in cls[ri].items() if ll > l]

                uD_r = [Reg() for _ in range(4)]
                uTq_r = [Reg() for _ in range(4)]
                for gq in range(4):
                    src_v = YU.t[:, 4 + gq, 0:SEQ].rearrange("p (j k) -> p k j", k=TC)
                    if gq % 2 == 0:
                        P.op("act", (lambda gq, src_v: lambda e: e.copy(uD[:, gq, :, :], src_v))(gq, src_v), reads=[uTq_r[gq]], writes=[uD_r[gq]])
                    else:
                        P.op("pool", (lambda gq, src_v: lambda e: e.tensor_copy(uD[:, gq, :, :], src_v))(gq, src_v), reads=[uTq_r[gq]], writes=[uD_r[gq]])
                bbts_r = [Reg()]
                wg2_v = BbTs.t[:].rearrange("p a b c d -> p (a b c d)").rearrange("p (k c) -> p k c", c=1024)
                idx = 0
                for gq in range(4):
                    for ri in range(2):
                        bks = [(idx * 4 + a4) % 8 for a4 in range(4)]
                        idx += 1

                        def emm(e, ri=ri, bks=bks, gq=gq):
                            ins = None
                            for s_ in range(TC):
                                for a4 in range(4):
                                    R0, R1 = a4 * 32, a4 * 32 + 32
                                    ins = e.matmul(pS[:, bks[a4], 0:NJ], BbTs[R0:R1, ri, gq, s_, :], uD[R0:R1, gq, s_, :],
                                                   start=(s_ == 0), stop=(s_ == TC - 1), tile_position=(R0, 0))
                            return ins
                        wr = []
                        for a4 in range(4):
                            wr += bank_r[bks[a4]]
                        P.op("pe", emm, reads=[uD_r[gq]] + bbts_r, writes=wr)
                        for a4 in range(4):
                            gp = gq * 4 + a4
                            P.op("act" if a4 % 2 == 0 else "dve", (lambda ri, gp, bk: lambda e: (
                                e.copy(Eall[:, ri, gp, :].rearrange("p (k jj) -> p jj k", k=4),
                                       pS[:, bk, 0:NJ].rearrange("p (jj k) -> p jj k", k=4)) if hasattr(e, "activation")
                                else e.tensor_copy(Eall[:, ri, gp, :].rearrange("p (k jj) -> p jj k", k=4),
                                                   pS[:, bk, 0:NJ].rearrange("p (jj k) -> p jj k", k=4))))(ri, gp, bks[a4]),
                                reads=[], writes=allcls[ri] + bank_r[bks[a4]])

                it_g_ = [0]
                gate_mm_regs = [Reg()]

                def gate_half(half):
                    gs_, gs_r = ((g1s, g1s_r), (g2s, g2s_r))[half]
                    wsel = wg1 if half == 0 else None
                    for ci, (c0, cn) in enumerate(CHUNKS):
                        for j in range(8):
                            bk = 6 + it_g_[0] % 2
                            sb_ = stg[it_g_[0] % 2]
                            it_g_[0] += 1

                            def g1mm(e, j=j, c0=c0, cn=cn, bk=bk, half=half):
                                ins = None
                                for kk in range(8):
                                    wv = wg1[:, kk, j * 128:(j + 1) * 128] if half == 0 else wg2_v[:, kk, j * 128:(j + 1) * 128]
                                    ins = e.matmul(pS[:, bk, 0:cn], wv, hT[:, kk, c0:c0 + cn],
                                                   start=(kk == 0), stop=(kk == 7))
                                return ins
                            P.op("pe", g1mm, reads=(wg1_r if half == 0 else bbts_r) + gate_mm_regs, writes=bank_r[bk])
                            P.op("act", (lambda cn, bk, sb_: lambda e: e.activation(out=sb_[:, 0:cn], in_=pS[:, bk, 0:cn], func=AF.Sigmoid))(cn, bk, sb_),
                                 reads=[], writes=bank_r[bk] + [sb_.r])
                            P.op("sp", (lambda j, c0, cn, sb_, gs_: lambda e: e.dma_start(out=gs_[:, j, c0:c0 + cn], in_=sb_[:, 0:cn]))(j, c0, cn, sb_, gs_),
                                 reads=[sb_.r], writes=[gs_r[ci]], dma=True)

                for kk in range(8):
                    P.op("pool", (lambda kk: lambda e: e.dma_start(out=wg2_v[:, kk, :], in_=w_in[kk * 128:(kk + 1) * 128, 2048:3072]))(kk),
                         writes=bbts_r, dma=True)
                gate_half(0)
                gate_half(1)
                tk = [0]

                def cstepb(m, tgt, srcv, nel, treg, sreg):
                    mi = MI[m]
                    prods = ((0, 0, 0), (1, 2, 0), (0, 1, 1), (1, 0, 1))
                    for (sri, lc, tri) in prods:
                        if nel <= 16:
                            tb_ = (tmpS + tmpSs)[tk[0] % 4]
                        else:
                            tb_ = tmpS[tk[0] % 2]
                        tk[0] += 1
                        Sv = Eall[:, sri, :, srcv]
                        Tv = Eall[:, tri, :, tgt]
                        lv = lam[:, :, mi, lc].unsqueeze(2).to_broadcast([128, 16, nel])
                        tv = tb_[:, :, 0:nel]
                        if nel >= 16 and ENG_M == "pool":
                            GS = 10
                            Sv1, Sv2 = Eall[:, sri, 0:GS, srcv], Eall[:, sri, GS:16, srcv]
                            lv1 = lam[:, 0:GS, mi, lc].unsqueeze(2).to_broadcast([128, GS, nel])
                            lv2 = lam[:, GS:16, mi, lc].unsqueeze(2).to_broadcast([128, 16 - GS, nel])
                            tv1, tv2 = tb_[:, 0:GS, 0:nel], tb_[:, GS:16, 0:nel]
                            r1, r2 = Reg(), Reg()
                            P.op("pool", (lambda tv1, Sv1, lv1: lambda e: e.tensor_tensor(out=tv1, in0=Sv1, in1=lv1, op=ALU.mult))(tv1, Sv1, lv1),
                                 reads=sreg(sri) + [tb_.r], writes=[r1])
                            P.op("dve", (lambda tv2, Sv2, lv2: lambda e: e.tensor_tensor(out=tv2, in0=Sv2, in1=lv2, op=ALU.mult))(tv2, Sv2, lv2),
                                 reads=sreg(sri) + [tb_.r], writes=[r2])
                            P.op("dve", (lambda tv, Tv: lambda e: e.tensor_add(Tv, Tv, tv))(tv, Tv),
                                 reads=[r1, r2] + treg(tri), writes=treg(tri) + [tb_.r])
                            continue
                        P.op(ENG_M, (lambda tv, Sv, lv: lambda e: e.tensor_tensor(out=tv, in0=Sv, in1=lv, op=ALU.mult))(tv, Sv, lv),
                             reads=sreg(sri), writes=[tb_.r])
                        P.op("dve", (lambda tv, Tv: lambda e: e.tensor_add(Tv, Tv, tv))(tv, Tv),
                             reads=[tb_.r] + treg(tri), writes=treg(tri))

                QB = NJ // 4

                def sl(l, kk, j0, cnt):
                    if l == 0:
                        return slice(kk * QB + j0, kk * QB + j0 + cnt)
                    sg = 4 ** (l - 1)
                    first = 3 * QB + (sg - 1) + sg * (4 * j0 + kk)
                    return slice(first, first + 4 * sg * (cnt - 1) + 1, 4 * sg)

                def scan_level(l):
                    sig, off, n = LV2[l]
                    if n == 1:
                        return
                    nch = n // 4
                    for kk in (1, 2, 3):
                        tgt = sl(l, kk, 0, nch)
                        srcv = sl(l, kk - 1, 0, nch)
                        if kk < 3:
                            treg = (lambda kk: lambda ri: [cls[ri][(l, kk)]])(kk)
                        else:
                            treg = lambda ri: deeper(ri, l)
                        sreg = (lambda kk: lambda ri: [cls[ri][(l, kk - 1)]])(kk)
                        cstepb(TC * sig, tgt, srcv, nch, treg, sreg)
                    scan_level(l + 1)
                    if nch > 1:
                        for kk in (0, 1, 2):
                            tgt = sl(l, kk, 1, nch - 1)
                            srcv = sl(l, 3, 0, nch - 1)
                            treg = (lambda kk: lambda ri: [cls[ri][(l, kk)]])(kk)
                            sreg = lambda ri: deeper(ri, l)
                            cstepb(TC * sig * (kk + 1), tgt, srcv, nch - 1, treg, sreg)
                scan_level(0)

                P.op("pool", lambda e: e.memset(Sbf[:, :, :, 0:1], 0.0), writes=[Sbf.r])
                for ri in range(2):
                    P.op("act", (lambda ri: lambda e: e.copy(Hfin[:, ri, :], Eall[:, ri, :, NJ - 1]))(ri),
                         reads=allcls[ri], writes=[Hfin.r])
                    for k4 in range(4):
                        cnt = QB if k4 < 3 else QB - 1
                        P.op("act", (lambda ri, k4, cnt: lambda e: e.copy(
                            Sbf[:, ri, :, 1 + k4:1 + k4 + 4 * (cnt - 1) + 1:4], Eall[:, ri, :, k4 * QB:k4 * QB + cnt]))(ri, k4, cnt),
                            reads=allcls[ri], writes=[Sbf.r])
                for ri in range(2):
                    bk = ri
                    P.op("pe", (lambda ri, bk: lambda e: e.transpose(pS[0:16, bk, 0:128], Hfin[:, ri, :], ident_f[:, :]))(ri, bk),
                         reads=[Hfin.r], writes=bank_r[bk])
                    P.op("act", (lambda ri, bk: lambda e: e.copy(fin[0:16, ri, :], pS[0:16, bk, 0:128]))(ri, bk),
                         reads=bank_r[bk], writes=[fin.r])
                P.op("sp", lambda e: e.dma_start(out=o_re_p[:, :], in_=fin[0:16, 0, :]), reads=[fin.r], dma=True, out=True)
                P.op("sp", lambda e: e.dma_start(out=o_im_p[:, :], in_=fin[0:16, 1, :]), reads=[fin.r], dma=True, out=True)

                for gq in range(4):
                    bb = (gq % 2) * 4

                    def ymm2(e, gq=gq, bb=bb):
                        ins = None
                        for kk in range(TC):
                            ov = pS[:, bb + kk // 2, (kk % 2) * NJ:(kk % 2) * NJ + NJ]
                            for a4 in range(4):
                                gp = gq * 4 + a4
                                R0, R1 = a4 * 32, a4 * 32 + 32
                                ovq = pS[R0:R1, bb + kk // 2, (kk % 2) * NJ:(kk % 2) * NJ + NJ]
                                e.matmul(ovq, Wt[:, 0, kk, gp, :], Sbf[:, 0, gp, :], start=True, stop=False,
                                         tile_position=(0, R0))
                                e.matmul(ovq, Wt[:, 1, kk, gp, :], Sbf[:, 1, gp, :], start=False, stop=False,
                                         tile_position=(0, R0))
                            for s_ in range(kk + 1):
                                ins = e.matmul(ov, Ktf[:, gq, kk - s_, :], uD[:, gq, s_, :], start=False, stop=(s_ == kk))
                        return ins
                    wr = []
                    for b4 in range(4):
                        wr += bank_r[bb + b4]
                    P.op("pe", ymm2, reads=[Sbf.r, uD_r[gq], fin.r], writes=wr)
                    yo_v = YU.t[:, 4 + gq, 0:SEQ].rearrange("p (j k) -> p k j", k=TC)
                    pv_ = pS[:, bb:bb + 4, :].rearrange("p b (h j) -> p (b h) j", j=NJ)
                    P.op("act", (lambda yo_v, pv_: lambda e: e.copy(yo_v, pv_))(yo_v, pv_),
                         reads=[], writes=wr + [uTg_r[gq * 4 + a4] for a4 in range(4)] + [uTq_r[gq]])
                for gp in range(16):
                    gq, a4 = gp // 4, gp % 4
                    R0, R1 = a4 * 32, a4 * 32 + 32
                    bk = gp % 8

                    def ysm(e, gp=gp, bk=bk, R0=R0, R1=R1):
                        e.matmul(pS[R0:R1, bk, 0:NS], CT[:, 0, gp, :], Hnb[:, 0, gp, :], start=True, stop=False, tile_position=(0, R0))
                        return e.matmul(pS[R0:R1, bk, 0:NS], CT[:, 1, gp, :], Hnb[:, 1, gp, :], start=False, stop=True,
                                        tile_position=(0, R0))
                    P.op("pe", ysm, reads=[uTg_r[gp]], writes=[bank_r[bk][a4]])
                    P.op("dve", (lambda bk, gq=gq, R0=R0, R1=R1: lambda e: e.scalar_tensor_tensor(
                        out=uT[R0:R1, gq, SEQ:NCOL], in0=uT[R0:R1, gq, SEQ:NCOL], scalar=dcol[R0:R1, gq:gq + 1],
                        in1=pS[R0:R1, bk, 0:NS], op0=ALU.mult, op1=ALU.add))(bk),
                        reads=[bank_r[bk][a4], uTg_r[gp]], writes=[uTg_r[gp]])
                hTflat = hT.t[:].rearrange("p a b -> p (a b)")
                po_v = hTflat[:, 0:4096].rearrange("p (k c) -> p k c", c=1024)
                gl_v = hTflat[:, 4096:12288].rearrange("p (k c) -> p k c", c=2048)
                hT_all = [Reg()]
                for kk in range(4):
                    P.op("pool", (lambda kk: lambda e: e.dma_start(out=po_v[:, kk, :], in_=pool_out[kk * 128:(kk + 1) * 128, :]))(kk),
                         writes=hT_all + gate_mm_regs, dma=True)
                for kk in range(4):
                    P.op("pool", (lambda kk: lambda e: e.dma_start(out=gl_v[:, kk, :], in_=glu[kk * 128:(kk + 1) * 128, :]))(kk),
                         writes=hT_all + gate_mm_regs, dma=True)
                P.flush()
            sT.close()

            if dbg == "B":
                dbg_outs["d_YU"] = (YU, [128, 8 * NCOL], BF16)
                _emit_dbg(nc, P, dbg_outs, es)
                P.flush(final=True)
                return nc, list(dbg_outs.keys()) + ["o_pool_p", "o_re_p", "o_im_p", "o_pool_s", "o_re_s", "o_im_s"]
            mT = YU
            mT_r = [[(yp_r[j][ci] if j < 4 else uT_r[j - 4][ci]) for ci in range(len(CHUNKS))] for j in range(8)]
            with ExitStack() as sC:
                sg1 = [sb("sg1_%d" % i, [128, 8, 512], BF16, sC) for i in range(2)]
                sg2 = [sb("sg2_%d" % i, [128, 8, 512], BF16, sC) for i in range(2)]
                class _V:
                    def __init__(self, ap):
                        self.ap = ap

                    def __getitem__(self, k):
                        return self.ap[k]
                po = _V(po_v)
                gl = _V(gl_v)
                mtmp = [sb("mtmp%d" % i, [128, 8, 512], BF16, sC) for i in range(2)]
                sz2 = [sb("sz2%d" % i, [128, 512], BF16, sC) for i in range(4)]
                ta = [sb("ta%d" % i, [128, 512], F32, sC) for i in range(4)]
                tb = [sb("tb%d" % i, [128, 512], F32, sC) for i in range(4)]
                pC = [pst("pC%d" % i, [128, 512], F32, sC) for i in range(8)]
                pck = [0]

                def nextc():
                    t = pC[pck[0] % 8]
                    pck[0] += 1
                    return t
                wg_r = [Reg() for _ in range(8)]
                po_r = hT_all
                gl_r = hT_all
                it = 0

                def load_sg1(ci):
                    c0, cn = CHUNKS[ci]
                    t_ = sg1[ci % 2]
                    P.op("sp", (lambda c0, cn, t_: lambda e: e.dma_start(out=t_[:, :, 0:cn], in_=g1s[:, :, c0:c0 + cn]))(c0, cn, t_),
                         reads=[g1s_r[ci]], writes=[t_.r], dma=True)
                    t2_ = sg2[ci % 2]
                    P.op("sp", (lambda c0, cn, t2_: lambda e: e.dma_start(out=t2_[:, :, 0:cn], in_=g2s[:, :, c0:c0 + cn]))(c0, cn, t2_),
                         reads=[g2s_r[ci]], writes=[t2_.r], dma=True)
                load_sg1(0)
                for ci, (c0, cn) in enumerate(CHUNKS):
                    tiles = [ti for ti, (t0, tn) in enumerate(TTILES) if t0 >= c0 and t0 < c0 + cn]
                    hreg = [hT_r[ti] for ti in tiles]
                    mt = mtmp[ci % 2]
                    if ci + 1 < len(CHUNKS):
                        load_sg1(ci + 1)
                    sg_c = sg1[ci % 2]
                    sg2_c = sg2[ci % 2]
                    for j in range(8):
                        b = it % 4
                        it += 1

                        def grp(wt, nk, col0, rhsT, rsel, pt, c0=c0, cn=cn):
                            def f(e):
                                ins = None
                                for kk in range(nk):
                                    ins = e.matmul(pt[:, 0:cn], wt[:, kk, col0:col0 + 128], rhsT[:, rsel + kk, c0:c0 + cn],
                                                   start=(kk == 0), stop=(kk == nk - 1))
                                return ins
                            return f
                        p3, p4, p5 = nextc(), nextc(), nextc()
                        P.op("pe", grp(gl, 4, 1024 + j * 128, YU, 4, p5), reads=[uT_r[kk][ci] for kk in range(4)] + gl_r, writes=[p5.r])
                        P.op("pe", grp(po, 4, j * 128, YU, 0, p3), reads=[yp_r[kk][ci] for kk in range(4)] + po_r, writes=[p3.r])
                        P.op("pe", grp(gl, 4, j * 128, YU, 4, p4), reads=[uT_r[kk][ci] for kk in range(4)] + gl_r, writes=[p4.r])
                        P.op("act", (lambda b, cn, p5: lambda e: e.activation(out=sz2[b][:, 0:cn], in_=p5[:, 0:cn], func=AF.Sigmoid))(b, cn, p5),
                             reads=[p5.r], writes=[sz2[b].r])
                        P.op("dve", (lambda b, cn, p3, j, sg_c: lambda e: e.tensor_mul(ta[b][:, 0:cn], sg_c[:, j, 0:cn], p3[:, 0:cn]))(b, cn, p3, j, sg_c),
                             reads=[sg_c.r, p3.r], writes=[ta[b].r, p3.r])
                        P.op("dve", (lambda b, cn, p4: lambda e: e.tensor_mul(tb[b][:, 0:cn], sz2[b][:, 0:cn], p4[:, 0:cn]))(b, cn, p4),
                             reads=[sz2[b].r, p4.r], writes=[tb[b].r])
                        P.op("pool", (lambda b, cn, j, sg2_c: lambda e: e.tensor_mul(tb[b][:, 0:cn], tb[b][:, 0:cn], sg2_c[:, j, 0:cn]))(b, cn, j, sg2_c),
                             reads=[tb[b].r, sg2_c.r], writes=[tb[b].r])
                        P.op("dve", (lambda b, cn, j, mt: lambda e: e.tensor_add(mt[:, j, 0:cn], ta[b][:, 0:cn], tb[b][:, 0:cn]))(b, cn, j, mt),
                             reads=[ta[b].r, tb[b].r], writes=[mt.r])
                    P.op("dve", (lambda c0, cn, mt: lambda e: e.tensor_copy(YU[:, :, c0:c0 + cn], mt[:, :, 0:cn]))(c0, cn, mt),
                         reads=[mt.r], writes=[mT_r[j][ci] for j in range(8)])
                P.flush()

            if dbg == "C":
                dbg_outs["d_mT"] = (YU, [128, 8 * NCOL], BF16)
                _emit_dbg(nc, P, dbg_outs, es)
                P.flush(final=True)
                return nc, list(dbg_outs.keys())

            with ExitStack() as sDE:
                x1 = sb("x1", [128, 17, D], F32, sDE)
                wu0 = sb("wu0", [128, 8, 512], BF16, sDE)
                wd0 = sb("wd0", [128, 4, D], BF16, sDE)
                x1_r = [Reg() for _ in TTILES]
                h2_r = [Reg() for _ in TTILES]
                with ExitStack() as sD:
                    wo = sb("wo", [128, 8, D], BF16, sD)
                    g2b = sb("g2b", [128, D], F32, sD)
                    xt = [sb("xtd%d" % i, [128, D], F32, sD) for i in range(2)]
                    sq = [sb("sqd%d" % i, [128, D], BF16, sD) for i in range(2)]
                    hb = [sb("hbd%d" % i, [128, D], BF16, sD) for i in range(2)]
                    ssq = [sb("ssqd%d" % i, [128, 1], F32, sD) for i in range(2)]
                    rstd = [sb("rstdd%d" % i, [128, 1], F32, sD) for i in range(2)]
                    ptr = [pst("ptrd%d" % i, [128, 8, 128], BF16, sD) for i in range(2)]
                    pD = [pst("pD%d" % i, [128, 512], F32, sD) for i in range(4)]
                    wo_r = [Reg() for _ in range(8)]
                    for kk in range(8):
                        P.op("pool", (lambda kk: lambda e: e.dma_start(out=wo[:, kk, :], in_=w_out[kk * 128:(kk + 1) * 128, :]))(kk),
                             writes=[wo_r[kk]], dma=True)
                    P.op("sp", lambda e: e.dma_start(out=g2b[:, :], in_=n2g.partition_broadcast(128)), writes=[g2b.r], dma=True)
                    pend_d = []
                    P.op("pool", lambda e: e.dma_start(
                        out=wu0[:, :, :], in_=dram_ap(w_up, 0, [[DFF, 128], [128 * DFF, 8], [1, 512]])), writes=[wu0.r], dma=True)
                    P.op("pool", lambda e: e.dma_start(
                        out=wd0[:, :, :], in_=dram_ap(w_down, 0, [[D, 128], [128 * D, 4], [1, D]])), writes=[wd0.r], dma=True)
                    for ti, (t0, tn) in enumerate(TTILES):
                        b = ti % 2
                        ci = min(ti // 4, 4)
                        src = x[t0:t0 + tn, :] if ti < 16 else xs[:, :]
                        P.op("sp", (lambda b, tn, src: lambda e: e.dma_start(out=xt[b][0:tn, :], in_=src))(b, tn, src),
                             writes=[xt[b].r], dma=True)
                        while len(pend_d) > 1:
                            pend_d.pop(0)()
                        for hf in range(2):
                            pt = pD[(2 * ti + hf) % 4]

                            def mo(e, t0=t0, tn=tn, hf=hf, pt=pt):
                                ins = None
                                for kk in range(8):
                                    ins = e.matmul(pt[0:tn, :], YU[:, kk, t0:t0 + tn], wo[:, kk, hf * 512:(hf + 1) * 512],
                                                   start=(kk == 0), stop=(kk == 7))
                                return ins
                            P.op("pe", mo, reads=[mT_r[j][ci] for j in range(8)] + wo_r, writes=[pt.r])
                            P.op("dve", (lambda b, tn, hf, pt, ti: lambda e: e.tensor_add(
                                x1[0:tn, ti, hf * 512:(hf + 1) * 512], pt[0:tn, :], xt[b][0:tn, hf * 512:(hf + 1) * 512]))(b, tn, hf, pt, ti),
                                reads=[pt.r, xt[b].r], writes=[x1_r[ti], pt.r])
                        while pend_d:
                            pend_d.pop(0)()
                        P.op("act", (lambda b, tn, ti: lambda e: e.activation(
                            out=sq[b][0:tn, :], in_=x1[0:tn, ti, :], func=AF.Square, accum_out=ssq[b][0:tn, :]))(b, tn, ti),
                            reads=[x1_r[ti]], writes=[sq[b].r, ssq[b].r])
                        P.op("dve", (lambda b, tn: lambda e: e.tensor_scalar(
                            rstd[b][0:tn, :], ssq[b][0:tn, :], 1.0 / D, EPS, ALU.mult, ALU.add))(b, tn),
                            reads=[ssq[b].r], writes=[rstd[b].r])
                        P.op("act", (lambda b, tn: lambda e: e.activation(
                            out=rstd[b][0:tn, :], in_=rstd[b][0:tn, :], func=AF.Sqrt))(b, tn),
                            reads=[rstd[b].r], writes=[rstd[b].r])
                        P.op("dve", (lambda b, tn: lambda e: e.reciprocal(rstd[b][0:tn, :], rstd[b][0:tn, :]))(b, tn),
                             reads=[rstd[b].r], writes=[rstd[b].r])
                        P.op("dve", (lambda b, tn, ti: lambda e: e.scalar_tensor_tensor(
                            out=hb[b][0:tn, :], in0=x1[0:tn, ti, :], scalar=rstd[b][0:tn, 0:1], in1=g2b[0:tn, :],
                            op0=ALU.mult, op1=ALU.mult))(b, tn, ti),
                            reads=[x1_r[ti], rstd[b].r, g2b.r], writes=[hb[b].r])

                        def tr2(e, b=b, tn=tn):
                            ins = None
                            for kk in range(8):
                                ins = e.transpose(ptr[b][:, kk, 0:tn], hb[b][0:tn, kk * 128:(kk + 1) * 128], ident_b[0:tn, 0:tn])
                            return ins
                        def late(tr2=tr2, b=b, t0=t0, tn=tn, ti=ti):
                            P.op("pe", tr2, reads=[hb[b].r], writes=[ptr[b].r])
                            P.op("act", (lambda b, t0, tn: lambda e: e.copy(hT[:, :, t0:t0 + tn], ptr[b][:, :, 0:tn]))(b, t0, tn),
                                 reads=[ptr[b].r], writes=[h2_r[ti], ptr[b].r])
                        pend_d.append(late)
                    for f_ in pend_d:
                        f_()
                    P.flush()

                if dbg == "D":
                    dbg_outs["d_x1"] = (x1, [128, 17 * D], F32)
                    dbg_outs["d_h2T"] = (hT, [128, 8 * NCOL], BF16)
                    _emit_dbg(nc, P, dbg_outs, es)
                    P.flush(final=True)
                    return nc, list(dbg_outs.keys())

                with ExitStack() as sE:
                    NE = 8
                    wu = [wu0, sb("wu1", [128, 8, 512], BF16, sE)]
                    wd = [wd0, sb("wd1", [128, 4, D], BF16, sE)]
                    gfb = sb("gfb", [128, D], F32, sE)
                    rt = [sb("rt%d" % i, [128, 512], BF16, sE) for i in range(2)]
                    aT = [sb("aT%d" % i, [128, 4, 512], BF16, sE) for i in range(2)]
                    aT_r = [[Reg() for _ in range(4)] for _ in range(2)]
                    yo = [sb("yo%d" % i, [128, D], F32, sE) for i in range(2)]
                    sq = [sb("sqe%d" % i, [128, D], BF16, sE) for i in range(2)]
                    ssq = [sb("ssqe%d" % i, [128, 1], F32, sE) for i in range(2)]
                    rstd = [sb("rstde%d" % i, [128, 1], F32, sE) for i in range(2)]
                    pE = [pst("pE%d" % i, [128, 512], F32, sE) for i in range(8)]
                    pek = [0]

                    def nexte():
                        t = pE[pek[0] % 8]
                        pek[0] += 1
                        return t
                    P.op("sp", lambda e: e.dma_start(out=gfb[:, :], in_=nfg.partition_broadcast(128)), writes=[gfb.r], dma=True)
                    def final_norm(ti):
                        t0, tn = TTILES[ti]
                        b = ti % 2
                        P.op("act", (lambda b, tn, ti: lambda e: e.activation(
                            out=sq[b][0:tn, :], in_=x1[0:tn, ti, :], func=AF.Square, accum_out=ssq[b][0:tn, :]))(b, tn, ti),
                            reads=[x1_r[ti]], writes=[sq[b].r, ssq[b].r])
                        P.op("dve", (lambda b, tn: lambda e: e.tensor_scalar(
                            rstd[b][0:tn, :], ssq[b][0:tn, :], 1.0 / D, EPS, ALU.mult, ALU.add))(b, tn),
                            reads=[ssq[b].r], writes=[rstd[b].r])
                        P.op("act", (lambda b, tn: lambda e: e.activation(
                            out=rstd[b][0:tn, :], in_=rstd[b][0:tn, :], func=AF.Sqrt))(b, tn),
                            reads=[rstd[b].r], writes=[rstd[b].r])
                        P.op("dve", (lambda b, tn: lambda e: e.reciprocal(rstd[b][0:tn, :], rstd[b][0:tn, :]))(b, tn),
                             reads=[rstd[b].r], writes=[rstd[b].r])
                        P.op("dve", (lambda b, tn, ti: lambda e: e.scalar_tensor_tensor(
                            out=yo[b][0:tn, :], in0=x1[0:tn, ti, :], scalar=rstd[b][0:tn, 0:1], in1=gfb[0:tn, :],
                            op0=ALU.mult, op1=ALU.mult))(b, tn, ti),
                            reads=[x1_r[ti], rstd[b].r, gfb.r], writes=[yo[b].r])
                        dst = y[t0:t0 + tn, :] if ti < 16 else ysamp[:, :]
                        P.op("sp", (lambda b, tn, dst: lambda e: e.dma_start(out=dst, in_=yo[b][0:tn, :]))(b, tn, dst),
                             reads=[yo[b].r], dma=True, out=True)
                    rkk = [0]
                    pend_fn = []

                    def prefetch(e8):
                        if e8 + 1 < NE:
                            nb_ = (e8 + 1) % 2
                            P.op("pool", (lambda nb_, e8: lambda e: e.dma_start(
                                out=wu[nb_][:, :, :], in_=dram_ap(w_up, (e8 + 1) * 512, [[DFF, 128], [128 * DFF, 8], [1, 512]])))(nb_, e8),
                                writes=[wu[nb_].r], dma=True)
                            P.op("pool", (lambda nb_, e8: lambda e: e.dma_start(
                                out=wd[nb_][:, :, :], in_=dram_ap(w_down, (e8 + 1) * 512 * D, [[D, 128], [128 * D, 4], [1, D]])))(nb_, e8),
                                writes=[wd[nb_].r], dma=True)

                    def emit_up(e8, ci, cb):
                        b = e8 % 2
                        c0, cn = CHUNKS[ci]
                        tiles = [ti for ti, (t0, tn) in enumerate(TTILES) if t0 >= c0 and t0 < c0 + cn]
                        for m in range(4):
                            pt = nexte()

                            def up(e, m=m, c0=c0, cn=cn, pt=pt, b=b):
                                ins = None
                                for kk in range(8):
                                    ins = e.matmul(pt[:, 0:cn], wu[b][:, kk, m * 128:(m + 1) * 128], hT[:, kk, c0:c0 + cn],
                                                   start=(kk == 0), stop=(kk == 7))
                                return ins
                            P.op("pe", up, reads=[h2_r[ti] for ti in tiles] + [wu[b].r], writes=[pt.r])
                            rb = rkk[0] % 2
                            rkk[0] += 1
                            P.op("act", (lambda rb, cn, pt: lambda e: e.activation(out=rt[rb][:, 0:cn], in_=pt[:, 0:cn], func=AF.Relu))(rb, cn, pt),
                                 reads=[pt.r], writes=[rt[rb].r, pt.r])
                            P.op("pool", (lambda rb, cn, cb, m: lambda e: e.tensor_mul(
                                aT[cb][:, m, 0:cn], rt[rb][:, 0:cn], rt[rb][:, 0:cn]))(rb, cn, cb, m),
                                reads=[rt[rb].r], writes=[aT_r[cb][m]])

                    def emit_down(e8, ci, cb):
                        b = e8 % 2
                        c0, cn = CHUNKS[ci]
                        tiles = [ti for ti, (t0, tn) in enumerate(TTILES) if t0 >= c0 and t0 < c0 + cn]
                        for ti in tiles:
                            t0, tn = TTILES[ti]
                            for hf in range(2):
                                pt = nexte()

                                def dn(e, t0=t0, tn=tn, c0=c0, hf=hf, pt=pt, b=b, cb=cb):
                                    ins = None
                                    for kk in range(4):
                                        ins = e.matmul(pt[0:tn, :], aT[cb][:, kk, t0 - c0:t0 - c0 + tn], wd[b][:, kk, hf * 512:(hf + 1) * 512],
                                                       start=(kk == 0), stop=(kk == 3))
                                    return ins
                                P.op("pe", dn, reads=aT_r[cb] + [wd[b].r], writes=[pt.r])
                                P.op("dve", (lambda tn, ti, hf, pt: lambda e: e.tensor_add(
                                    x1[0:tn, ti, hf * 512:(hf + 1) * 512], x1[0:tn, ti, hf * 512:(hf + 1) * 512], pt[0:tn, :]))(tn, ti, hf, pt),
                                    reads=[pt.r, x1_r[ti]], writes=[x1_r[ti], pt.r])
                            if e8 == NE - 1:
                                pend_fn.append(ti)
                                while len(pend_fn) > 2:
                                    final_norm(pend_fn.pop(0))

                    items = [(e8, ci) for e8 in range(NE) for ci in range(len(CHUNKS))]
                    emit_up(items[0][0], items[0][1], 0)
                    for i_, (e8, ci) in enumerate(items):
                        if ci == 0:
                            prefetch(e8)
                        if i_ + 1 < len(items):
                            emit_up(items[i_ + 1][0], items[i_ + 1][1], (i_ + 1) % 2)
                        emit_down(e8, ci, i_ % 2)
                    while pend_fn:
                        final_norm(pend_fn.pop(0))
                    P.flush(final=True)
    return nc, []


def _emit_dbg(nc, P, dbg_outs, es):
    for nm, (t, shp, dt) in dbg_outs.items():
        d = nc.dram_tensor(nm, list(shp), dt, kind="ExternalOutput").ap()
        nd = len(t.t.shape)
        flat = {2: lambda: t.t[:], 3: lambda: t.t[:].rearrange("p a b -> p (a b)"),
                4: lambda: t.t[:].rearrange("p a b c -> p (a b c)"),
                5: lambda: t.t[:].rearrange("p a b c d -> p (a b c d)")}[nd]()
        P.op("sp", (lambda d, flat: lambda e: e.dma_start(out=d[:, :], in_=flat))(d, flat),
             reads=[t.r], writes=[], dma=True, out=True)


def make_in_maps(inp):
    f = lambda a: np.ascontiguousarray(np.asarray(a, dtype=np.float32))
    shared = {
        "n1g": f(inp["norm1_g"]).reshape(1, D),
        "w_in": f(inp["w_in"]).reshape(D, 3072),
        "pool_w": f(inp["pool_w"]).reshape(512, 128),
        "pscale": f(inp["pool_scale"]).reshape(1, PW),
        "pool_out": f(inp["pool_out"]).reshape(PW, D),
        "a_re": f(inp["ssm_a_re"]).reshape(1, 2048),
        "a_im": f(inp["ssm_a_im"]).reshape(1, 2048),
        "log_dt": f(inp["ssm_log_dt"]).reshape(1, 32),
        "b_re": f(inp["ssm_b_re"]).reshape(1, -1),
        "b_im": f(inp["ssm_b_im"]).reshape(1, -1),
        "c_re": f(inp["ssm_c_re"]).reshape(512, 64),
        "c_im": f(inp["ssm_c_im"]).reshape(512, 64),
        "ssm_d": f(inp["ssm_d"]).reshape(1, 512),
        "glu": f(inp["ssm_glu"]).reshape(512, 2048),
        "w_out": f(inp["w_out"]).reshape(D, D),
        "n2g": f(inp["norm2_g"]).reshape(1, D),
        "w_up": f(inp["w_up"]).reshape(D, DFF),
        "w_down": f(inp["w_down"]).reshape(DFF, D),
        "nfg": f(inp["normf_g"]).reshape(1, D),
    }
    xp = f(inp["x_prompt"])
    xsm = f(inp["x_sample"]).reshape(128, D)
    cp = f(inp["cache_pool"]).reshape(128, PB, PW)
    sr = f(inp["state_ssm_re"]).reshape(128, 2048)
    si = f(inp["state_ssm_im"]).reshape(128, 2048)
    maps = []
    for c in range(NCORES):
        m = dict(shared)
        m["x"] = xp[c]
        m["xs"] = xsm[c * NS:(c + 1) * NS]
        m["cpool"] = cp[c * NS:(c + 1) * NS].reshape(NS * PB, PW)
        m["sre"] = sr[c * NS:(c + 1) * NS]
        m["sim"] = si[c * NS:(c + 1) * NS]
        maps.append(m)
    return maps


def kernel(**inp):
    nc, _ = build()
    maps = make_in_maps(inp)
    res = run_bass_kernel_spmd(nc, maps, core_ids=list(range(NCORES)))
    R = res.results
    y_prompt = np.stack([R[c]["y"] for c in range(NCORES)], 0).astype(np.float32)
    y_sample = np.concatenate([R[c]["ysamp"] for c in range(NCORES)], 0).reshape(128, 1, D).astype(np.float32)
    pool_p = np.stack([R[c]["o_pool_p"] for c in range(NCORES)], 0).reshape(1, 8, PB, PW).astype(np.float32)
    re_p = np.stack([R[c]["o_re_p"].reshape(32, 64) for c in range(NCORES)], 0).reshape(1, 8, 32, 64).astype(np.float32)
    im_p = np.stack([R[c]["o_im_p"].reshape(32, 64) for c in range(NCORES)], 0).reshape(1, 8, 32, 64).astype(np.float32)
    pool_s = np.concatenate([R[c]["o_pool_s"].reshape(NS, PB, PW) for c in range(NCORES)], 0).reshape(1, 128, PB, PW).astype(np.float32)
    re_s = np.concatenate([R[c]["o_re_s"].reshape(NS, 32, 64) for c in range(NCORES)], 0).reshape(1, 128, 32, 64).astype(np.float32)
    im_s = np.concatenate([R[c]["o_im_s"].reshape(NS, 32, 64) for c in range(NCORES)], 0).reshape(1, 128, 32, 64).astype(np.float32)
    return (y_prompt, y_sample, pool_p, re_p, im_p, pool_s, re_s, im_s)
```

```python
import math
SKIP = ''
ENG_M = 'pool'
from contextlib import ExitStack

import numpy as np
import concourse.bass as bass
import concourse.mybir as mybir
from concourse.bass_utils import run_bass_kernel_spmd

F32 = mybir.dt.float32
BF16 = mybir.dt.bfloat16
ALU = mybir.AluOpType
AF = mybir.ActivationFunctionType
AX = mybir.AxisListType

NCORES = 8
D = 1024
SEQ = 2048
NS = 16
NCOL = SEQ + NS
PB = 15
PW = 512
DFF = 4096
EPS = 1e-6
PAD = 16
CHUNKS = [(0, 512), (512, 512), (1024, 512), (1536, 512), (2048, NS)]
TTILES = [(i * 128, 128) for i in range(16)] + [(SEQ, NS)]
MLIST = [1, 2, 3, 4, 5, 6, 7, 8, 16, 24, 32, 64, 96, 128, 256, 384, 512, 1024, 1536]
NM = len(MLIST)
MI = {m: i for i, m in enumerate(MLIST)}
TC = 8
NJ = SEQ // TC
TWO_PI = 2.0 * math.pi


class Reg:
    __slots__ = ("w", "rs")

    def __init__(self):
        self.w = None
        self.rs = []


class Op:
    __slots__ = ("eng", "fn", "deps", "dma", "sig", "idx", "sem", "val", "phase")


class Prog:
    ND = 8

    def __init__(self, nc, es):
        self.nc = nc
        self.es = es
        self.ops = []
        self.phase = 0
        self.esem = {}
        self.ecount = {}
        for e in ("pe", "act", "dve", "pool", "sp"):
            self.esem[e] = es.enter_context(nc.semaphore("s_" + e))
            self.ecount[e] = 0
        self.dsem = {}
        self.dcount = {}
        for q in ("sp", "pool", "act"):
            self.dsem[q] = [es.enter_context(nc.semaphore("d_%s%d" % (q, i))) for i in range(self.ND)]
            self.dcount[q] = 0
        self.waited = {e: {} for e in ("pe", "act", "dve", "pool", "sp")}
        self.out_dmas = []

    def op(self, eng, fn, reads=(), writes=(), dma=False, out=False):
        o = Op()
        o.eng = eng
        o.fn = fn
        o.dma = dma
        o.sig = False
        o.idx = 0
        o.sem = None
        o.val = 0
        o.phase = self.phase
        deps = []
        for r in reads:
            if r.w is not None:
                deps.append(r.w)
        for w in writes:
            if w.w is not None:
                deps.append(w.w)
            deps.extend(w.rs)
        for r in reads:
            r.rs.append(o)
        for w in writes:
            w.w = o
            w.rs = []
        seen = set()
        o.deps = []
        for d in deps:
            if d is o or id(d) in seen:
                continue
            seen.add(id(d))
            o.deps.append(d)
        self.ops.append(o)
        if out:
            self.out_dmas.append(o)
        return o

    def interleave(self, n_head, n0):
        ops = self.ops
        head, a, b = ops[:n_head], ops[n_head:n0], ops[n0:]
        out = list(head)
        ia = 0
        for ib, o in enumerate(b):
            tgt = (ib + 1) * len(a) // max(len(b), 1)
            while ia < tgt:
                out.append(a[ia])
                ia += 1
            out.append(o)
        out.extend(a[ia:])
        self.ops = out

    def flush(self, final=False):
        nc = self.nc
        ops = self.ops
        self.ops = []
        if True:
            fence = Op()
            fence.eng = "sp"
            fence.fn = None
            fence.dma = False
            fence.sig = False
            fence.idx = 0
            fence.sem = None
            fence.val = 0
            fence.phase = self.phase
            fence.deps = [o for o in ops if o.dma] + (list(self.out_dmas) if final else [])
            ops.append(fence)
        for o in ops:
            o.deps = [d for d in o.deps if d.dma or d.phase == self.phase]
        for o in ops:
            for d in o.deps:
                if d.dma:
                    continue
                if d.eng == "pe" and o.eng == "pe":
                    continue
                d.sig = True
        for o in ops:
            if o.fn is None:
                continue
            if o.dma:
                q = o.eng
                j = self.dcount[q]
                self.dcount[q] = j + 1
                o.sem = self.dsem[q][j % self.ND]
                o.val = 16 * (j // self.ND + 1)
            elif o.sig:
                self.ecount[o.eng] += 1
                o.idx = self.ecount[o.eng]
        per = {e: [o for o in ops if o.eng == e] for e in ("pe", "act", "dve", "pool", "sp")}

        def emit(engname, eng):
            wd = self.waited[engname]
            for o in per[engname]:
                need = {}
                for d in o.deps:
                    if d.dma:
                        s, v = d.sem, d.val
                    else:
                        if d.eng == "pe" and o.eng == "pe":
                            continue
                        s, v = self.esem[d.eng], d.idx
                    key = id(s)
                    if key not in need or need[key][1] < v:
                        need[key] = (s, v)
                if o.dma and o.val > 16:
                    key = id(o.sem)
                    v = o.val - 16
                    if key not in need or need[key][1] < v:
                        need[key] = (o.sem, v)
                for key, (s, v) in need.items():
                    if wd.get(key, 0) >= v:
                        continue
                    eng.wait_ge(s, v)
                    wd[key] = v
                if o.fn is None:
                    continue
                ins = o.fn(eng)
                if o.dma:
                    ins.then_inc(o.sem, 16)
                elif o.sig:
                    ins.then_inc(self.esem[engname], 1)

        with nc.Block() as block:
            if per["pe"]:
                @block.tensor
                def _(eng):
                    emit("pe", eng)
            if per["act"]:
                @block.scalar
                def _(eng):
                    emit("act", eng)
            if per["dve"]:
                @block.vector
                def _(eng):
                    emit("dve", eng)
            if per["pool"]:
                @block.gpsimd
                def _(eng):
                    emit("pool", eng)
            if per["sp"]:
                @block.sync
                def _(eng):
                    emit("sp", eng)
        self.phase += 1


class Sub:
    def __init__(self, tl, lo):
        self.tl = tl
        self.lo = lo

    def __getitem__(self, k):
        p, q, c = k
        return self.tl.t[p, self.lo + q, c]


class Tl:
    def __init__(self, t):
        self.t = t
        self.r = Reg()

    def __getitem__(self, k):
        return self.t[k]


def build(dbg=None):
    nc = bass.Bass("TRN2", target_bir_lowering=False)

    def din(name, shape):
        return nc.dram_tensor(name, list(shape), F32, kind="ExternalInput").ap()

    def dout(name, shape):
        return nc.dram_tensor(name, list(shape), F32, kind="ExternalOutput").ap()

    x = din("x", [SEQ, D])
    xs = din("xs", [NS, D])
    cpool = din("cpool", [NS * PB, PW])
    sre = din("sre", [NS, 2048])
    sim = din("sim", [NS, 2048])
    n1g = din("n1g", [1, D])
    w_in = din("w_in", [D, 3072])
    pool_w = din("pool_w", [512, 128])
    pscale = din("pscale", [1, PW])
    pool_out = din("pool_out", [PW, D])
    a_re = din("a_re", [1, 2048])
    a_im = din("a_im", [1, 2048])
    log_dt = din("log_dt", [1, 32])
    b_re = din("b_re", [1, 32 * 64 * 16])
    b_im = din("b_im", [1, 32 * 64 * 16])
    c_re = din("c_re", [512, 64])
    c_im = din("c_im", [512, 64])
    ssm_d = din("ssm_d", [1, 512])
    glu = din("glu", [512, 2048])
    w_out = din("w_out", [D, D])
    n2g = din("n2g", [1, D])
    w_up = din("w_up", [D, DFF])
    w_down = din("w_down", [DFF, D])
    nfg = din("nfg", [1, D])

    y = dout("y", [SEQ, D])
    ysamp = dout("ysamp", [NS, D])
    o_pool_p = dout("o_pool_p", [PB, PW])
    o_re_p = dout("o_re_p", [16, 128])
    o_im_p = dout("o_im_p", [16, 128])
    o_pool_s = dout("o_pool_s", [NS * PB, PW])
    o_re_s = dout("o_re_s", [NS, 2048])
    o_im_s = dout("o_im_s", [NS, 2048])

    g1s = nc.dram_tensor("g1s", [128, 8, NCOL], BF16, kind="Internal").ap()
    g1s_r = [Reg() for _ in CHUNKS]
    g2s = nc.dram_tensor("g2s", [128, 8, NCOL], BF16, kind="Internal").ap()
    g2s_r = [Reg() for _ in CHUNKS]
    dbg_outs = {}

    with ExitStack() as es:
        P = Prog(nc, es)

        def sb(name, shape, dt, stack=None):
            return Tl((stack or es).enter_context(nc.sbuf_tensor(name, list(shape), dt)))

        def pst(name, shape, dt, stack=None):
            return Tl((stack or es).enter_context(nc.psum_tensor(name, list(shape), dt)))

        def dram_ap(base, offset, pattern):
            return bass.AP(base.tensor, offset, pattern)

        ident_f = sb("ident_f", [128, 128], F32)
        ident_b = sb("ident_b", [128, 128], BF16)
        dcol = sb("dcol", [128, 4], F32)
        pscol = sb("pscol", [128, 4], F32)
        rct = sb("rct", [128, 16], F32)
        hT = sb("hT", [128, 8, NCOL], BF16)
        YU = sb("YU", [128, 8, NCOL], BF16)
        sT = es.enter_context(ExitStack())
        lam = sb("lam", [128, 16, NM, 3], F32, sT)
        BbTs = sb("BbTs", [128, 2, 4, TC, 128], BF16, sT)
        CT = sb("CT", [128, 2, 16, 32], BF16, sT)
        Wt = sb("Wt", [128, 2, TC, 16, 32], BF16, sT)
        Kt = sb("Kt", [128, 4, TC, 32], BF16, sT)
        Hnb = sb("Hnb", [128, 2, 16, 16], BF16, sT)
        Ktf = sb("Ktf", [128, 4, TC, 128], BF16, sT)

        sW = es.enter_context(ExitStack())
        wvu = sb("wvu", [128, 8, 1024], BF16, sW)
        s0 = es.enter_context(ExitStack())
        if True:
            ar = sb("ar", [128, 16], F32, s0)
            ai = sb("ai", [128, 16], F32, s0)
            ldt = sb("ldt", [128, 16], F32, s0)
            dtt = sb("dtt", [128, 16], F32, s0)
            adt = sb("adt", [128, 16], F32, s0)
            wdt = sb("wdt", [128, 16], F32, s0)
            MT = sb("MT", [128, 16, NM], F32, s0)
            ARG = sb("ARG", [128, 16, NM], F32, s0)
            EXA = sb("EXA", [128, 16, NM], F32, s0)
            MAG = sb("MAG", [128, 16, NM], F32, s0)
            SA = sb("SA", [128, 16, NM], F32, s0)
            CA = sb("CA", [128, 16, NM], F32, s0)
            SN = sb("SN", [128, 16, NM], F32, s0)
            CS = sb("CS", [128, 16, NM], F32, s0)
            negpi = sb("negpi", [128, 1], F32, s0)
            t1 = sb("t1", [128, 16], F32, s0)
            t2 = sb("t2", [128, 16], F32, s0)
            t3 = sb("t3", [128, 16], F32, s0)
            t4 = sb("t4", [128, 16], F32, s0)
            cre = sb("cre", [128, 16], F32, s0)
            cim = sb("cim", [128, 16], F32, s0)
            Bn_re = sb("Bn_re", [128, 16, 16], F32, s0)
            Bn_im = sb("Bn_im", [128, 16, 16], F32, s0)
            Bb_re = sb("Bb_re", [128, 16, 16], F32, s0)
            Bb_im = sb("Bb_im", [128, 16, 16], F32, s0)
            Bt1 = sb("Bt1", [128, 16, 16], F32, s0)
            Bt2 = sb("Bt2", [128, 16, 16], F32, s0)
            Bpad = sb("Bpad", [128, 2, 16, 32], F32, s0)
            Cin = [sb("Cin%d" % i, [128, 4, 128], F32, s0) for i in range(2)]
            iot = sb("iot", [128, 16], F32, s0)
            ps0 = [pst("ps0_%d" % i, [128, 512], F32, s0) for i in range(4)]

            P.op("pool", lambda e: e.memset(ident_f[:, :], 0.0), writes=[ident_f.r])
            P.op("pool", lambda e: e.affine_select(
                out=ident_f[:, :], in_=ident_f[:, :], pattern=[[-1, 128]], compare_op=ALU.not_equal,
                fill=1.0, base=0, channel_multiplier=1), reads=[ident_f.r], writes=[ident_f.r])
            P.op("dve", lambda e: e.tensor_copy(ident_b[:, :], ident_f[:, :]), reads=[ident_f.r], writes=[ident_b.r])
            P.op("pool", lambda e: e.iota(iot[:, :], [[1, 16]], base=1, channel_multiplier=0,
                                          allow_small_or_imprecise_dtypes=True), writes=[iot.r])
            P.op("dve", lambda e: e.reciprocal(rct[:, :], iot[:, :]), reads=[iot.r], writes=[rct.r])
            P.op("dve", lambda e: e.memset(negpi[:, :], -math.pi), writes=[negpi.r])

            def bcast_row(dst, src_row, n):
                return lambda e: e.dma_start(out=dst[:, :], in_=src_row.partition_broadcast(128))

            def small_t(dst, src):
                def f(e):
                    with nc.allow_non_contiguous_dma(reason="tiny param load"):
                        return e.dma_start(out=dst, in_=src)
                return f
            prow = sb("prow", [16, 4, 128], F32, s0)
            Lb = sb("Lb", [128, 32], F32, s0)
            P.op("pool", lambda e: e.memset(prow[:, :, :], 0.0), writes=[prow.r])
            P.op("sp", lambda e: e.dma_start(out=prow[0:16, 0, :], in_=dram_ap(a_re, 0, [[128, 16], [1, 128]])), writes=[prow.r], dma=True)
            P.op("sp", lambda e: e.dma_start(out=prow[0:16, 1, :], in_=dram_ap(a_im, 0, [[128, 16], [1, 128]])), writes=[prow.r], dma=True)
            P.op("sp", lambda e: e.dma_start(out=prow[0:4, 2, :], in_=dram_ap(ssm_d, 0, [[128, 4], [1, 128]])), writes=[prow.r], dma=True)
            P.op("sp", lambda e: e.dma_start(out=prow[0:4, 3, :], in_=dram_ap(pscale, 0, [[128, 4], [1, 128]])), writes=[prow.r], dma=True)
            P.op("sp", lambda e: e.dma_start(out=Lb[:, :], in_=log_dt.partition_broadcast(128)), writes=[Lb.r], dma=True)
            ptp = ps0[3]

            def ptr_params(e):
                ins = None
                for w_ in range(4):
                    ins = e.transpose(ptp[:, w_ * 16:(w_ + 1) * 16], prow[0:16, w_, :], ident_f[0:16, 0:16])
                return ins
            P.op("pe", ptr_params, reads=[prow.r, ident_f.r], writes=[ptp.r])
            P.op("act", lambda e: e.copy(ar[:, :], ptp[:, 0:16]), reads=[ptp.r], writes=[ar.r, ptp.r])
            P.op("act", lambda e: e.copy(ai[:, :], ptp[:, 16:32]), reads=[ptp.r], writes=[ai.r, ptp.r])
            P.op("act", lambda e: e.copy(dcol[:, :], ptp[:, 32:36]), reads=[ptp.r], writes=[dcol.r, ptp.r])
            P.op("act", lambda e: e.copy(pscol[:, :], ptp[:, 48:52]), reads=[ptp.r], writes=[pscol.r, ptp.r])
            r_ldt = [Reg(), Reg()]
            for h in range(2):
                P.op("dve", (lambda h: lambda e: e.tensor_copy(ldt[h * 64:(h + 1) * 64, :], Lb[h * 64:(h + 1) * 64, h:32:2]))(h),
                     reads=[Lb.r], writes=[r_ldt[h]])
            P.op("sp", lambda e: e.dma_start(out=Bn_re[:, :, :], in_=dram_ap(b_re, 0, [[16, 128], [2048, 16], [1, 16]])),
                 writes=[Bn_re.r], dma=True)
            P.op("sp", lambda e: e.dma_start(out=Bn_im[:, :, :], in_=dram_ap(b_im, 0, [[16, 128], [2048, 16], [1, 16]])),
                 writes=[Bn_im.r], dma=True)

            P.op("act", lambda e: e.activation(out=dtt[:, :], in_=ldt[:, :], func=AF.Exp),
                 reads=r_ldt, writes=[dtt.r])
            P.op("dve", lambda e: e.tensor_mul(adt[:, :], ar[:, :], dtt[:, :]), reads=[ar.r, dtt.r], writes=[adt.r])
            P.op("dve", lambda e: e.tensor_mul(wdt[:, :], ai[:, :], dtt[:, :]), reads=[ai.r, dtt.r], writes=[wdt.r])
            for mi, m in enumerate(MLIST):
                P.op("pool", (lambda mi, m: lambda e: e.memset(MT[:, :, mi], float(m)))(mi, m), writes=[MT.r])
            P.op("dve", lambda e: e.tensor_mul(ARG[:, :, :], MT[:, :, :], wdt[:, :].unsqueeze(2).to_broadcast([128, 16, NM])),
                 reads=[MT.r, wdt.r], writes=[ARG.r])
            P.op("dve", lambda e: e.tensor_mul(EXA[:, :, :], MT[:, :, :], adt[:, :].unsqueeze(2).to_broadcast([128, 16, NM])),
                 reads=[MT.r, adt.r], writes=[EXA.r])
            P.op("act", lambda e: e.activation(out=MAG[:, :, :], in_=EXA[:, :, :], func=AF.Exp),
                 reads=[EXA.r], writes=[MAG.r])
            KI = sb("KI", [128, 16, NM], mybir.dt.int32, s0)
            KF = sb("KF", [128, 16, NM], F32, s0)
            MK = sb("MK", [128, 16, NM], F32, s0)

            def sin_of(dst, A, shift):
                P.op("dve", lambda e: e.tensor_scalar(A[:, :, :], ARG[:, :, :], float(shift), None, ALU.add),
                     reads=[ARG.r], writes=[A.r])
                P.op("dve", lambda e: e.tensor_scalar(KI[:, :, :], A[:, :, :], 1.0 / TWO_PI, None, ALU.mult),
                     reads=[A.r], writes=[KI.r])
                P.op("dve", lambda e: e.tensor_copy(KF[:, :, :], KI[:, :, :]), reads=[KI.r], writes=[KF.r])
                P.op("dve", lambda e: e.scalar_tensor_tensor(out=A[:, :, :], in0=KF[:, :, :], scalar=-TWO_PI, in1=A[:, :, :],
                                                             op0=ALU.mult, op1=ALU.add),
                     reads=[KF.r, A.r], writes=[A.r])
                P.op("dve", lambda e: e.tensor_scalar(MK[:, :, :], A[:, :, :], math.pi, TWO_PI, ALU.is_gt, ALU.mult),
                     reads=[A.r], writes=[MK.r])
                P.op("dve", lambda e: e.tensor_sub(A[:, :, :], A[:, :, :], MK[:, :, :]), reads=[A.r, MK.r], writes=[A.r])
                P.op("dve", lambda e: e.tensor_scalar(MK[:, :, :], A[:, :, :], -math.pi, TWO_PI, ALU.is_lt, ALU.mult),
                     reads=[A.r], writes=[MK.r])
                P.op("dve", lambda e: e.tensor_add(A[:, :, :], A[:, :, :], MK[:, :, :]), reads=[A.r, MK.r], writes=[A.r])
                P.op("dve", lambda e: e.tensor_scalar(A[:, :, :], A[:, :, :], -math.pi, math.pi, ALU.max, ALU.min),
                     reads=[A.r], writes=[A.r])
                P.op("act", lambda e: e.activation(out=dst[:, :, :], in_=A[:, :, :], func=AF.Sin),
                     reads=[A.r], writes=[dst.r])
            sin_of(SN, SA, 0.0)
            sin_of(CS, CA, 0.5 * math.pi)
            r_lre, r_lim, r_lni = Reg(), Reg(), Reg()
            P.op("dve", lambda e: e.tensor_mul(lam[:, :, :, 0], MAG[:, :, :], CS[:, :, :]), reads=[MAG.r, CS.r], writes=[r_lre])
            P.op("dve", lambda e: e.tensor_mul(lam[:, :, :, 1], MAG[:, :, :], SN[:, :, :]), reads=[MAG.r, SN.r], writes=[r_lim])
            P.op("dve", lambda e: e.tensor_scalar(lam[:, :, :, 2], lam[:, :, :, 1], -1.0, None, ALU.mult),
                 reads=[r_lim], writes=[r_lni])
            lam_regs = [r_lre, r_lim, r_lni]

            lr1 = lam[:, :, 0, 0]
            li1 = lam[:, :, 0, 1]
            P.op("dve", lambda e: e.tensor_scalar(t1[:, :], lr1, -1.0, None, ALU.add), reads=[r_lre], writes=[t1.r])
            P.op("dve", lambda e: e.tensor_mul(t2[:, :], ar[:, :], ar[:, :]), reads=[ar.r], writes=[t2.r])
            P.op("dve", lambda e: e.tensor_mul(t3[:, :], ai[:, :], ai[:, :]), reads=[ai.r], writes=[t3.r])
            P.op("dve", lambda e: e.tensor_add(t2[:, :], t2[:, :], t3[:, :]), reads=[t2.r, t3.r], writes=[t2.r])
            P.op("dve", lambda e: e.reciprocal(t2[:, :], t2[:, :]), reads=[t2.r], writes=[t2.r])
            P.op("dve", lambda e: e.tensor_mul(t3[:, :], t1[:, :], ar[:, :]), reads=[t1.r, ar.r], writes=[t3.r])
            P.op("dve", lambda e: e.tensor_mul(t4[:, :], li1, ai[:, :]), reads=[r_lim, ai.r], writes=[t4.r])
            P.op("dve", lambda e: e.tensor_add(t3[:, :], t3[:, :], t4[:, :]), reads=[t3.r, t4.r], writes=[t3.r])
            P.op("dve", lambda e: e.tensor_mul(cre[:, :], t3[:, :], t2[:, :]), reads=[t3.r, t2.r], writes=[cre.r])
            P.op("dve", lambda e: e.tensor_mul(t3[:, :], li1, ar[:, :]), reads=[r_lim, ar.r, cre.r], writes=[t3.r])
            P.op("dve", lambda e: e.tensor_mul(t4[:, :], t1[:, :], ai[:, :]), reads=[t1.r, ai.r, t3.r], writes=[t4.r])
            P.op("dve", lambda e: e.tensor_sub(t3[:, :], t3[:, :], t4[:, :]), reads=[t3.r, t4.r], writes=[t3.r])
            P.op("dve", lambda e: e.tensor_mul(cim[:, :], t3[:, :], t2[:, :]), reads=[t3.r, t2.r], writes=[cim.r])

            creb = cre[:, :].unsqueeze(2).to_broadcast([128, 16, 16])
            cimb = cim[:, :].unsqueeze(2).to_broadcast([128, 16, 16])
            P.op("dve", lambda e: e.tensor_mul(Bt1[:, :, :], Bn_re[:, :, :], creb), reads=[Bn_re.r, cre.r], writes=[Bt1.r])
            P.op("dve", lambda e: e.tensor_mul(Bt2[:, :, :], Bn_im[:, :, :], cimb), reads=[Bn_im.r, cim.r], writes=[Bt2.r])
            P.op("dve", lambda e: e.tensor_sub(Bb_re[:, :, :], Bt1[:, :, :], Bt2[:, :, :]), reads=[Bt1.r, Bt2.r], writes=[Bb_re.r])
            P.op("dve", lambda e: e.tensor_mul(Bt1[:, :, :], Bn_im[:, :, :], creb), reads=[Bn_im.r, cre.r, Bb_re.r], writes=[Bt1.r])
            P.op("dve", lambda e: e.tensor_mul(Bt2[:, :, :], Bn_re[:, :, :], cimb), reads=[Bn_re.r, cim.r, Bb_re.r], writes=[Bt2.r])
            P.op("dve", lambda e: e.tensor_add(Bb_im[:, :, :], Bt1[:, :, :], Bt2[:, :, :]), reads=[Bt1.r, Bt2.r], writes=[Bb_im.r])
            P.op("pool", lambda e: e.memset(Bpad[:, :, :, :], 0.0), writes=[Bpad.r])
            for ri, src in enumerate((Bb_re, Bb_im)):
                for h in range(2):
                    P.op("dve", (lambda ri, src, h: lambda e: e.tensor_copy(
                        Bpad[h * 64:(h + 1) * 64, ri, :, h * 16:(h + 1) * 16], src[h * 64:(h + 1) * 64, :, :]))(ri, src, h),
                        reads=[src.r], writes=[Bpad.r])
            CTf = sb("CTf", [128, 2, 16, 32], F32, s0)
            Bpb = sb("Bpb", [128, 2, 16, 32], BF16, s0)
            BL = [sb("BL%d" % i, [128, 2, 16, 32], F32, s0) for i in range(1)]
            Wtmp = [sb("Wtmp%d" % i, [128, 16, 32], F32, s0) for i in range(4)]
            I32 = sb("I32", [128, 32], F32, s0)
            wk = [0]

            def nxtw():
                t = Wtmp[wk[0] % 4]
                wk[0] += 1
                return t

            def lb(m, c):
                return lam[:, :, MI[m], c].unsqueeze(2).to_broadcast([128, 16, 32])
            k = 0
            for m in range(TC if 'b' not in SKIP else 1):
                if m == 0:
                    srcB = Bpad
                else:
                    srcB = BL[0]
                    w1, w2 = nxtw(), nxtw()
                    P.op("dve", (lambda m, srcB: lambda e: e.tensor_tensor(out=srcB[:, 0, :, :], in0=Bpad[:, 0, :, :], in1=lb(m, 0), op=ALU.mult))(m, srcB),
                         reads=[Bpad.r] + lam_regs, writes=[srcB.r])
                    P.op("dve", (lambda m, w1: lambda e: e.tensor_tensor(out=w1[:, :, :], in0=Bpad[:, 1, :, :], in1=lb(m, 2), op=ALU.mult))(m, w1),
                         reads=[Bpad.r] + lam_regs, writes=[w1.r])
                    P.op("dve", (lambda srcB, w1: lambda e: e.tensor_add(srcB[:, 0, :, :], srcB[:, 0, :, :], w1[:, :, :]))(srcB, w1),
                         reads=[srcB.r, w1.r], writes=[srcB.r])
                    P.op("dve", (lambda m, srcB: lambda e: e.tensor_tensor(out=srcB[:, 1, :, :], in0=Bpad[:, 0, :, :], in1=lb(m, 1), op=ALU.mult))(m, srcB),
                         reads=[Bpad.r] + lam_regs, writes=[srcB.r])
                    P.op("dve", (lambda m, w2: lambda e: e.tensor_tensor(out=w2[:, :, :], in0=Bpad[:, 1, :, :], in1=lb(m, 0), op=ALU.mult))(m, w2),
                         reads=[Bpad.r] + lam_regs, writes=[w2.r])
                    P.op("dve", (lambda srcB, w2: lambda e: e.tensor_add(srcB[:, 1, :, :], srcB[:, 1, :, :], w2[:, :, :]))(srcB, w2),
                         reads=[srcB.r, w2.r], writes=[srcB.r])
                for ri in range(2):
                    for gq in range(4):
                        pt = ps0[k % 4]
                        k += 1
                        P.op("pe", (lambda ri, gq, pt, srcB: lambda e: e.transpose(
                            pt[:, 0:128], srcB[:, ri, gq * 4:(gq + 1) * 4, :], ident_f[:, :]))(ri, gq, pt, srcB),
                            reads=[srcB.r, ident_f.r], writes=[pt.r])
                        P.op("act", (lambda ri, gq, pt, m: lambda e: e.copy(BbTs[:, ri, gq, TC - 1 - m, :], pt[:, 0:128]))(ri, gq, pt, m),
                             reads=[pt.r], writes=[BbTs.r])
            P.op("dve", lambda e: e.tensor_copy(Bpb[:, :, :, :], Bpad[:, :, :, :]), reads=[Bpad.r], writes=[Bpb.r])

            Cnat = [sb("Cnat%d" % i, [128, 4, 64], F32, s0) for i in range(2)]
            pidx = sb("pidx", [128, 1], mybir.dt.int32, s0)
            mko = sb("mko", [128, 1], F32, s0)
            mke = sb("mke", [128, 1], F32, s0)
            P.op("pool", lambda e: e.iota(pidx[:, :], [[0, 1]], base=0, channel_multiplier=1), writes=[pidx.r])
            P.op("dve", lambda e: e.tensor_scalar(pidx[:, :], pidx[:, :], 4, 1, ALU.arith_shift_right, ALU.bitwise_and),
                 reads=[pidx.r], writes=[pidx.r])
            P.op("dve", lambda e: e.tensor_copy(mko[:, :], pidx[:, :]), reads=[pidx.r], writes=[mko.r])
            P.op("dve", lambda e: e.tensor_scalar(mke[:, :], mko[:, :], -1.0, 1.0, ALU.mult, ALU.add), reads=[mko.r], writes=[mke.r])
            for ri, csrc in enumerate((c_re, c_im)):
                cin = Cin[ri]
                cn_ = Cnat[ri]
                P.op("sp", (lambda cn_, csrc: lambda e: e.dma_start(
                    out=cn_[:, :, :], in_=dram_ap(csrc, 0, [[64, 128], [128 * 64, 4], [1, 64]])))(cn_, csrc),
                    writes=[cn_.r], dma=True)
                P.op("dve", (lambda cin, cn_: lambda e: e.tensor_scalar(cin[:, :, 0:64], cn_[:, :, :], mke[:, 0:1], None, ALU.mult))(cin, cn_),
                     reads=[cn_.r, mke.r], writes=[cin.r])
                P.op("dve", (lambda cin, cn_: lambda e: e.tensor_scalar(cin[:, :, 64:128], cn_[:, :, :], mko[:, 0:1], None, ALU.mult))(cin, cn_),
                     reads=[cn_.r, mko.r], writes=[cin.r])
                for r in range(4):
                    pt = ps0[k % 4]
                    k += 1
                    P.op("pe", (lambda cin, r, pt: lambda e: e.transpose(pt[:, 0:128], cin[:, r, :], ident_f[:, :]))(cin, r, pt),
                         reads=[cin.r, ident_f.r], writes=[pt.r])
                    pv = pt[:, 0:128].rearrange("p (a c) -> p a c", c=32)
                    if ri == 0:
                        P.op("act", (lambda r, pv: lambda e: e.copy(CT[:, 0, r * 4:(r + 1) * 4, :], pv))(r, pv),
                             reads=[pt.r], writes=[CT.r, pt.r])
                    else:
                        P.op("act", (lambda r, pv: lambda e: e.mul(CT[:, 1, r * 4:(r + 1) * 4, :], pv, -1.0))(r, pv),
                             reads=[pt.r], writes=[CT.r, pt.r])
                    P.op("dve", (lambda ri, r, pv: lambda e: e.tensor_copy(CTf[:, ri, r * 4:(r + 1) * 4, :], pv))(ri, r, pv),
                         reads=[pt.r], writes=[CTf.r, pt.r])

            for kk in range(TC if 'c' not in SKIP else 0):
                m = kk + 1
                w1, w2, w3, w4 = nxtw(), nxtw(), nxtw(), nxtw()
                P.op("dve", (lambda m, w1: lambda e: e.tensor_tensor(out=w1[:, :, :], in0=CTf[:, 0, :, :], in1=lb(m, 0), op=ALU.mult))(m, w1),
                     reads=[CTf.r] + lam_regs, writes=[w1.r])
                P.op("dve", (lambda m, w2: lambda e: e.tensor_tensor(out=w2[:, :, :], in0=CTf[:, 1, :, :], in1=lb(m, 2), op=ALU.mult))(m, w2),
                     reads=[CTf.r] + lam_regs, writes=[w2.r])
                P.op("dve", (lambda kk, w1, w2: lambda e: e.tensor_add(Wt[:, 0, kk, :, :], w1[:, :, :], w2[:, :, :]))(kk, w1, w2),
                     reads=[w1.r, w2.r], writes=[Wt.r])
                P.op("dve", (lambda m, w3: lambda e: e.tensor_tensor(out=w3[:, :, :], in0=CTf[:, 0, :, :], in1=lb(m, 2), op=ALU.mult))(m, w3),
                     reads=[CTf.r] + lam_regs, writes=[w3.r])
                P.op("dve", (lambda m, w4: lambda e: e.tensor_tensor(out=w4[:, :, :], in0=CTf[:, 1, :, :], in1=lb(m, 0), op=ALU.mult))(m, w4),
                     reads=[CTf.r] + lam_regs, writes=[w4.r])
                P.op("dve", (lambda kk, w3, w4: lambda e: e.tensor_sub(Wt[:, 1, kk, :, :], w3[:, :, :], w4[:, :, :]))(kk, w3, w4),
                     reads=[w3.r, w4.r], writes=[Wt.r])

            for gq in range(4 if 't' not in SKIP else 0):
                pt = ps0[gq]

                def kmm(e, gq=gq, pt=pt):
                    ins = None
                    for a4 in range(4):
                        gp = gq * 4 + a4
                        R0, R1 = a4 * 32, a4 * 32 + 32
                        e.matmul(pt[R0:R1, 0:32], Bpb[:, 0, gp, :], CT[:, 0, gp, :], start=True, stop=False, tile_position=(0, R0))
                        e.matmul(pt[R0:R1, 0:32], Bpb[:, 1, gp, :], CT[:, 1, gp, :], start=False, stop=True, tile_position=(0, R0))
                        e.matmul(pt[R0:R1, 32:256], Bpb[:, 0, gp, :], Wt[:, 0, 0:TC - 1, gp, :], start=True, stop=False, tile_position=(0, R0))
                        ins = e.matmul(pt[R0:R1, 32:256], Bpb[:, 1, gp, :], Wt[:, 1, 0:TC - 1, gp, :], start=False, stop=True, tile_position=(0, R0))
                    return ins
                if "k" not in SKIP:
                    P.op("pe", kmm, reads=[Bpb.r, CT.r, Wt.r], writes=[pt.r])
                P.op("act", (lambda gq, pt: lambda e: e.copy(Kt[:, gq, :, :], pt[:, 0:256].rearrange("p (t c) -> p t c", c=32)))(gq, pt),
                     reads=[pt.r], writes=[Kt.r])
            P.op("dve", lambda e: e.tensor_add(I32[:, :], ident_f[:, 0:32], ident_f[:, 32:64]), reads=[ident_f.r], writes=[I32.r])
            P.op("dve", lambda e: e.tensor_add(I32[:, :], I32[:, :], ident_f[:, 64:96]), reads=[ident_f.r, I32.r], writes=[I32.r])
            P.op("dve", lambda e: e.tensor_add(I32[:, :], I32[:, :], ident_f[:, 96:128]), reads=[ident_f.r, I32.r], writes=[I32.r])
            for gq in range(4 if 't' not in SKIP else 0):
                P.op("dve", (lambda gq: lambda e: e.scalar_tensor_tensor(
                    out=Kt[:, gq, 0, :], in0=I32[:, :], scalar=dcol[:, gq:gq + 1], in1=Kt[:, gq, 0, :],
                    op0=ALU.mult, op1=ALU.add))(gq), reads=[I32.r, Kt.r, dcol.r], writes=[Kt.r])
            P.op("pool", lambda e: e.memset(Ktf[:, :, :, :], 0.0), writes=[Ktf.r])
            for a4 in range(4):
                P.op("dve", (lambda a4: lambda e: e.tensor_copy(
                    Ktf[a4 * 32:(a4 + 1) * 32, :, :, a4 * 32:(a4 + 1) * 32], Kt[a4 * 32:(a4 + 1) * 32, :, :, :]))(a4),
                    reads=[Kt.r], writes=[Ktf.r])
        n_ops_p0 = len(P.ops)

        with ExitStack() as sAC:
            ypT = Sub(YU, 0)
            uT = Sub(YU, 4)
            hT_r = [Reg() for _ in TTILES]
            yp_r = [[Reg() for _ in CHUNKS] for _ in range(4)]
            ys_r = [[Reg() for _ in CHUNKS] for _ in range(4)]

            with ExitStack() as sAB:
                vT_r = [[Reg() for _ in CHUNKS] for _ in range(4)]
                uT_r = [[Reg() for _ in CHUNKS] for _ in range(4)]
                vpad_r = [Reg() for _ in range(4)]
                vTs_r = [Reg() for _ in range(4)]

                with ExitStack() as sA:
                    g1b = sb("g1b", [128, D], F32, sA)
                    P.op("sp", lambda e: e.dma_start(out=g1b[:, :], in_=n1g.partition_broadcast(128)), writes=[g1b.r], dma=True)
                    xt = [sb("xt%d" % i, [128, D], F32, sA) for i in range(2)]
                    sq = [sb("sq%d" % i, [128, D], BF16, sA) for i in range(1)] * 2
                    hb = [sb("hb%d" % i, [128, D], BF16, sA) for i in range(2)]
                    ssq = [sb("ssq%d" % i, [128, 1], F32, sA) for i in range(2)]
                    rstd = [sb("rstd%d" % i, [128, 1], F32, sA) for i in range(2)]
                    ptr = [pst("ptr%d" % i, [128, 8, 128], BF16, sA) for i in range(2)]
                    pmm = [pst("pmm%d" % i, [128, 512], F32, sA) for i in range(2)]

                    wvu_r = [Reg() for _ in range(8)]
                    for kk in range(8):
                        P.op("pool", (lambda kk: lambda e: e.dma_start(
                            out=wvu[:, kk, :], in_=w_in[kk * 128:(kk + 1) * 128, 0:1024]))(kk),
                            writes=[wvu_r[kk]], dma=True)

                    pend_a = []
                    for ti, (t0, tn) in enumerate(TTILES):
                        b = ti % 2
                        src = x[t0:t0 + tn, :] if ti < 16 else xs[:, :]
                        P.op("sp", (lambda b, tn, src: lambda e: e.dma_start(out=xt[b][0:tn, :], in_=src))(b, tn, src),
                             writes=[xt[b].r], dma=True)
                        P.op("act", (lambda b, tn: lambda e: e.activation(
                            out=sq[b][0:tn, :], in_=xt[b][0:tn, :], func=AF.Square, accum_out=ssq[b][0:tn, :]))(b, tn),
                            reads=[xt[b].r], writes=[sq[b].r, ssq[b].r])
                        P.op("dve", (lambda b, tn: lambda e: e.tensor_scalar(
                            rstd[b][0:tn, :], ssq[b][0:tn, :], 1.0 / D, EPS, ALU.mult, ALU.add))(b, tn),
                            reads=[ssq[b].r], writes=[rstd[b].r])
                        P.op("act", (lambda b, tn: lambda e: e.activation(
                            out=rstd[b][0:tn, :], in_=rstd[b][0:tn, :], func=AF.Sqrt))(b, tn),
                            reads=[rstd[b].r], writes=[rstd[b].r])
                        P.op("dve", (lambda b, tn: lambda e: e.reciprocal(rstd[b][0:tn, :], rstd[b][0:tn, :]))(b, tn),
                             reads=[rstd[b].r], writes=[rstd[b].r])
                        P.op("dve", (lambda b, tn: lambda e: e.scalar_tensor_tensor(
                            out=hb[b][0:tn, :], in0=xt[b][0:tn, :], scalar=rstd[b][0:tn, 0:1], in1=g1b[0:tn, :],
                            op0=ALU.mult, op1=ALU.mult))(b, tn),
                            reads=[xt[b].r, rstd[b].r, g1b.r], writes=[hb[b].r])

                        def tr(e, b=b, tn=tn):
                            ins = None
                            for kk in range(8):
                                ins = e.transpose(ptr[b][:, kk, 0:tn], hb[b][0:tn, kk * 128:(kk + 1) * 128], ident_b[0:tn, 0:tn])
                            return ins
                        def late_a(tr=tr, b=b, t0=t0, tn=tn, ti=ti):
                            P.op("pe", tr, reads=[hb[b].r, ident_b.r], writes=[ptr[b].r])
                            P.op("act", (lambda b, t0, tn: lambda e: e.copy(hT[:, :, t0:t0 + tn], ptr[b][:, :, 0:tn]))(b, t0, tn),
                                 reads=[ptr[b].r], writes=[hT_r[ti], ptr[b].r])
                        while pend_a:
                            pend_a.pop(0)()
                        pend_a.append(late_a)
                    while pend_a:
                        pend_a.pop(0)()

                    k = 0
                    for ci, (c0, cn) in enumerate(CHUNKS):
                        tiles = [ti for ti, (t0, tn) in enumerate(TTILES) if t0 >= c0 and t0 < c0 + cn]
                        for m in range(4, 8):
                            pt = pmm[k % 2]
                            k += 1

                            def mmf(e, m=m, c0=c0, cn=cn, pt=pt):
                                ins = None
                                for kk in range(8):
                                    ins = e.matmul(pt[:, 0:cn], wvu[:, kk, m * 128:(m + 1) * 128], hT[:, kk, c0:c0 + cn],
                                                   start=(kk == 0), stop=(kk == 7))
                                return ins
                            P.op("pe", mmf, reads=[hT_r[ti] for ti in tiles] + wvu_r, writes=[pt.r])
                            P.op("dve", (lambda m, c0, cn, pt: lambda e: e.tensor_copy(
                                uT[:, m - 4, c0:c0 + cn], pt[:, 0:cn]))(m, c0, cn, pt),
                                reads=[pt.r], writes=[uT_r[m - 4][ci], pt.r])
                    P.interleave(8, n_ops_p0)
                    P.flush()

                s0.close()
                if dbg == "A":
                    dbg_outs["d_hT"] = (hT, [128, 8 * NCOL], BF16)
                    dbg_outs["d_YU"] = (YU, [128, 8 * NCOL], BF16)
                    dbg_outs["d_lam"] = (lam, [128, 16 * NM * 3], F32)
                    dbg_outs["d_BbTs"] = (BbTs, [128, 2 * 4 * TC * 128], BF16)
                    dbg_outs["d_CT"] = (CT, [128, 2 * 16 * 32], BF16)
                    dbg_outs["d_Wt"] = (Wt, [128, 2 * TC * 16 * 32], BF16)
                    dbg_outs["d_Kt"] = (Kt, [128, 4 * TC * 32], BF16)
                    _emit_dbg(nc, P, dbg_outs, es)
                    P.flush(final=True)
                    return nc, list(dbg_outs.keys())

                with ExitStack() as sB:
                    vT = sb("vT", [128, 4, PAD + SEQ], F32, sB)
                    vTs = sb("vTs", [128, 4, NS, 16], F32, sB)
                    pw_b = sb("pw_b", [128, 4, 128], BF16, sB)
                    Abuf = sb("Abuf", [128, PAD + SEQ], F32, sB)
                    Bbuf = sb("Bbuf", [128, PAD + SEQ], F32, sB)
                    ptmp = sb("ptmp", [128, 16], F32, sB)
                    wsum = sb("wsum", [128, 16], F32, sB)
                    cpl = [sb("cpl%d" % i, [120, 512], F32, sB) for i in range(2)]
                    tok15 = sb("tok15", [16, 512], F32, sB)
                    toks = sb("toks", [16, 512], F32, sB)
                    H0 = sb("H0", [128, 2, 16, 16], F32, sB)
                    Hn = sb("Hn", [128, 2, 16, 16], F32, sB)
                    BUs = sb("BUs", [128, 2, 16, 16], F32, sB)
                    T1 = sb("T1", [128, 16, 16], F32, sB)
                    T2 = sb("T2", [128, 16, 16], F32, sB)
                    srow = [Abuf, Bbuf]
                    pB = [pst("pB%d" % i, [128, 512], F32, sB) for i in range(8)]
                    pbk = [0]

                    def nextp():
                        t = pB[pbk[0] % 8]
                        pbk[0] += 1
                        return t

                    pw_r = [Reg() for _ in range(4)]
                    for q in range(4):
                        P.op("pool", (lambda q: lambda e: e.dma_start(out=pw_b[:, q, :], in_=pool_w[q * 128:(q + 1) * 128, :]))(q),
                             writes=[pw_r[q]], dma=True)
                    for j in range(2):
                        P.op("sp", (lambda j: lambda e: e.dma_start(out=cpl[j][:, :], in_=cpool[j * 120:(j + 1) * 120, :]))(j),
                             writes=[cpl[j].r], dma=True)
                    if 'd' not in SKIP:
                      P.op("sp", lambda e: e.dma_start(
                        out=dram_ap(o_pool_s, 0, [[PB * PW, NS], [1, 14 * PW]]),
                        in_=dram_ap(cpool, PW, [[PB * PW, NS], [1, 14 * PW]])), dma=True, out=True)
                    for j in range(2 if 't' not in SKIP else 0):
                        for q in range(4):
                            pt = nextp()
                            P.op("pe", (lambda j, q, pt: lambda e: e.transpose(
                                pt[:, 0:120], cpl[j][0:120, q * 128:(q + 1) * 128], ident_f[0:120, 0:120]))(j, q, pt),
                                reads=[cpl[j].r], writes=[pt.r])
                            P.op("act", (lambda j, q, pt: lambda e: e.copy(
                                vTs[:, q, j * 8:(j + 1) * 8, 0:15], pt[:, 0:120].rearrange("p (b r) -> p b r", r=15)))(j, q, pt),
                                reads=[pt.r], writes=[vTs_r[q]])

                    P.op("dve", lambda e: e.memset(vT[:, :, 0:PAD], 0.0), writes=vpad_r)
                    for m in range(4):
                        for ci, (c0, cn) in enumerate(CHUNKS):
                            pt = nextp()

                            def mmv(e, m=m, c0=c0, cn=cn, pt=pt):
                                ins = None
                                for kk in range(8):
                                    ins = e.matmul(pt[:, 0:cn], wvu[:, kk, m * 128:(m + 1) * 128], hT[:, kk, c0:c0 + cn],
                                                   start=(kk == 0), stop=(kk == 7))
                                return ins
                            P.op("pe", mmv, writes=[pt.r])
                            if ci < 4:
                                P.op("act", (lambda m, c0, cn, pt: lambda e: e.copy(
                                    vT[:, m, PAD + c0:PAD + c0 + cn], pt[:, 0:cn]))(m, c0, cn, pt),
                                    reads=[pt.r], writes=[vT_r[m][ci], pt.r])
                            else:
                                P.op("act", (lambda m, pt: lambda e: e.copy(vTs[:, m, :, 15], pt[:, 0:NS]))(m, pt),
                                     reads=[pt.r], writes=[vT_r[m][ci], pt.r])
                    P.op("dve", lambda e: e.memset(Abuf[:, 0:PAD], 0.0), writes=[Abuf.r])
                    P.op("dve", lambda e: e.memset(Bbuf[:, 0:PAD], 0.0), writes=[Bbuf.r])

                    for q in range(4):
                        w = 2 ** (q + 1)
                        allv = [vT_r[q][ci] for ci in range(4)] + [vpad_r[q]]
                        src = vT
                        P.op("dve", (lambda q: lambda e: e.tensor_add(
                            Abuf[:, PAD:], vT[:, q, PAD:], vT[:, q, PAD - 1:PAD + SEQ - 1]))(q),
                            reads=allv, writes=[Abuf.r])
                        cur, oth = Abuf, Bbuf
                        sh = 2
                        while sh < w:
                            P.op("dve", (lambda cur, oth, sh: lambda e: e.tensor_add(
                                oth[:, PAD:], cur[:, PAD:], cur[:, PAD - sh:PAD + SEQ - sh]))(cur, oth, sh),
                                reads=[cur.r], writes=[oth.r])
                            cur, oth = oth, cur
                            sh *= 2
                        P.op("dve", (lambda q, cur, w: lambda e: e.scalar_tensor_tensor(
                            out=ypT[:, q, 0:SEQ], in0=cur[:, PAD:], scalar=1.0 / w, in1=vT[:, q, PAD:],
                            op0=ALU.mult, op1=ALU.subtract))(q, cur, w),
                            reads=[cur.r] + allv, writes=[yp_r[q][ci] for ci in range(4)])
                        if w > 1:
                            P.op("dve", (lambda q, cur, w: lambda e: e.tensor_mul(
                                ptmp[:, 0:w - 1], cur[:, PAD:PAD + w - 1], rct[:, 0:w - 1]))(q, cur, w),
                                reads=[cur.r], writes=[ptmp.r])
                            P.op("dve", (lambda q, w: lambda e: e.tensor_sub(
                                ypT[:, q, 0:w - 1], ptmp[:, 0:w - 1], vT[:, q, PAD:PAD + w - 1]))(q, w),
                                reads=[ptmp.r] + allv, writes=[yp_r[q][0]])
                        P.op("dve", (lambda q, w: lambda e: e.tensor_reduce(
                            out=wsum[:, :], in_=vTs[:, q, :, 16 - w:16], axis=AX.X, op=ALU.add))(q, w),
                            reads=[vTs_r[q], vT_r[q][4]], writes=[wsum.r])
                        P.op("dve", (lambda q, w: lambda e: e.scalar_tensor_tensor(
                            out=ypT[:, q, SEQ:NCOL], in0=wsum[:, :], scalar=1.0 / w, in1=vTs[:, q, :, 15],
                            op0=ALU.mult, op1=ALU.subtract))(q, w),
                            reads=[wsum.r, vT_r[q][4]], writes=[yp_r[q][4]])
                        if 'o' in SKIP:
                            continue
                        pt = nextp()
                        P.op("pe", (lambda q, pt: lambda e: e.transpose(
                            pt[0:16, 0:128], vT[:, q, PAD + SEQ - 16:PAD + SEQ], ident_f[:, :]))(q, pt),
                            reads=[vT_r[q][3]], writes=[pt.r])
                        P.op("act", (lambda q, pt: lambda e: e.copy(tok15[0:16, q * 128:(q + 1) * 128], pt[0:16, 0:128]))(q, pt),
                             reads=[pt.r], writes=[tok15.r])
                        pt = nextp()
                        P.op("pe", (lambda q, pt: lambda e: e.transpose(
                            pt[0:16, 0:128], vTs[:, q, :, 15], ident_f[:, :]))(q, pt),
                            reads=[vT_r[q][4]], writes=[pt.r])
                        P.op("act", (lambda q, pt: lambda e: e.copy(toks[0:16, q * 128:(q + 1) * 128], pt[0:16, 0:128]))(q, pt),
                             reads=[pt.r], writes=[toks.r])
                        for ci, (c0, cn) in enumerate(CHUNKS):
                            pt = nextp()
                            P.op("pe", (lambda q, c0, cn, pt: lambda e: e.matmul(
                                pt[:, 0:cn], pw_b[:, q, :], ypT[:, q, c0:c0 + cn], start=True, stop=True))(q, c0, cn, pt),
                                reads=[pw_r[q], yp_r[q][ci]], writes=[pt.r])
                            P.op("act", (lambda q, c0, cn, pt: lambda e: e.mul(
                                ypT[:, q, c0:c0 + cn], pt[:, 0:cn], pscol[:, q:q + 1]))(q, c0, cn, pt),
                                reads=[pt.r], writes=[yp_r[q][ci]])
                    P.op("sp", lambda e: e.dma_start(out=o_pool_p[:, :], in_=tok15[1:16, :]), reads=[tok15.r], dma=True, out=True)
                    P.op("sp", lambda e: e.dma_start(out=dram_ap(o_pool_s, 14 * PW, [[PB * PW, NS], [1, PW]]), in_=toks[0:16, :]),
                         reads=[toks.r], dma=True, out=True)

                    if dbg == "B1":
                        P.flush()
                        dbg_outs["d_YU"] = (YU, [128, 8 * NCOL], BF16)
                        _emit_dbg(nc, P, dbg_outs, es)
                        P.flush(final=True)
                        return nc, list(dbg_outs.keys()) + ["o_pool_p", "o_pool_s"]
                    P.op("sp", lambda e: e.dma_start(out=srow[0][0:16, 0:2048], in_=sre[:, :]), writes=[srow[0].r], dma=True)
                    P.op("sp", lambda e: e.dma_start(out=srow[1][0:16, 0:2048], in_=sim[:, :]), writes=[srow[1].r], dma=True)
                    for ri in range(2):
                        pt = nextp()

                        def trs(e, ri=ri, pt=pt):
                            ins = None
                            for gp in range(16):
                                ins = e.transpose(pt[:, gp * 16:(gp + 1) * 16], srow[ri][0:16, gp * 128:(gp + 1) * 128], ident_f[0:16, 0:16])
                            return ins
                        if '1' not in SKIP:
                            P.op("pe", trs, reads=[srow[ri].r], writes=[pt.r])
                        P.op("act", (lambda ri, pt: lambda e: e.copy(
                            H0[:, ri, :, :], pt[:, 0:256].rearrange("p (g b) -> p g b", b=16)))(ri, pt),
                            reads=[pt.r], writes=[H0.r])
                    for ri in range(2):
                        for a4 in range(4):
                            pt = nextp()

                            def bus(e, ri=ri, pt=pt, a4=a4):
                                ins = None
                                for gq in range(4):
                                    ins = e.matmul(pt[:, gq * 16:(gq + 1) * 16], BbTs[a4 * 32:(a4 + 1) * 32, ri, gq, TC - 1, :],
                                                   uT[a4 * 32:(a4 + 1) * 32, gq, SEQ:NCOL], start=True, stop=True,
                                                   tile_position=(a4 * 32, 0))
                                return ins
                            if '2' not in SKIP:
                                P.op("pe", bus, reads=[uT_r[gq][4] for gq in range(4)], writes=[pt.r])
                            P.op("act", (lambda ri, pt, a4: lambda e: e.copy(
                                BUs[:, ri, a4:16:4, :], pt[:, 0:64].rearrange("p (g b) -> p g b", b=16)))(ri, pt, a4),
                                reads=[pt.r], writes=[BUs.r])
                    lreb = lam[:, :, 0, 0].unsqueeze(2).to_broadcast([128, 16, 16])
                    limb = lam[:, :, 0, 1].unsqueeze(2).to_broadcast([128, 16, 16])
                    P.op("dve", lambda e: e.tensor_mul(T1[:, :, :], H0[:, 0, :, :], lreb), reads=[H0.r], writes=[T1.r])
                    P.op("dve", lambda e: e.tensor_mul(T2[:, :, :], H0[:, 1, :, :], limb), reads=[H0.r], writes=[T2.r])
                    P.op("dve", lambda e: e.tensor_sub(T1[:, :, :], T1[:, :, :], T2[:, :, :]), reads=[T1.r, T2.r], writes=[T1.r])
                    P.op("dve", lambda e: e.tensor_add(Hn[:, 0, :, :], T1[:, :, :], BUs[:, 0, :, :]), reads=[T1.r, BUs.r], writes=[Hn.r])
                    P.op("dve", lambda e: e.tensor_mul(T1[:, :, :], H0[:, 1, :, :], lreb), reads=[H0.r, Hn.r], writes=[T1.r])
                    P.op("dve", lambda e: e.tensor_mul(T2[:, :, :], H0[:, 0, :, :], limb), reads=[H0.r, Hn.r], writes=[T2.r])
                    P.op("dve", lambda e: e.tensor_add(T1[:, :, :], T1[:, :, :], T2[:, :, :]), reads=[T1.r, T2.r], writes=[T1.r])
                    r_hn1 = Reg()
                    P.op("dve", lambda e: e.tensor_add(Hn[:, 1, :, :], T1[:, :, :], BUs[:, 1, :, :]), reads=[T1.r, BUs.r, Hn.r], writes=[r_hn1])
                    P.op("act", lambda e: e.copy(Hnb[:, :, :, :], Hn[:, :, :, :]), reads=[Hn.r, r_hn1], writes=[Hnb.r])
                    for ri in range(2):
                        for g4 in range(4):
                            pt = nextp()

                            def tro(e, ri=ri, g4=g4, pt=pt):
                                ins = None
                                for a in range(4):
                                    gp = g4 * 4 + a
                                    ins = e.transpose(pt[0:16, a * 128:(a + 1) * 128], Hn[:, ri, gp, :], ident_f[:, :])
                                return ins
                            if '3' not in SKIP:
                                P.op("pe", tro, reads=[Hn.r, r_hn1], writes=[pt.r])
                            P.op("act", (lambda ri, g4, pt: lambda e: e.copy(
                                srow[ri][0:16, g4 * 512:(g4 + 1) * 512], pt[0:16, 0:512]))(ri, g4, pt),
                                reads=[pt.r], writes=[srow[ri].r])
                    P.op("sp", lambda e: e.dma_start(out=o_re_s[:, :], in_=srow[0][0:16, 0:2048]), reads=[srow[0].r], dma=True, out=True)
                    P.op("sp", lambda e: e.dma_start(out=o_im_s[:, :], in_=srow[1][0:16, 0:2048]), reads=[srow[1].r], dma=True, out=True)

                    if dbg == "B2":
                        P.flush()
                        dbg_outs["d_YU"] = (YU, [128, 8 * NCOL], BF16)
                        _emit_dbg(nc, P, dbg_outs, es)
                        P.flush(final=True)
                        return nc, list(dbg_outs.keys()) + ["o_pool_p", "o_pool_s", "o_re_s", "o_im_s"]
                    P.flush()

            sW.close()
            with ExitStack() as sS:
                Eall = sb("Eall", [128, 2, 16, NJ], F32, sS)
                Sbf = sb("Sbf", [128, 2, 16, NJ], BF16, sS)
                uD = sb("uD", [128, 4, TC, NJ], BF16, sS)
                wg1 = sb("wg1", [128, 8, 1024], BF16, sS)
                stg = [sb("stg%d" % i, [128, 512], BF16, sS) for i in range(2)]
                wg1_r = [Reg() for _ in range(8)]
                for kk in range(8):
                    P.op("pool", (lambda kk: lambda e: e.dma_start(out=wg1[:, kk, :], in_=w_in[kk * 128:(kk + 1) * 128, 1024:2048]))(kk),
                         writes=[wg1_r[kk]], dma=True)
                tmpS = [sb("tmpS%d" % i, [128, 16, 64], F32, sS) for i in range(2)]
                tmpSs = [sb("tmpSs%d" % i, [128, 16, 16], F32, sS) for i in range(2)]
                Hfin = sb("Hfin", [128, 2, 16], F32, sS)
                fin = sb("fin", [16, 2, 128], F32, sS)
                pS = pst("pS", [128, 8, 512], F32, sS)
                bank_r = [[Reg() for _ in range(4)] for _ in range(8)]
                uTg_r = [Reg() for _ in range(16)]
                LV2 = [(4 ** l, 4 ** l - 1, NJ // (4 ** l)) for l in range(5)]
                cls = [dict() for _ in range(2)]
                for ri in range(2):
                    for l in range(4):
                        for kk in range(3):
                            cls[ri][(l, kk)] = Reg()
                    cls[ri][(4, 0)] = Reg()
                allcls = [list(cls[ri].values()) for ri in range(2)]

                def deeper(ri, l):
                    return [r for (ll, kk), r in cls[ri].items() if ll > l]

                uD_r = [Reg() for _ in range(4)]
                uTq_r = [Reg() for _ in range(4)]
                for gq in range(4):
                    src_v = YU.t[:, 4 + gq, 0:SEQ].rearrange("p (j k) -> p k j", k=TC)
                    if gq % 2 == 0:
                        P.op("act", (lambda gq, src_v: lambda e: e.copy(uD[:, gq, :, :], src_v))(gq, src_v), reads=[uTq_r[gq]], writes=[uD_r[gq]])
                    else:
                        P.op("pool", (lambda gq, src_v: lambda e: e.tensor_copy(uD[:, gq, :, :], src_v))(gq, src_v), reads=[uTq_r[gq]], writes=[uD_r[gq]])
                bbts_r = [Reg()]
                wg2_v = BbTs.t[:].rearrange("p a b c d -> p (a b c d)").rearrange("p (k c) -> p k c", c=1024)
                idx = 0
                for gq in range(4):
                    for ri in range(2):
                        bks = [(idx * 4 + a4) % 8 for a4 in range(4)]
                        idx += 1

                        def emm(e, ri=ri, bks=bks, gq=gq):
                            ins = None
                            for s_ in range(TC):
                                for a4 in range(4):
                                    R0, R1 = a4 * 32, a4 * 32 + 32
                                    ins = e.matmul(pS[:, bks[a4], 0:NJ], BbTs[R0:R1, ri, gq, s_, :], uD[R0:R1, gq, s_, :],
                                                   start=(s_ == 0), stop=(s_ == TC - 1), tile_position=(R0, 0))
                            return ins
                        wr = []
                        for a4 in range(4):
                            wr += bank_r[bks[a4]]
                        P.op("pe", emm, reads=[uD_r[gq]] + bbts_r, writes=wr)
                        for a4 in range(4):
                            gp = gq * 4 + a4
                            P.op("act" if a4 % 2 == 0 else "dve", (lambda ri, gp, bk: lambda e: (
                                e.copy(Eall[:, ri, gp, :].rearrange("p (k jj) -> p jj k", k=4),
                                       pS[:, bk, 0:NJ].rearrange("p (jj k) -> p jj k", k=4)) if hasattr(e, "activation")
                                else e.tensor_copy(Eall[:, ri, gp, :].rearrange("p (k jj) -> p jj k", k=4),
                                                   pS[:, bk, 0:NJ].rearrange("p (jj k) -> p jj k", k=4))))(ri, gp, bks[a4]),
                                reads=[], writes=allcls[ri] + bank_r[bks[a4]])

                it_g_ = [0]
                gate_mm_regs = [Reg()]

                def gate_half(half):
                    gs_, gs_r = ((g1s, g1s_r), (g2s, g2s_r))[half]
                    wsel = wg1 if half == 0 else None
                    for ci, (c0, cn) in enumerate(CHUNKS):
                        for j in range(8):
                            bk = 6 + it_g_[0] % 2
                            sb_ = stg[it_g_[0] % 2]
                            it_g_[0] += 1

                            def g1mm(e, j=j, c0=c0, cn=cn, bk=bk, half=half):
                                ins = None
                                for kk in range(8):
                                    wv = wg1[:, kk, j * 128:(j + 1) * 128] if half == 0 else wg2_v[:, kk, j * 128:(j + 1) * 128]
                                    ins = e.matmul(pS[:, bk, 0:cn], wv, hT[:, kk, c0:c0 + cn],
                                                   start=(kk == 0), stop=(kk == 7))
                                return ins
                            P.op("pe", g1mm, reads=(wg1_r if half == 0 else bbts_r) + gate_mm_regs, writes=bank_r[bk])
                            P.op("act", (lambda cn, bk, sb_: lambda e: e.activation(out=sb_[:, 0:cn], in_=pS[:, bk, 0:cn], func=AF.Sigmoid))(cn, bk, sb_),
                                 reads=[], writes=bank_r[bk] + [sb_.r])
                            P.op("sp", (lambda j, c0, cn, sb_, gs_: lambda e: e.dma_start(out=gs_[:, j, c0:c0 + cn], in_=sb_[:, 0:cn]))(j, c0, cn, sb_, gs_),
                                 reads=[sb_.r], writes=[gs_r[ci]], dma=True)

                for kk in range(8):
                    P.op("pool", (lambda kk: lambda e: e.dma_start(out=wg2_v[:, kk, :], in_=w_in[kk * 128:(kk + 1) * 128, 2048:3072]))(kk),
                         writes=bbts_r, dma=True)
                gate_half(0)
                gate_half(1)
                tk = [0]

                def cstepb(m, tgt, srcv, nel, treg, sreg):
                    mi = MI[m]
                    prods = ((0, 0, 0), (1, 2, 0), (0, 1, 1), (1, 0, 1))
                    for (sri, lc, tri) in prods:
                        if nel <= 16:
                            tb_ = (tmpS + tmpSs)[tk[0] % 4]
                        else:
                            tb_ = tmpS[tk[0] % 2]
                        tk[0] += 1
                        Sv = Eall[:, sri, :, srcv]
                        Tv = Eall[:, tri, :, tgt]
                        lv = lam[:, :, mi, lc].unsqueeze(2).to_broadcast([128, 16, nel])
                        tv = tb_[:, :, 0:nel]
                        if nel >= 16 and ENG_M == "pool":
                            GS = 10
                            Sv1, Sv2 = Eall[:, sri, 0:GS, srcv], Eall[:, sri, GS:16, srcv]
                            lv1 = lam[:, 0:GS, mi, lc].unsqueeze(2).to_broadcast([128, GS, nel])
                            lv2 = lam[:, GS:16, mi, lc].unsqueeze(2).to_broadcast([128, 16 - GS, nel])
                            tv1, tv2 = tb_[:, 0:GS, 0:nel], tb_[:, GS:16, 0:nel]
                            r1, r2 = Reg(), Reg()
                            P.op("pool", (lambda tv1, Sv1, lv1: lambda e: e.tensor_tensor(out=tv1, in0=Sv1, in1=lv1, op=ALU.mult))(tv1, Sv1, lv1),
                                 reads=sreg(sri) + [tb_.r], writes=[r1])
                            P.op("dve", (lambda tv2, Sv2, lv2: lambda e: e.tensor_tensor(out=tv2, in0=Sv2, in1=lv2, op=ALU.mult))(tv2, Sv2, lv2),
                                 reads=sreg(sri) + [tb_.r], writes=[r2])
                            P.op("dve", (lambda tv, Tv: lambda e: e.tensor_add(Tv, Tv, tv))(tv, Tv),
                                 reads=[r1, r2] + treg(tri), writes=treg(tri) + [tb_.r])
                            continue
                        P.op(ENG_M, (lambda tv, Sv, lv: lambda e: e.tensor_tensor(out=tv, in0=Sv, in1=lv, op=ALU.mult))(tv, Sv, lv),
                             reads=sreg(sri), writes=[tb_.r])
                        P.op("dve", (lambda tv, Tv: lambda e: e.tensor_add(Tv, Tv, tv))(tv, Tv),
                             reads=[tb_.r] + treg(tri), writes=treg(tri))

                QB = NJ // 4

                def sl(l, kk, j0, cnt):
                    if l == 0:
                        return slice(kk * QB + j0, kk * QB + j0 + cnt)
                    sg = 4 ** (l - 1)
                    first = 3 * QB + (sg - 1) + sg * (4 * j0 + kk)
                    return slice(first, first + 4 * sg * (cnt - 1) + 1, 4 * sg)

                def scan_level(l):
                    sig, off, n = LV2[l]
                    if n == 1:
                        return
                    nch = n // 4
                    for kk in (1, 2, 3):
                        tgt = sl(l, kk, 0, nch)
                        srcv = sl(l, kk - 1, 0, nch)
                        if kk < 3:
                            treg = (lambda kk: lambda ri: [cls[ri][(l, kk)]])(kk)
                        else:
                            treg = lambda ri: deeper(ri, l)
                        sreg = (lambda kk: lambda ri: [cls[ri][(l, kk - 1)]])(kk)
                        cstepb(TC * sig, tgt, srcv, nch, treg, sreg)
                    scan_level(l + 1)
                    if nch > 1:
                        for kk in (0, 1, 2):
                            tgt = sl(l, kk, 1, nch - 1)
                            srcv = sl(l, 3, 0, nch - 1)
                            treg = (lambda kk: lambda ri: [cls[ri][(l, kk)]])(kk)
                            sreg = lambda ri: deeper(ri, l)
                            cstepb(TC * sig * (kk + 1), tgt, srcv, nch - 1, treg, sreg)
                scan_level(0)

                P.op("pool", lambda e: e.memset(Sbf[:, :, :, 0:1], 0.0), writes=[Sbf.r])
                for ri in range(2):
                    P.op("dve", (lambda ri: lambda e: e.tensor_copy(Hfin[:, ri, :], Eall[:, ri, :, NJ - 1]))(ri),
                         reads=allcls[ri], writes=[Hfin.r])
                    for k4 in range(4):
                        cnt = QB if k4 < 3 else QB - 1
                        P.op("dve" if k4 % 2 == 0 else "pool", (lambda ri, k4, cnt: lambda e: e.tensor_copy(
                            Sbf[:, ri, :, 1 + k4:1 + k4 + 4 * (cnt - 1) + 1:4], Eall[:, ri, :, k4 * QB:k4 * QB + cnt]))(ri, k4, cnt),
                            reads=allcls[ri], writes=[Sbf.r])
                for ri in range(2):
                    bk = ri
                    P.op("pe", (lambda ri, bk: lambda e: e.transpose(pS[0:16, bk, 0:128], Hfin[:, ri, :], ident_f[:, :]))(ri, bk),
                         reads=[Hfin.r], writes=bank_r[bk])
                    P.op("act", (lambda ri, bk: lambda e: e.copy(fin[0:16, ri, :], pS[0:16, bk, 0:128]))(ri, bk),
                         reads=bank_r[bk], writes=[fin.r])
                P.op("sp", lambda e: e.dma_start(out=o_re_p[:, :], in_=fin[0:16, 0, :]), reads=[fin.r], dma=True, out=True)
                P.op("sp", lambda e: e.dma_start(out=o_im_p[:, :], in_=fin[0:16, 1, :]), reads=[fin.r], dma=True, out=True)

                for gq in range(4):
                    bb = (gq % 2) * 4

                    def ymm2(e, gq=gq, bb=bb):
                        ins = None
                        for kk in range(TC):
                            ov = pS[:, bb + kk // 2, (kk % 2) * NJ:(kk % 2) * NJ + NJ]
                            for a4 in range(4):
                                gp = gq * 4 + a4
                                R0, R1 = a4 * 32, a4 * 32 + 32
                                ovq = pS[R0:R1, bb + kk // 2, (kk % 2) * NJ:(kk % 2) * NJ + NJ]
                                e.matmul(ovq, Wt[:, 0, kk, gp, :], Sbf[:, 0, gp, :], start=True, stop=False,
                                         tile_position=(0, R0))
                                e.matmul(ovq, Wt[:, 1, kk, gp, :], Sbf[:, 1, gp, :], start=False, stop=False,
                                         tile_position=(0, R0))
                            for s_ in range(kk + 1):
                                ins = e.matmul(ov, Ktf[:, gq, kk - s_, :], uD[:, gq, s_, :], start=False, stop=(s_ == kk))
                        return ins
                    wr = []
                    for b4 in range(4):
                        wr += bank_r[bb + b4]
                    P.op("pe", ymm2, reads=[Sbf.r, uD_r[gq], fin.r], writes=wr)
                    yo_v = YU.t[:, 4 + gq, 0:SEQ].rearrange("p (j k) -> p k j", k=TC)
                    pv_ = pS[:, bb:bb + 4, :].rearrange("p b (h j) -> p (b h) j", j=NJ)
                    P.op("act", (lambda yo_v, pv_: lambda e: e.copy(yo_v, pv_))(yo_v, pv_),
                         reads=[], writes=wr + [uTg_r[gq * 4 + a4] for a4 in range(4)] + [uTq_r[gq]])
                for gp in range(16):
                    gq, a4 = gp // 4, gp % 4
                    R0, R1 = a4 * 32, a4 * 32 + 32
                    bk = gp % 8

                    def ysm(e, gp=gp, bk=bk, R0=R0, R1=R1):
                        e.matmul(pS[R0:R1, bk, 0:NS], CT[:, 0, gp, :], Hnb[:, 0, gp, :], start=True, stop=False, tile_position=(0, R0))
                        return e.matmul(pS[R0:R1, bk, 0:NS], CT[:, 1, gp, :], Hnb[:, 1, gp, :], start=False, stop=True,
                                        tile_position=(0, R0))
                    P.op("pe", ysm, reads=[uTg_r[gp]], writes=[bank_r[bk][a4]])
                    P.op("dve", (lambda bk, gq=gq, R0=R0, R1=R1: lambda e: e.scalar_tensor_tensor(
                        out=uT[R0:R1, gq, SEQ:NCOL], in0=uT[R0:R1, gq, SEQ:NCOL], scalar=dcol[R0:R1, gq:gq + 1],
                        in1=pS[R0:R1, bk, 0:NS], op0=ALU.mult, op1=ALU.add))(bk),
                        reads=[bank_r[bk][a4], uTg_r[gp]], writes=[uTg_r[gp]])
                hTflat = hT.t[:].rearrange("p a b -> p (a b)")
                po_v = hTflat[:, 0:4096].rearrange("p (k c) -> p k c", c=1024)
                gl_v = hTflat[:, 4096:12288].rearrange("p (k c) -> p k c", c=2048)
                hT_all = [Reg()]
                for kk in range(4):
                    P.op("pool", (lambda kk: lambda e: e.dma_start(out=po_v[:, kk, :], in_=pool_out[kk * 128:(kk + 1) * 128, :]))(kk),
                         writes=hT_all + gate_mm_regs, dma=True)
                for kk in range(4):
                    P.op("pool", (lambda kk: lambda e: e.dma_start(out=gl_v[:, kk, :], in_=glu[kk * 128:(kk + 1) * 128, :]))(kk),
                         writes=hT_all + gate_mm_regs, dma=True)
                P.flush()
            sT.close()

            if dbg == "B":
                dbg_outs["d_YU"] = (YU, [128, 8 * NCOL], BF16)
                _emit_dbg(nc, P, dbg_outs, es)
                P.flush(final=True)
                return nc, list(dbg_outs.keys()) + ["o_pool_p", "o_re_p", "o_im_p", "o_pool_s", "o_re_s", "o_im_s"]
            mT = YU
            mT_r = [[(yp_r[j][ci] if j < 4 else uT_r[j - 4][ci]) for ci in range(len(CHUNKS))] for j in range(8)]
            with ExitStack() as sC:
                sg1 = [sb("sg1_%d" % i, [128, 8, 512], BF16, sC) for i in range(2)]
                sg2 = [sb("sg2_%d" % i, [128, 8, 512], BF16, sC) for i in range(2)]
                class _V:
                    def __init__(self, ap):
                        self.ap = ap

                    def __getitem__(self, k):
                        return self.ap[k]
                po = _V(po_v)
                gl = _V(gl_v)
                mtmp = [sb("mtmp%d" % i, [128, 8, 512], BF16, sC) for i in range(2)]
                sz2 = [sb("sz2%d" % i, [128, 512], BF16, sC) for i in range(4)]
                ta = [sb("ta%d" % i, [128, 512], F32, sC) for i in range(4)]
                tb = [sb("tb%d" % i, [128, 512], F32, sC) for i in range(4)]
                pC = [pst("pC%d" % i, [128, 512], F32, sC) for i in range(8)]
                pck = [0]

                def nextc():
                    t = pC[pck[0] % 8]
                    pck[0] += 1
                    return t
                wg_r = [Reg() for _ in range(8)]
                po_r = hT_all
                gl_r = hT_all
                it = 0

                def load_sg1(ci):
                    c0, cn = CHUNKS[ci]
                    t_ = sg1[ci % 2]
                    P.op("sp", (lambda c0, cn, t_: lambda e: e.dma_start(out=t_[:, :, 0:cn], in_=g1s[:, :, c0:c0 + cn]))(c0, cn, t_),
                         reads=[g1s_r[ci]], writes=[t_.r], dma=True)
                    t2_ = sg2[ci % 2]
                    P.op("sp", (lambda c0, cn, t2_: lambda e: e.dma_start(out=t2_[:, :, 0:cn], in_=g2s[:, :, c0:c0 + cn]))(c0, cn, t2_),
                         reads=[g2s_r[ci]], writes=[t2_.r], dma=True)
                load_sg1(0)
                for ci, (c0, cn) in enumerate(CHUNKS):
                    tiles = [ti for ti, (t0, tn) in enumerate(TTILES) if t0 >= c0 and t0 < c0 + cn]
                    hreg = [hT_r[ti] for ti in tiles]
                    mt = mtmp[ci % 2]
                    if ci + 1 < len(CHUNKS):
                        load_sg1(ci + 1)
                    sg_c = sg1[ci % 2]
                    sg2_c = sg2[ci % 2]
                    for j in range(8):
                        b = it % 4
                        it += 1

                        def grp(wt, nk, col0, rhsT, rsel, pt, c0=c0, cn=cn):
                            def f(e):
                                ins = None
                                for kk in range(nk):
                                    ins = e.matmul(pt[:, 0:cn], wt[:, kk, col0:col0 + 128], rhsT[:, rsel + kk, c0:c0 + cn],
                                                   start=(kk == 0), stop=(kk == nk - 1))
                                return ins
                            return f
                        p3, p4, p5 = nextc(), nextc(), nextc()
                        P.op("pe", grp(gl, 4, 1024 + j * 128, YU, 4, p5), reads=[uT_r[kk][ci] for kk in range(4)] + gl_r, writes=[p5.r])
                        P.op("pe", grp(po, 4, j * 128, YU, 0, p3), reads=[yp_r[kk][ci] for kk in range(4)] + po_r, writes=[p3.r])
                        P.op("pe", grp(gl, 4, j * 128, YU, 4, p4), reads=[uT_r[kk][ci] for kk in range(4)] + gl_r, writes=[p4.r])
                        P.op("act", (lambda b, cn, p5: lambda e: e.activation(out=sz2[b][:, 0:cn], in_=p5[:, 0:cn], func=AF.Sigmoid))(b, cn, p5),
                             reads=[p5.r], writes=[sz2[b].r])
                        P.op("dve", (lambda b, cn, p3, j, sg_c: lambda e: e.tensor_mul(ta[b][:, 0:cn], sg_c[:, j, 0:cn], p3[:, 0:cn]))(b, cn, p3, j, sg_c),
                             reads=[sg_c.r, p3.r], writes=[ta[b].r, p3.r])
                        P.op("dve", (lambda b, cn, p4: lambda e: e.tensor_mul(tb[b][:, 0:cn], sz2[b][:, 0:cn], p4[:, 0:cn]))(b, cn, p4),
                             reads=[sz2[b].r, p4.r], writes=[tb[b].r])
                        P.op("pool", (lambda b, cn, j, sg2_c: lambda e: e.tensor_mul(tb[b][:, 0:cn], tb[b][:, 0:cn], sg2_c[:, j, 0:cn]))(b, cn, j, sg2_c),
                             reads=[tb[b].r, sg2_c.r], writes=[tb[b].r])
                        P.op("dve", (lambda b, cn, j, mt: lambda e: e.tensor_add(mt[:, j, 0:cn], ta[b][:, 0:cn], tb[b][:, 0:cn]))(b, cn, j, mt),
                             reads=[ta[b].r, tb[b].r], writes=[mt.r])
                    P.op("dve", (lambda c0, cn, mt: lambda e: e.tensor_copy(YU[:, :, c0:c0 + cn], mt[:, :, 0:cn]))(c0, cn, mt),
                         reads=[mt.r], writes=[mT_r[j][ci] for j in range(8)])
                P.flush()

            if dbg == "C":
                dbg_outs["d_mT"] = (YU, [128, 8 * NCOL], BF16)
                _emit_dbg(nc, P, dbg_outs, es)
                P.flush(final=True)
                return nc, list(dbg_outs.keys())

            with ExitStack() as sDE:
                x1 = sb("x1", [128, 17, D], F32, sDE)
                wu0 = sb("wu0", [128, 8, 512], BF16, sDE)
                wd0 = sb("wd0", [128, 4, D], BF16, sDE)
                x1_r = [Reg() for _ in TTILES]
                h2_r = [Reg() for _ in TTILES]
                with ExitStack() as sD:
                    wo = sb("wo", [128, 8, D], BF16, sD)
                    g2b = sb("g2b", [128, D], F32, sD)
                    xt = [sb("xtd%d" % i, [128, D], F32, sD) for i in range(2)]
                    sq = [sb("sqd%d" % i, [128, D], BF16, sD) for i in range(2)]
                    hb = [sb("hbd%d" % i, [128, D], BF16, sD) for i in range(2)]
                    ssq = [sb("ssqd%d" % i, [128, 1], F32, sD) for i in range(2)]
                    rstd = [sb("rstdd%d" % i, [128, 1], F32, sD) for i in range(2)]
                    ptr = [pst("ptrd%d" % i, [128, 8, 128], BF16, sD) for i in range(2)]
                    pD = [pst("pD%d" % i, [128, 512], F32, sD) for i in range(4)]
                    wo_r = [Reg() for _ in range(8)]
                    for kk in range(8):
                        P.op("pool", (lambda kk: lambda e: e.dma_start(out=wo[:, kk, :], in_=w_out[kk * 128:(kk + 1) * 128, :]))(kk),
                             writes=[wo_r[kk]], dma=True)
                    P.op("sp", lambda e: e.dma_start(out=g2b[:, :], in_=n2g.partition_broadcast(128)), writes=[g2b.r], dma=True)
                    pend_d = []
                    P.op("pool", lambda e: e.dma_start(
                        out=wu0[:, :, :], in_=dram_ap(w_up, 0, [[DFF, 128], [128 * DFF, 8], [1, 512]])), writes=[wu0.r], dma=True)
                    P.op("pool", lambda e: e.dma_start(
                        out=wd0[:, :, :], in_=dram_ap(w_down, 0, [[D, 128], [128 * D, 4], [1, D]])), writes=[wd0.r], dma=True)
                    for ti, (t0, tn) in enumerate(TTILES):
                        b = ti % 2
                        ci = min(ti // 4, 4)
                        src = x[t0:t0 + tn, :] if ti < 16 else xs[:, :]
                        P.op("sp", (lambda b, tn, src: lambda e: e.dma_start(out=xt[b][0:tn, :], in_=src))(b, tn, src),
                             writes=[xt[b].r], dma=True)
                        while len(pend_d) > 1:
                            pend_d.pop(0)()
                        for hf in range(2):
                            pt = pD[(2 * ti + hf) % 4]

                            def mo(e, t0=t0, tn=tn, hf=hf, pt=pt):
                                ins = None
                                for kk in range(8):
                                    ins = e.matmul(pt[0:tn, :], YU[:, kk, t0:t0 + tn], wo[:, kk, hf * 512:(hf + 1) * 512],
                                                   start=(kk == 0), stop=(kk == 7))
                                return ins
                            P.op("pe", mo, reads=[mT_r[j][ci] for j in range(8)] + wo_r, writes=[pt.r])
                            P.op("dve", (lambda b, tn, hf, pt, ti: lambda e: e.tensor_add(
                                x1[0:tn, ti, hf * 512:(hf + 1) * 512], pt[0:tn, :], xt[b][0:tn, hf * 512:(hf + 1) * 512]))(b, tn, hf, pt, ti),
                                reads=[pt.r, xt[b].r], writes=[x1_r[ti], pt.r])
                        while pend_d:
                            pend_d.pop(0)()
                        P.op("act", (lambda b, tn, ti: lambda e: e.activation(
                            out=sq[b][0:tn, :], in_=x1[0:tn, ti, :], func=AF.Square, accum_out=ssq[b][0:tn, :]))(b, tn, ti),
                            reads=[x1_r[ti]], writes=[sq[b].r, ssq[b].r])
                        P.op("dve", (lambda b, tn: lambda e: e.tensor_scalar(
                            rstd[b][0:tn, :], ssq[b][0:tn, :], 1.0 / D, EPS, ALU.mult, ALU.add))(b, tn),
                            reads=[ssq[b].r], writes=[rstd[b].r])
                        P.op("act", (lambda b, tn: lambda e: e.activation(
                            out=rstd[b][0:tn, :], in_=rstd[b][0:tn, :], func=AF.Sqrt))(b, tn),
                            reads=[rstd[b].r], writes=[rstd[b].r])
                        P.op("dve", (lambda b, tn: lambda e: e.reciprocal(rstd[b][0:tn, :], rstd[b][0:tn, :]))(b, tn),
                             reads=[rstd[b].r], writes=[rstd[b].r])
                        P.op("dve", (lambda b, tn, ti: lambda e: e.scalar_tensor_tensor(
                            out=hb[b][0:tn, :], in0=x1[0:tn, ti, :], scalar=rstd[b][0:tn, 0:1], in1=g2b[0:tn, :],
                            op0=ALU.mult, op1=ALU.mult))(b, tn, ti),
                            reads=[x1_r[ti], rstd[b].r, g2b.r], writes=[hb[b].r])

                        def tr2(e, b=b, tn=tn):
                            ins = None
                            for kk in range(8):
                                ins = e.transpose(ptr[b][:, kk, 0:tn], hb[b][0:tn, kk * 128:(kk + 1) * 128], ident_b[0:tn, 0:tn])
                            return ins
                        def late(tr2=tr2, b=b, t0=t0, tn=tn, ti=ti):
                            P.op("pe", tr2, reads=[hb[b].r], writes=[ptr[b].r])
                            P.op("act", (lambda b, t0, tn: lambda e: e.copy(hT[:, :, t0:t0 + tn], ptr[b][:, :, 0:tn]))(b, t0, tn),
                                 reads=[ptr[b].r], writes=[h2_r[ti], ptr[b].r])
                        pend_d.append(late)
                    for f_ in pend_d:
                        f_()
                    P.flush()

                if dbg == "D":
                    dbg_outs["d_x1"] = (x1, [128, 17 * D], F32)
                    dbg_outs["d_h2T"] = (hT, [128, 8 * NCOL], BF16)
                    _emit_dbg(nc, P, dbg_outs, es)
                    P.flush(final=True)
                    return nc, list(dbg_outs.keys())

                with ExitStack() as sE:
                    NE = 8
                    wu = [wu0, sb("wu1", [128, 8, 512], BF16, sE)]
                    wd = [wd0, sb("wd1", [128, 4, D], BF16, sE)]
                    gfb = sb("gfb", [128, D], F32, sE)
                    rt = [sb("rt%d" % i, [128, 512], BF16, sE) for i in range(2)]
                    aT = [sb("aT%d" % i, [128, 4, 512], BF16, sE) for i in range(2)]
                    aT_r = [[Reg() for _ in range(4)] for _ in range(2)]
                    yo = [sb("yo%d" % i, [128, D], F32, sE) for i in range(2)]
                    sq = [sb("sqe%d" % i, [128, D], BF16, sE) for i in range(2)]
                    ssq = [sb("ssqe%d" % i, [128, 1], F32, sE) for i in range(2)]
                    rstd = [sb("rstde%d" % i, [128, 1], F32, sE) for i in range(2)]
                    pE = [pst("pE%d" % i, [128, 512], F32, sE) for i in range(8)]
                    pek = [0]

                    def nexte():
                        t = pE[pek[0] % 8]
                        pek[0] += 1
                        return t
                    P.op("sp", lambda e: e.dma_start(out=gfb[:, :], in_=nfg.partition_broadcast(128)), writes=[gfb.r], dma=True)
                    def final_norm(ti):
                        t0, tn = TTILES[ti]
                        b = ti % 2
                        P.op("act", (lambda b, tn, ti: lambda e: e.activation(
                            out=sq[b][0:tn, :], in_=x1[0:tn, ti, :], func=AF.Square, accum_out=ssq[b][0:tn, :]))(b, tn, ti),
                            reads=[x1_r[ti]], writes=[sq[b].r, ssq[b].r])
                        P.op("dve", (lambda b, tn: lambda e: e.tensor_scalar(
                            rstd[b][0:tn, :], ssq[b][0:tn, :], 1.0 / D, EPS, ALU.mult, ALU.add))(b, tn),
                            reads=[ssq[b].r], writes=[rstd[b].r])
                        P.op("act", (lambda b, tn: lambda e: e.activation(
                            out=rstd[b][0:tn, :], in_=rstd[b][0:tn, :], func=AF.Sqrt))(b, tn),
                            reads=[rstd[b].r], writes=[rstd[b].r])
                        P.op("dve", (lambda b, tn: lambda e: e.reciprocal(rstd[b][0:tn, :], rstd[b][0:tn, :]))(b, tn),
                             reads=[rstd[b].r], writes=[rstd[b].r])
                        P.op("dve", (lambda b, tn, ti: lambda e: e.scalar_tensor_tensor(
                            out=yo[b][0:tn, :], in0=x1[0:tn, ti, :], scalar=rstd[b][0:tn, 0:1], in1=gfb[0:tn, :],
                            op0=ALU.mult, op1=ALU.mult))(b, tn, ti),
                            reads=[x1_r[ti], rstd[b].r, gfb.r], writes=[yo[b].r])
                        dst = y[t0:t0 + tn, :] if ti < 16 else ysamp[:, :]
                        P.op("sp", (lambda b, tn, dst: lambda e: e.dma_start(out=dst, in_=yo[b][0:tn, :]))(b, tn, dst),
                             reads=[yo[b].r], dma=True, out=True)
                    rkk = [0]
                    pend_fn = []

                    def prefetch(e8):
                        if e8 + 1 < NE:
                            nb_ = (e8 + 1) % 2
                            P.op("pool", (lambda nb_, e8: lambda e: e.dma_start(
                                out=wu[nb_][:, :, :], in_=dram_ap(w_up, (e8 + 1) * 512, [[DFF, 128], [128 * DFF, 8], [1, 512]])))(nb_, e8),
                                writes=[wu[nb_].r], dma=True)
                            P.op("pool", (lambda nb_, e8: lambda e: e.dma_start(
                                out=wd[nb_][:, :, :], in_=dram_ap(w_down, (e8 + 1) * 512 * D, [[D, 128], [128 * D, 4], [1, D]])))(nb_, e8),
                                writes=[wd[nb_].r], dma=True)

                    def emit_up(e8, ci, cb):
                        b = e8 % 2
                        c0, cn = CHUNKS[ci]
                        tiles = [ti for ti, (t0, tn) in enumerate(TTILES) if t0 >= c0 and t0 < c0 + cn]
                        for m in range(4):
                            pt = nexte()

                            def up(e, m=m, c0=c0, cn=cn, pt=pt, b=b):
                                ins = None
                                for kk in range(8):
                                    ins = e.matmul(pt[:, 0:cn], wu[b][:, kk, m * 128:(m + 1) * 128], hT[:, kk, c0:c0 + cn],
                                                   start=(kk == 0), stop=(kk == 7))
                                return ins
                            P.op("pe", up, reads=[h2_r[ti] for ti in tiles] + [wu[b].r], writes=[pt.r])
                            rb = rkk[0] % 2
                            rkk[0] += 1
                            P.op("act", (lambda rb, cn, pt: lambda e: e.activation(out=rt[rb][:, 0:cn], in_=pt[:, 0:cn], func=AF.Relu))(rb, cn, pt),
                                 reads=[pt.r], writes=[rt[rb].r, pt.r])
                            P.op("pool", (lambda rb, cn, cb, m: lambda e: e.tensor_mul(
                                aT[cb][:, m, 0:cn], rt[rb][:, 0:cn], rt[rb][:, 0:cn]))(rb, cn, cb, m),
                                reads=[rt[rb].r], writes=[aT_r[cb][m]])

                    def emit_down(e8, ci, cb):
                        b = e8 % 2
                        c0, cn = CHUNKS[ci]
                        tiles = [ti for ti, (t0, tn) in enumerate(TTILES) if t0 >= c0 and t0 < c0 + cn]
                        for ti in tiles:
                            t0, tn = TTILES[ti]
                            for hf in range(2):
                                pt = nexte()

                                def dn(e, t0=t0, tn=tn, c0=c0, hf=hf, pt=pt, b=b, cb=cb):
                                    ins = None
                                    for kk in range(4):
                                        ins = e.matmul(pt[0:tn, :], aT[cb][:, kk, t0 - c0:t0 - c0 + tn], wd[b][:, kk, hf * 512:(hf + 1) * 512],
                                                       start=(kk == 0), stop=(kk == 3))
                                    return ins
                                P.op("pe", dn, reads=aT_r[cb] + [wd[b].r], writes=[pt.r])
                                P.op("dve", (lambda tn, ti, hf, pt: lambda e: e.tensor_add(
                                    x1[0:tn, ti, hf * 512:(hf + 1) * 512], x1[0:tn, ti, hf * 512:(hf + 1) * 512], pt[0:tn, :]))(tn, ti, hf, pt),
                                    reads=[pt.r, x1_r[ti]], writes=[x1_r[ti], pt.r])
                            if e8 == NE - 1:
                                pend_fn.append(ti)
                                while len(pend_fn) > 2:
                                    final_norm(pend_fn.pop(0))

                    items = [(e8, ci) for e8 in range(NE) for ci in range(len(CHUNKS))]
                    emit_up(items[0][0], items[0][1], 0)
                    for i_, (e8, ci) in enumerate(items):
                        if ci == 0:
                            prefetch(e8)
                        if i_ + 1 < len(items):
                            emit_up(items[i_ + 1][0], items[i_ + 1][1], (i_ + 1) % 2)
                        emit_down(e8, ci, i_ % 2)
                    while pend_fn:
                        final_norm(pend_fn.pop(0))
                    P.flush(final=True)
    return nc, []


def _emit_dbg(nc, P, dbg_outs, es):
    for nm, (t, shp, dt) in dbg_outs.items():
        d = nc.dram_tensor(nm, list(shp), dt, kind="ExternalOutput").ap()
        nd = len(t.t.shape)
        flat = {2: lambda: t.t[:], 3: lambda: t.t[:].rearrange("p a b -> p (a b)"),
                4: lambda: t.t[:].rearrange("p a b c -> p (a b c)"),
                5: lambda: t.t[:].rearrange("p a b c d -> p (a b c d)")}[nd]()
        P.op("sp", (lambda d, flat: lambda e: e.dma_start(out=d[:, :], in_=flat))(d, flat),
             reads=[t.r], writes=[], dma=True, out=True)


def make_in_maps(inp):
    f = lambda a: np.ascontiguousarray(np.asarray(a, dtype=np.float32))
    shared = {
        "n1g": f(inp["norm1_g"]).reshape(1, D),
        "w_in": f(inp["w_in"]).reshape(D, 3072),
        "pool_w": f(inp["pool_w"]).reshape(512, 128),
        "pscale": f(inp["pool_scale"]).reshape(1, PW),
        "pool_out": f(inp["pool_out"]).reshape(PW, D),
        "a_re": f(inp["ssm_a_re"]).reshape(1, 2048),
        "a_im": f(inp["ssm_a_im"]).reshape(1, 2048),
        "log_dt": f(inp["ssm_log_dt"]).reshape(1, 32),
        "b_re": f(inp["ssm_b_re"]).reshape(1, -1),
        "b_im": f(inp["ssm_b_im"]).reshape(1, -1),
        "c_re": f(inp["ssm_c_re"]).reshape(512, 64),
        "c_im": f(inp["ssm_c_im"]).reshape(512, 64),
        "ssm_d": f(inp["ssm_d"]).reshape(1, 512),
        "glu": f(inp["ssm_glu"]).reshape(512, 2048),
        "w_out": f(inp["w_out"]).reshape(D, D),
        "n2g": f(inp["norm2_g"]).reshape(1, D),
        "w_up": f(inp["w_up"]).reshape(D, DFF),
        "w_down": f(inp["w_down"]).reshape(DFF, D),
        "nfg": f(inp["normf_g"]).reshape(1, D),
    }
    xp = f(inp["x_prompt"])
    xsm = f(inp["x_sample"]).reshape(128, D)
    cp = f(inp["cache_pool"]).reshape(128, PB, PW)
    sr = f(inp["state_ssm_re"]).reshape(128, 2048)
    si = f(inp["state_ssm_im"]).reshape(128, 2048)
    maps = []
    for c in range(NCORES):
        m = dict(shared)
        m["x"] = xp[c]
        m["xs"] = xsm[c * NS:(c + 1) * NS]
        m["cpool"] = cp[c * NS:(c + 1) * NS].reshape(NS * PB, PW)
        m["sre"] = sr[c * NS:(c + 1) * NS]
        m["sim"] = si[c * NS:(c + 1) * NS]
        maps.append(m)
    return maps


def kernel(**inp):
    nc, _ = build()
    maps = make_in_maps(inp)
    res = run_bass_kernel_spmd(nc, maps, core_ids=list(range(NCORES)))
    R = res.results
    y_prompt = np.stack([R[c]["y"] for c in range(NCORES)], 0).astype(np.float32)
    y_sample = np.concatenate([R[c]["ysamp"] for c in range(NCORES)], 0).reshape(128, 1, D).astype(np.float32)
    pool_p = np.stack([R[c]["o_pool_p"] for c in range(NCORES)], 0).reshape(1, 8, PB, PW).astype(np.float32)
    re_p = np.stack([R[c]["o_re_p"].reshape(32, 64) for c in range(NCORES)], 0).reshape(1, 8, 32, 64).astype(np.float32)
    im_p = np.stack([R[c]["o_im_p"].reshape(32, 64) for c in range(NCORES)], 0).reshape(1, 8, 32, 64).astype(np.float32)
    pool_s = np.concatenate([R[c]["o_pool_s"].reshape(NS, PB, PW) for c in range(NCORES)], 0).reshape(1, 128, PB, PW).astype(np.float32)
    re_s = np.concatenate([R[c]["o_re_s"].reshape(NS, 32, 64) for c in range(NCORES)], 0).reshape(1, 128, 32, 64).astype(np.float32)
    im_s = np.concatenate([R[c]["o_im_s"].reshape(NS, 32, 64) for c in range(NCORES)], 0).reshape(1, 128, 32, 64).astype(np.float32)
    return (y_prompt, y_sample, pool_p, re_p, im_p, pool_s, re_s, im_s)
```
